# Optimizing a Trainium2 kernel written in Bass

```python
import jax, jax.numpy as jnp
from jax import lax
import numpy as np

D_MODEL = 1024
BATCH = 8
SEQ = 8192
DEPTH = 1

HG_HEADS = 8
HG_DIM = 128
HG_WIDTH = HG_HEADS * HG_DIM
HG_CHUNK = 32
MLA_HEADS = 8
QK_NOPE = 128
QK_ROPE = 64
QK_DIM = QK_NOPE + QK_ROPE
V_DIM = 128
Q_LORA = 3 * D_MODEL // 8
KV_LORA = D_MODEL // 4
MLA_WIDTH = MLA_HEADS * V_DIM
Q_BLOCK = 128
ROPE_THETA = 10000.0
N_BRANCH = 2
EPS = 1e-6
IN_SPLITS = (HG_WIDTH, HG_WIDTH, HG_WIDTH, HG_WIDTH,
             Q_LORA, KV_LORA, QK_ROPE, MLA_WIDTH,
             N_BRANCH * D_MODEL)
IN_COLS = sum(IN_SPLITS)

kernel_name = "hgrn2_mla_gated_parallel_hybrid"


def _split_points():
    pts, acc = [], 0
    for w in IN_SPLITS[:-1]:
        acc += w
        pts.append(acc)
    return pts


def rms_norm(x, g):
    xf = x.astype(jnp.float32)
    y = xf * lax.rsqrt(jnp.mean(xf * xf, axis=-1, keepdims=True) + EPS)
    return (y * g.astype(jnp.float32)).astype(x.dtype)


def forget_lower_bounds(lb_logits):
    return jnp.cumsum(jax.nn.softmax(lb_logits.astype(jnp.float32), axis=0), axis=0)[:DEPTH]


def rope_tables(seq):
    inv = ROPE_THETA ** (-jnp.arange(0, QK_ROPE, 2, dtype=jnp.float32) / QK_ROPE)
    ang = jnp.arange(seq, dtype=jnp.float32)[:, None] * inv[None, :]
    return jnp.cos(ang), jnp.sin(ang)


def apply_rope(x, cos, sin):
    xf = x.astype(jnp.float32)
    x1, x2 = xf[..., : QK_ROPE // 2], xf[..., QK_ROPE // 2:]
    out = jnp.concatenate([x1 * cos - x2 * sin, x2 * cos + x1 * sin], axis=-1)
    return out.astype(x.dtype)


def hgrn2_recurrence(q, k, v, log_f):
    B, S, H, dk = q.shape
    dv = v.shape[-1]
    C = HG_CHUNK
    N = S // C

    def to_chunks(t):
        return t.reshape(B, N, C, H, t.shape[-1]).transpose(1, 0, 3, 2, 4)

    q, k, v, log_f = to_chunks(q), to_chunks(k), to_chunks(v), to_chunks(log_f)
    b = jnp.cumsum(log_f, axis=3)
    b_last = b[:, :, :, -1:, :]
    q_in = q * jnp.exp(b)
    k_in = k * jnp.exp(-b)
    k_out = k * jnp.exp(b_last - b)
    chunk_decay = jnp.exp(b_last[:, :, :, 0, :])

    causal = jnp.tril(jnp.ones((C, C), dtype=bool))
    scores = jnp.einsum('nbhtk,nbhsk->nbhts', q_in, k_in)
    scores = jnp.where(causal, scores, 0.0)
    o_intra = jnp.einsum('nbhts,nbhsv->nbhtv', scores, v)

    def step(state, inp):
        q_n, k_n, v_n, dec_n = inp
        o_inter = jnp.einsum('bhtk,bhkv->bhtv', q_n, state)
        state = state * dec_n[..., None] + jnp.einsum('bhsk,bhsv->bhkv', k_n, v_n)
        return state, o_inter

    state0 = jnp.zeros((B, H, dk, dv), jnp.float32)
    _, o_inter = lax.scan(step, state0, (q_in, k_out, v, chunk_decay))
    o = o_intra + o_inter
    return o.transpose(1, 0, 3, 2, 4).reshape(B, S, H, dv)


def hgrn2_branch(hq, hf, hi, hz, lb, hg_norm_g):
    B, S, _ = hq.shape
    dt = hq.dtype
    f = lb + (1.0 - lb) * jax.nn.sigmoid(hf.astype(jnp.float32))
    q = jax.nn.silu(hq.astype(jnp.float32)).reshape(B, S, HG_HEADS, HG_DIM)
    k = (1.0 - f).reshape(B, S, HG_HEADS, HG_DIM)
    v = hi.astype(jnp.float32).reshape(B, S, HG_HEADS, HG_DIM)
    log_f = jnp.log(f).reshape(B, S, HG_HEADS, HG_DIM)
    o = hgrn2_recurrence(q, k, v, log_f).astype(dt)
    o = rms_norm(o, hg_norm_g)
    o = o * jax.nn.silu(hz).reshape(B, S, HG_HEADS, HG_DIM)
    return o.reshape(B, S, HG_WIDTH)


def mla_branch(cq, ckv, kr, mz, q_a_g, w_uq, kv_a_g, w_ukv):
    B, S, _ = cq.shape
    cos, sin = rope_tables(S)
    q = (rms_norm(cq, q_a_g) @ w_uq).reshape(B, S, MLA_HEADS, QK_DIM)
    q_nope = q[..., :QK_NOPE]
    q_pe = apply_rope(q[..., QK_NOPE:], cos[:, None, :], sin[:, None, :])
    kv = (rms_norm(ckv, kv_a_g) @ w_ukv).reshape(B, S, MLA_HEADS, QK_NOPE + V_DIM)
    k_nope, v = kv[..., :QK_NOPE], kv[..., QK_NOPE:]
    k_pe = apply_rope(kr, cos, sin)
    scale = QK_DIM ** -0.5
    key_pos = jnp.arange(S)

    def attend_block(blk):
        start = blk * Q_BLOCK
        qn = lax.dynamic_slice_in_dim(q_nope, start, Q_BLOCK, axis=1)
        qp = lax.dynamic_slice_in_dim(q_pe, start, Q_BLOCK, axis=1)
        s = (jnp.einsum('bqhd,bkhd->bhqk', qn, k_nope)
             + jnp.einsum('bqhr,bkr->bhqk', qp, k_pe)).astype(jnp.float32) * scale
        q_pos = start + jnp.arange(Q_BLOCK)
        s = jnp.where(q_pos[:, None] >= key_pos[None, :], s, -jnp.inf)
        p = jax.nn.softmax(s, axis=-1).astype(v.dtype)
        return jnp.einsum('bhqk,bkhd->bqhd', p, v)

    out = lax.map(attend_block, jnp.arange(S // Q_BLOCK))
    out = out.transpose(1, 0, 2, 3, 4).reshape(B, S, MLA_WIDTH)
    return out * jax.nn.silu(mz)


def setup_inputs(seed: int = 0) -> dict:
    key = jax.random.key(seed)
    ks = jax.random.split(key, 16)
    f32 = jnp.float32

    def w(k, shape, fan_in):
        return jax.random.normal(k, shape, f32) * fan_in ** -0.5

    def gain(k, shape):
        return 1.0 + 0.02 * jax.random.normal(k, shape, f32)

    return {
        "x": jax.random.normal(ks[0], (BATCH, SEQ, D_MODEL), f32),
        "norm_g": gain(ks[1], (DEPTH, D_MODEL)),
        "w_in": w(ks[2], (DEPTH, D_MODEL, IN_COLS), D_MODEL),
        "b_gate": 0.02 * jax.random.normal(ks[3], (DEPTH, N_BRANCH * D_MODEL), f32),
        "lb_logits": 0.1 * jax.random.normal(ks[4], (DEPTH + 1, HG_WIDTH), f32),
        "hg_norm_g": gain(ks[5], (DEPTH, HG_DIM)),
        "q_a_g": gain(ks[6], (DEPTH, Q_LORA)),
        "w_uq": w(ks[7], (DEPTH, Q_LORA, MLA_HEADS * QK_DIM), Q_LORA),
        "kv_a_g": gain(ks[8], (DEPTH, KV_LORA)),
        "w_ukv": w(ks[9], (DEPTH, KV_LORA, MLA_HEADS * (QK_NOPE + V_DIM)), KV_LORA),
        "w_proj_a": w(ks[10], (DEPTH, HG_WIDTH, D_MODEL), HG_WIDTH),
        "w_proj_b": w(ks[11], (DEPTH, MLA_WIDTH, D_MODEL), MLA_WIDTH),
        "w_out": w(ks[12], (DEPTH, D_MODEL, D_MODEL), D_MODEL),
        "final_norm_g": gain(ks[13], (D_MODEL,)),
    }


def reference(x, norm_g, w_in, b_gate, lb_logits, hg_norm_g, q_a_g, w_uq, kv_a_g, w_ukv,
              w_proj_a, w_proj_b, w_out, final_norm_g):
    B, S, _ = x.shape
    lower_bounds = forget_lower_bounds(lb_logits)
    pts = _split_points()
    for l in range(DEPTH):
        h = rms_norm(x, norm_g[l])
        proj = h @ w_in[l]
        hq, hf, hi, hz, cq, ckv, kr, mz, glog = jnp.split(proj, pts, axis=-1)
        y_a = hgrn2_branch(hq, hf, hi, hz, lower_bounds[l], hg_norm_g[l])
        y_b = mla_branch(cq, ckv, kr, mz, q_a_g[l], w_uq[l], kv_a_g[l], w_ukv[l])
        gates = jax.nn.sigmoid((glog + b_gate[l]).astype(jnp.float32)).astype(x.dtype)
        gates = gates.reshape(B, S, N_BRANCH, D_MODEL)
        merged = gates[:, :, 0] * (y_a @ w_proj_a[l]) + gates[:, :, 1] * (y_b @ w_proj_b[l])
        x = x + merged @ w_out[l]
    return rms_norm(x, final_norm_g)
```

```python
import numpy as np
from contextlib import ExitStack
import concourse.bass as bass
import concourse.mybir as mybir
from concourse.bass_utils import run_bass_kernel_spmd

F32 = mybir.dt.float32
BF16 = mybir.dt.bfloat16
ALU = mybir.AluOpType
AF = mybir.ActivationFunctionType

D = 1024
H = 8
T = 512
NT = 4
EPS = 1e-6
QSCALE = 192.0 ** -0.5
IN_COLS = 7872
O_HQ, O_HF, O_HI, O_HZ, O_CQ, O_CKV, O_KR, O_MZ, O_GL = 0, 1024, 2048, 3072, 4096, 4480, 4736, 4800, 5824

COMPUTE = ("pe", "act", "dve", "pool")
SEM_CHUNK = 12000


class Buf:
    __slots__ = ("name", "writers", "readers", "dsem", "dcount", "psum")

    def __init__(self, name):
        self.name = name
        self.writers = {}
        self.readers = []
        self.dsem = None
        self.dcount = 0
        self.psum = False


class Tl(Buf):
    __slots__ = ("t",)

    def __init__(self, name, t):
        Buf.__init__(self, name)
        self.t = t

    def __getitem__(self, k):
        return self.t[k]


class Op:
    __slots__ = ("eng", "fn", "deps", "flag", "tok", "is_dma", "dbuf", "dval")

    def __init__(self, eng, fn, is_dma, dbuf):
        self.eng = eng
        self.fn = fn
        self.deps = []
        self.flag = False
        self.tok = None
        self.is_dma = is_dma
        self.dbuf = dbuf
        self.dval = 0


class Prog:
    def __init__(self, nc):
        self.nc = nc
        self.q = {k: [] for k in ("pe", "act", "dve", "pool", "sp")}
        self.dma_bufs = []
        self.owners = {}

    def sb(self, name, shape, dtype):
        return Tl(name, self.nc.alloc_sbuf_tensor(name, list(shape), dtype))

    def ps(self, name, shape, dtype=F32):
        t = Tl(name, self.nc.alloc_psum_tensor(name, list(shape), dtype))
        t.psum = True
        return t

    def dram(self, name, shape, dtype, kind="Internal"):
        return Tl(name, self.nc.dram_tensor(name, list(shape), dtype, kind=kind))

    def op(self, eng, fn, reads=(), writes=(), dma_dst=None):
        is_dma = dma_dst is not None
        o = Op(eng, fn, is_dma, dma_dst)
        deps = {}

        def add(d, kind):
            if d is o:
                return
            if d.is_dma:
                if is_dma and kind == "waw" and d.dbuf is dma_dst:
                    return
                deps[id(d)] = d
                return
            if (not is_dma) and d.eng == eng:
                if eng == "pe" or kind != "raw":
                    return
            deps[id(d)] = d

        for b in reads:
            for w in b.writers.values():
                add(w, "raw")
            if b.psum:
                for r in b.readers:
                    if r.eng != eng:
                        add(r, "raw")
        for b in writes:
            for r in b.readers:
                add(r, "war")
            for w in b.writers.values():
                add(w, "waw")
        o.deps = list(deps.values())
        for d in o.deps:
            if not d.is_dma:
                d.flag = True
        for b in reads:
            if not is_dma:
                b.readers = [r for r in b.readers if r.is_dma or r.eng != eng]
            b.readers.append(o)
        for b in writes:
            b.readers = []
            b.writers = {(("dma", id(dma_dst)) if is_dma else eng): o}
        if is_dma:
            if dma_dst.dcount == 0:
                self.dma_bufs.append(dma_dst)
            dma_dst.dcount += 1
            o.dval = 16 * dma_dst.dcount
        self.q[eng].append(o)
        return o

    def mm(self, out, lhsT, rhs, start, stop, reads, writes, **kw):
        return self.op("pe", lambda e: e.matmul(out, lhsT, rhs, start=start, stop=stop, **kw), reads, writes)

    def tr(self, out, in_, ident, reads, writes):
        return self.op("pe", lambda e: e.transpose(out, in_, ident), reads, writes)

    def act(self, out, in_, func, reads, writes, **kw):
        return self.op("act", lambda e: e.activation(out, in_, func, **kw), reads, writes)

    def dma(self, eng, out, in_, reads, writes, dst):
        key = (id(dst), eng)
        if key not in self.owners:
            self.owners[key] = Buf(dst.name + "@" + eng)
        return self.op(eng, lambda e: e.dma_start(out, in_), reads, writes, dma_dst=self.owners[key])

    def tt(self, eng, out, in0, in1, op, reads, writes):
        return self.op(eng, lambda e: e.tensor_tensor(out, in0, in1, op), reads, writes)

    def ts(self, eng, out, in0, s1, s2, op0, op1, reads, writes):
        return self.op(eng, lambda e: e.tensor_scalar(out, in0, s1, s2, op0, op1), reads, writes)

    def stt(self, out, in0, scalar, in1, op0, op1, reads, writes):
        return self.op("dve", lambda e: e.scalar_tensor_tensor(out, in0, scalar, in1, op0, op1), reads, writes)

    def cp(self, eng, out, in_, reads, writes):
        if eng == "act":
            return self.op("act", lambda e: e.activation(out, in_, AF.Copy), reads, writes)
        return self.op(eng, lambda e: e.tensor_copy(out, in_), reads, writes)

    def emit(self, final_reads=()):
        nc = self.nc
        self.op("sp", None, reads=final_reads)
        with ExitStack() as es:
            esems = {}
            for k in COMPUTE:
                n = sum(1 for o in self.q[k] if o.flag)
                ns = max(1, (n + SEM_CHUNK - 1) // SEM_CHUNK)
                esems[k] = [es.enter_context(nc.semaphore(f"s_{k}{i}")) for i in range(ns)]
                c = 0
                for o in self.q[k]:
                    if o.flag:
                        o.tok = (esems[k][c // SEM_CHUNK], (c % SEM_CHUNK) + 1)
                        c += 1
            for i, b in enumerate(self.dma_bufs):
                b.dsem = es.enter_context(nc.semaphore(f"d{i}"))
            for k in self.q:
                for o in self.q[k]:
                    if o.is_dma:
                        o.tok = (o.dbuf.dsem, o.dval)
            self.nsem = sum(len(v) for v in esems.values()) + len(self.dma_bufs)
            block = es.enter_context(nc.Block())

            def run(e, k):
                waited = {}
                for o in self.q[k]:
                    need = {}
                    for d in o.deps:
                        s, v = d.tok
                        sid = id(s)
                        if waited.get(sid, 0) >= v:
                            continue
                        if sid not in need or need[sid][1] < v:
                            need[sid] = (s, v)
                    for sid, (s, v) in need.items():
                        e.wait_ge(s, v)
                        waited[sid] = v
                    if o.fn is None:
                        continue
                    ins = o.fn(e)
                    if o.is_dma:
                        ins.then_inc(o.tok[0], 16)
                    elif o.flag:
                        ins.then_inc(o.tok[0], 1)

            @block.tensor
            def _(e):
                run(e, "pe")

            @block.scalar
            def _(e):
                run(e, "act")

            @block.vector
            def _(e):
                run(e, "dve")

            @block.gpsimd
            def _(e):
                run(e, "pool")

            @block.sync
            def _(e):
                run(e, "sp")


def build(S, dbg=None, stop_after=None):
    NG = S // T
    nc = bass.Bass("TRN2", target_bir_lowering=False)
    P = Prog(nc)
    dbg_outs = {}

    x = P.dram("x", [S, D], F32, kind="ExternalInput")
    w_in = P.dram("w_in", [D, IN_COLS], F32, kind="ExternalInput")
    w_uq = P.dram("w_uq", [384, 1536], F32, kind="ExternalInput")
    w_ukv = P.dram("w_ukv", [256, 2048], F32, kind="ExternalInput")
    w_pa = P.dram("w_pa", [D, D], F32, kind="ExternalInput")
    w_pb = P.dram("w_pb", [D, D], F32, kind="ExternalInput")
    w_out = P.dram("w_out", [D, D], F32, kind="ExternalInput")
    vecs = P.dram("vecs", [128, 46], F32, kind="ExternalInput")
    fgb_d = P.dram("fgb", [128, D], F32, kind="ExternalInput")
    cst_d = P.dram("cst", [128, 896], F32, kind="ExternalInput")
    rope_d = P.dram("rope", [2, 64, S], F32, kind="ExternalInput")
    out = P.dram("out", [S, D], F32, kind="ExternalOutput")

    NCH = 22
    wsc = nc.dram_tensor("wsc", [NCH, 128, 8, 512], BF16)
    wscB = [Buf(f"wsc{c}") for c in range(NCH)]
    ksc = nc.dram_tensor("ksc", [H, 128, S], BF16)
    vsc = nc.dram_tensor("vsc", [H, 128, S // 128, 128], BF16)
    kscB = [Buf(f"ksc{g}") for g in range(NG)]
    vscB = [Buf(f"vsc{g}") for g in range(NG)]

    cstf = P.sb("cstf", [128, 896], F32)
    identb = P.sb("identb", [128, 128], BF16)
    maskbd = P.sb("maskbd", [128, 128], BF16)
    trib = P.sb("trib", [128, 128], BF16)
    onesb = P.sb("onesb", [128, 128], BF16)
    vc = P.sb("vc", [128, 46], F32)
    lbv = P.sb("lbv", [128, 24], F32)
    fgb = P.sb("fgbs", [128, D], F32)
    wuq = P.sb("wuq", [128, 3, 1536], BF16)
    wuqr = P.sb("wuqr", [128, 3, 8, 64], BF16)
    wukv = P.sb("wukv", [128, 2, 2048], BF16)
    NSLOT = 3
    slots = [P.sb(f"slot{i}", [128, 8, 512], BF16) for i in range(NSLOT)]
    xs = [P.sb(f"xs{i}", [128, D], F32) for i in range(2)]
    hb = P.sb("hb", [128, D], BF16)
    st4 = P.sb("st4", [128, 8], F32)
    hTs = [P.sb(f"hT{i}", [128, 8, T], BF16) for i in range(2)]
    NTMP = 9
    tmp = [P.sb(f"tmp{i}", [128, T], F32) for i in range(NTMP)]
    qin = [P.sb(f"qin{i}", [128, T], BF16) for i in range(4)]
    kin = [P.sb(f"kin{i}", [128, T], BF16) for i in range(4)]
    koT = [P.sb(f"koT{i}", [128, T], BF16) for i in range(2)]
    ko = [P.sb(f"ko{i}", [128, NT, 128], BF16) for i in range(4)]
    szh = P.sb("szh", [128, 4, T], BF16)
    vT = P.sb("vT", [128, NT, 512], BF16)
    dec = [P.sb(f"dec{i}", [128, 16], F32) for i in range(4)]
    Sst = [P.sb(f"Sst{i}", [128, 4, 128], F32) for i in range(2)]
    Sbf = [P.sb(f"Sbf{i}", [128, 4, 128], BF16) for i in range(2)]
    scm = P.sb("scm", [128, 4, 128], BF16)
    sqo = P.sb("sqo", [128, T], BF16)
    yaT = [P.sb(f"yaT{i}", [128, 4, T], BF16) for i in range(2)]
    ybT = [P.sb(f"ybT{h}", [128, T], BF16) for h in range(H)]
    mT = P.sb("mT", [128, 8, T], BF16)
    cqn = P.sb("cqn", [128, 3, T], BF16)
    ckvn = P.sb("ckvn", [128, 2, T], BF16)
    kpe = P.sb("kpe", [128, S], BF16)
    kpeB = [Buf(f"kpe{g}") for g in range(NG)]
    ropet = P.sb("ropet", [64, 4, T], F32)
    Kn = [P.sb(f"Kn{i}", [128, T], BF16) for i in range(2)]
    Vn = [P.sb(f"Vn{i}", [128, NT, 128], BF16) for i in range(2)]
    Qn = [P.sb(f"Qn{i}", [128, T], BF16) for i in range(2)]
    qpe = [P.sb(f"qpe{i}", [128, T], BF16) for i in range(2)]
    tmpD = [P.sb(f"tmpD{i}", [128, T], F32) for i in range(2)]
    KCH = 1024
    Kc = [P.sb(f"Kc{i}", [128, KCH], BF16) for i in range(2)]
    Vc = [P.sb(f"Vc{i}", [128, KCH // 128, 128], BF16) for i in range(2)]
    NPT = 5
    Pt = [P.sb(f"Pt{i}", [128, T], BF16) for i in range(NPT)]
    Ps = [P.sb(f"Ps{i}", [128, T], BF16) for i in range(2)]

    B = [P.ps(f"bank{i}", [128, 512], F32) for i in range(8)]

    tmp_i = [0]

    def gettmp():
        t = tmp[tmp_i[0] % NTMP]
        tmp_i[0] += 1
        return t

    def tap(name, tl, ap, shape, dtype=F32):
        if dbg is None or name not in dbg:
            return
        d = P.dram("dbg_" + name, list(shape), dtype, kind="ExternalOutput")
        P.dma("sp", d[:], ap, [tl], [d], tl)
        dbg_outs[name] = d

    P.dma("sp", cstf[:], cst_d[:], [cst_d], [cstf], cstf)
    P.dma("sp", vc[:], vecs[:], [vecs], [vc], vc)
    P.dma("sp", fgb[:], fgb_d[:], [fgb_d], [fgb], fgb)
    P.cp("dve", identb[:], cstf[:, 0:128], [cstf], [identb])
    P.cp("dve", maskbd[:], cstf[:, 128:256], [cstf], [maskbd])
    P.cp("dve", trib[:], cstf[:, 256:384], [cstf], [trib])
    resetm = cstf
    P.op("dve", lambda e: e.memset(onesb[:], 1.0), [], [onesb])
    P.op("pool", lambda e: e.memset(kpe[64:128, :], 0.0), [], [kpeB[g_] for g_ in range(NG)])
    for i_ in range(2):
        P.op("pool", lambda e, i_=i_: e.memset(qpe[i_][64:128, :], 0.0), [], [qpe[i_]])
    for h in range(2):
        P.op("pool", lambda e, h=h: e.memset(Sst[h][:], 0.0), [], [Sst[h]])
        P.op("pool", lambda e, h=h: e.memset(Sbf[h][:], 0.0), [], [Sbf[h]])
    V_NG, V_BG, V_L0, V_L1, V_HGG, V_QAG, V_KVAG = 0, 8, 24, 32, 40, 41, 44
    P.tt("dve", lbv[:, 16:24], vc[:, V_L0:V_L0 + 8], vc[:, V_L1:V_L1 + 8], ALU.subtract, [vc], [lbv])
    P.act(lbv[:, 0:8], lbv[:, 16:24], AF.Sigmoid, [lbv], [lbv])
    P.act(lbv[:, 8:16], lbv[:, 16:24], AF.Sigmoid, [lbv], [lbv], scale=-1.0)
    P.ts("dve", lbv[:, 16:24], lbv[:, 8:16], -1.0, None, ALU.mult, ALU.bypass, [lbv], [lbv])

    def wsrc(wt, c0, n):
        return wt.t.ap().rearrange("(kc p) c -> p kc c", p=128)[:, :, c0:c0 + n]

    def conv(ci, col, wt, c0, n):
        P.dma("pool", wsc[ci, :, :, col:col + n], wsrc(wt, c0, n), [wt], [wscB[ci]], wscB[ci])

    conv(8, 0, w_in, O_CQ, 384)
    conv(8, 384, w_in, O_KR, 64)
    conv(8, 448, w_in, O_KR + 32, 32)
    conv(8, 480, w_in, O_KR, 32)
    conv(9, 0, w_in, O_CKV, 256)
    conv(10, 0, w_in, O_MZ, 512)
    conv(11, 0, w_in, O_MZ + 512, 512)
    P.dma("pool", wuq[:], w_uq.t.ap().rearrange("(kc p) c -> p kc c", p=128), [w_uq], [wuq], wuq)
    for hf_ in range(2):
        P.dma("pool", wukv[:, :, hf_ * 1024:(hf_ + 1) * 1024],
              w_ukv.t.ap().rearrange("(kc p) c -> p kc c", p=128)[:, :, hf_ * 1024:(hf_ + 1) * 1024],
              [w_ukv], [wukv], wukv)
    for half in range(2):
        for j, o in enumerate((O_HQ, O_HF, O_HI, O_HZ)):
            conv(half * 4 + j, 0, w_in, o + half * 512, 512)
    for c in range(8):
        conv(12 + c, 0, w_in, O_GL + c * 128, 128)
        conv(12 + c, 128, w_in, O_GL + 1024 + c * 128, 128)
        conv(12 + c, 256, w_pa, c * 128, 128)
        conv(12 + c, 384, w_pb, c * 128, 128)
    conv(20, 0, w_out, 0, 512)
    conv(21, 0, w_out, 512, 512)
    CH_NCOL = [512] * 8 + [512, 256, 512, 512] + [512] * 8 + [512, 512]
    wuq4 = wuq[:, :, :].rearrange("p k (h c) -> p k h c", c=192)
    P.cp("dve", wuqr[:, :, :, 0:32], wuq4[:, :, :, 160:192], [wuq], [wuqr])
    P.cp("dve", wuqr[:, :, :, 32:64], wuq4[:, :, :, 128:160], [wuq], [wuqr])

    sstate = {"n": 0}

    def stream(ci):
        sl = slots[sstate["n"] % NSLOT]
        sstate["n"] += 1
        n = CH_NCOL[ci]
        P.dma("sp", sl[:, :, 0:n], wsc[ci, :, :, 0:n], [wscB[ci]], [sl], sl)
        return sl

    bank_rr = [0]

    def pbank(cands):
        b = cands[bank_rr[0] % len(cands)]
        bank_rr[0] += 1
        return B[b]

    def proj_fm(bank, sl, col, m, rhs_tl, rhs_of_kc, nk=8, rows=128):
        for kc in range(nk):
            P.mm(bank[0:m, 0:T], sl[0:rows, kc, col:col + m], rhs_of_kc(kc), kc == 0, kc == nk - 1,
                 [sl] + rhs_tl, [bank])

    def rstd_from(bank_or_tl, src_ap, dst_tl, dst_ap, scale):
        P.act(dst_ap, src_ap, AF.Ln, [bank_or_tl], [dst_tl], scale=scale, bias=EPS)
        P.act(dst_ap, dst_ap, AF.Exp, [dst_tl], [dst_tl], scale=-0.5)

    out_tiles = []

    def stage_A(g):
        t0 = g * T
        hT = hTs[g % 2]

        def load(i):
            xt = xs[i % 2]
            P.dma("sp", xt[:], x[t0 + i * 128:t0 + (i + 1) * 128, :], [x], [xt], xt)

        load(0)
        load(1)
        yield
        for i in range(NT):
            xt = xs[i % 2]
            P.act(hb[:], xt[:], AF.Square, [xt], [hb, st4], accum_out=st4[:, 0:1])
            yield
            rstd_from(st4, st4[:, 0:1], st4, st4[:, 1:2], 1.0 / D)
            P.act(hb[:], xt[:], AF.Copy, [xt, st4], [hb], scale=st4[:, 1:2])
            if i + 2 < NT:
                load(i + 2)
            yield
            ptb = B[2][:].bitcast(BF16)
            for kc in range(8):
                P.tr(ptb[:, kc * 128:(kc + 1) * 128], hb[:, kc * 128:(kc + 1) * 128], identb[:],
                     [hb, identb], [B[2]])
            P.tt("dve", hT[:, :, i * 128:(i + 1) * 128], ptb.rearrange("p (k t) -> p k t", t=128),
                 vc[:, V_NG:V_NG + 8].unsqueeze(2).to_broadcast([128, 8, 128]), ALU.mult, [B[2], vc], [hT])
            yield

    def stage_B(g):
        hT = hTs[g % 2]
        hT_k = lambda kc: hT[:, kc, :]
        for half in range(2):
            sl_q = stream(half * 4 + 0)
            sl_f = stream(half * 4 + 1)
            for pair in range(2):
                hhs = [pair * 2, pair * 2 + 1]
                sg, lf, eb, en = {}, {}, {}, {}
                for hh in hhs:
                    h = half * 4 + hh
                    bk = pbank([0, 1])
                    proj_fm(bk, sl_f, hh * 128, 128, [hT], hT_k)
                    sg[hh] = gettmp()
                    P.act(sg[hh][:], bk[:], AF.Sigmoid, [bk], [sg[hh]])
                yield
                for hh in hhs:
                    h = half * 4 + hh
                    lf[hh] = gettmp()
                    P.act(lf[hh][:], sg[hh][:], AF.Ln, [sg[hh], lbv], [lf[hh]],
                          scale=lbv[:, 8 + h:9 + h], bias=lbv[:, h:h + 1])
                    P.op("dve", lambda e, a=lf[hh]: e.tensor_tensor_scan(a[:], resetm[:, 384:896], a[:], 0.0,
                                                                          ALU.mult, ALU.add),
                         [cstf, lf[hh]], [lf[hh]])
                    P.ts("dve", sg[hh][:], sg[hh][:], lbv[:, 16 + h:17 + h], lbv[:, 8 + h:9 + h], ALU.mult, ALU.add,
                         [sg[hh], lbv], [sg[hh]])
                yield
                for hh in hhs:
                    eb[hh] = gettmp()
                    en[hh] = gettmp()
                    P.act(eb[hh][:], lf[hh][:], AF.Exp, [lf[hh]], [eb[hh]])
                    P.act(en[hh][:], lf[hh][:], AF.Exp, [lf[hh]], [en[hh]], scale=-1.0)
                    b3 = lf[hh][:, :].rearrange("p (c t) -> p c t", t=32)
                    P.tt("dve", b3, b3[:, :, 31:32].to_broadcast([128, 16, 32]), b3, ALU.subtract,
                         [lf[hh]], [lf[hh]])
                    P.act(lf[hh][:], lf[hh][:], AF.Exp, [lf[hh]], [lf[hh]])
                    P.cp("dve", dec[hh][:], eb[hh][:, :].rearrange("p (c t) -> p c t", t=32)[:, :, 31],
                         [eb[hh]], [dec[hh]])
                yield
                for hh in hhs:
                    bk = pbank([0, 1])
                    proj_fm(bk, sl_q, hh * 128, 128, [hT], hT_k)
                    sq = gettmp()
                    P.act(sq[:], bk[:], AF.Silu, [bk], [sq])
                    P.tt("dve", qin[hh][:], sq[:], eb[hh][:], ALU.mult, [sq, eb[hh]], [qin[hh]])
                    P.tt("dve", kin[hh][:], sg[hh][:], en[hh][:], ALU.mult, [sg[hh], en[hh]], [kin[hh]])
                    kt = koT[hh % 2]
                    P.tt("dve", kt[:], sg[hh][:], lf[hh][:], ALU.mult, [sg[hh], lf[hh]], [kt])
                    yield
                    trb = B[2][:].bitcast(BF16)
                    for i in range(NT):
                        P.tr(trb[:, i * 128:(i + 1) * 128], kt[:, i * 128:(i + 1) * 128], identb[:],
                             [kt, identb], [B[2]])
                    P.cp("dve", ko[hh][:].rearrange("p a b -> p (a b)"), trb[:, 0:512], [B[2]], [ko[hh]])
                    yield
            sl_i = stream(half * 4 + 2)
            for i in range(NT):
                bk = pbank([0, 1])
                for kc in range(8):
                    P.mm(bk[:], hT[:, kc, i * 128:(i + 1) * 128], sl_i[:, kc, :], kc == 0, kc == 7,
                         [hT, sl_i], [bk])
                P.cp("dve", vT[:, i, :], bk[:], [bk], [vT])
                yield
            sl_z = stream(half * 4 + 3)
            for hh in range(4):
                bk = pbank([0, 1])
                proj_fm(bk, sl_z, hh * 128, 128, [hT], hT_k)
                P.act(szh[:, hh, :], bk[:], AF.Silu, [bk], [szh])
                yield
            SC, OA, DS, SSB = B[0], B[1], B[2], B[0]
            def sc_step(i):
                for hh in range(4):
                    P.mm(SC[:, hh * 128:(hh + 1) * 128], kin[hh][:, i * 128:(i + 1) * 128],
                         qin[hh][:, i * 128:(i + 1) * 128], True, True, [kin[hh], qin[hh]], [SC])
                P.tt("dve", scm[:], SC[:, :].rearrange("p (h t) -> p h t", t=128),
                     maskbd[:, :].unsqueeze(1).to_broadcast([128, 4, 128]), ALU.mult, [SC, maskbd], [scm])

            sc_step(0)
            yield
            for i in range(NT):
                for hh in range(4):
                    P.mm(OA[:, hh * 128:(hh + 1) * 128], vT[:, i, hh * 128:(hh + 1) * 128], scm[:, hh, :],
                         hh == 0, False, [vT, scm], [OA], skip_group_check=True)
                for j in range(4):
                    for hh in range(4):
                        h = half * 4 + hh
                        c0 = i * 128 + j * 32
                        P.mm(OA[:, hh * 128 + j * 32:hh * 128 + (j + 1) * 32], Sbf[half][:, hh, :],
                             qin[hh][:, c0:c0 + 32], False, (j == 3 and hh == 3), [Sbf[half], qin[hh]], [OA],
                             skip_group_check=True)
                    for hh in range(4):
                        P.mm(DS[:, hh * 128:(hh + 1) * 128], ko[hh][32 * j:32 * (j + 1), i, :],
                             vT[32 * j:32 * (j + 1), i, hh * 128:(hh + 1) * 128], True, True,
                             [ko[hh], vT], [DS], tile_position=(32 * j, 0), skip_group_check=True)
                    for hh in range(4):
                        h = half * 4 + hh
                        cidx = i * 4 + j
                        P.stt(Sst[half][:, hh, :], Sst[half][:, hh, :], dec[hh][:, cidx:cidx + 1],
                              DS[:, hh * 128:(hh + 1) * 128], ALU.mult, ALU.add, [Sst[half], dec[hh], DS], [Sst[half]])
                    P.cp("dve", Sbf[half][:], Sst[half][:], [Sst[half]], [Sbf[half]])
                    yield
                if i + 1 < NT:
                    sc_step(i + 1)
                P.act(sqo[:], OA[:], AF.Square, [OA], [sqo])
                P.mm(SSB[:], onesb[:], sqo[:], True, True, [onesb, sqo], [SSB])
                rs = gettmp()
                rstd_from(SSB, SSB[:], rs, rs[:], 1.0 / 128)
                t1 = gettmp()
                P.stt(t1[:], OA[:], vc[:, V_HGG:V_HGG + 1], rs[:], ALU.mult, ALU.mult, [OA, vc, rs], [t1])
                P.tt("dve", yaT[half][:, :, i * 128:(i + 1) * 128], t1[:, :].rearrange("p (h t) -> p h t", t=128),
                     szh[:, :, i * 128:(i + 1) * 128], ALU.mult, [t1, szh], [yaT[half]])
                yield

    def stage_C1(g):
        t0 = g * T
        hT = hTs[g % 2]
        hT_k = lambda kc: hT[:, kc, :]
        CB = [0, 1]
        P.dma("sp", ropet[:, 0, :], rope_d[0, :, t0:t0 + T], [rope_d], [ropet], ropet)
        P.dma("sp", ropet[:, 1, :], rope_d[1, :, t0:t0 + T], [rope_d], [ropet], ropet)
        P.ts("dve", ropet[:, 2:4, :], ropet[:, 0:2, :], QSCALE, None, ALU.mult, ALU.bypass, [ropet], [ropet])
        sl8 = stream(8)
        SSB = B[2]
        cqf = []
        for k3 in range(3):
            bk = pbank(CB)
            proj_fm(bk, sl8, k3 * 128, 128, [hT], hT_k)
            cf = gettmp()
            cqf.append(cf)
            P.cp("dve", cf[:], bk[:], [bk], [cf])
            P.act(sqo[:], bk[:], AF.Square, [bk], [sqo])
            P.mm(SSB[:], onesb[:], sqo[:], k3 == 0, k3 == 2, [onesb, sqo], [SSB])
            yield
        rs = gettmp()
        rstd_from(SSB, SSB[:], rs, rs[:], 1.0 / 384)
        for k3 in range(3):
            P.stt(cqn[:, k3, :], cqf[k3][:], vc[:, V_QAG + k3:V_QAG + k3 + 1], rs[:], ALU.mult, ALU.mult,
                  [cqf[k3], vc, rs], [cqn])
        yield
        bka, bkb = pbank(CB), pbank(CB)
        proj_fm(bka, sl8, 384, 64, [hT], hT_k)
        proj_fm(bkb, sl8, 448, 64, [hT], hT_k)
        ta, tb = gettmp(), gettmp()
        P.tt("dve", ta[0:64, :], bka[0:64, :], ropet[:, 0, :], ALU.mult, [bka, ropet], [ta])
        P.tt("dve", tb[0:64, :], bkb[0:64, :], ropet[:, 1, :], ALU.mult, [bkb, ropet], [tb])
        P.tt("dve", kpe[0:64, t0:t0 + T], ta[0:64, :], tb[0:64, :], ALU.add, [ta, tb], [kpeB[g]])
        yield
        sl9 = stream(9)
        ckf = []
        for k2 in range(2):
            bk = pbank(CB)
            proj_fm(bk, sl9, k2 * 128, 128, [hT], hT_k)
            cf = gettmp()
            ckf.append(cf)
            P.cp("dve", cf[:], bk[:], [bk], [cf])
            P.act(sqo[:], bk[:], AF.Square, [bk], [sqo])
            P.mm(SSB[:], onesb[:], sqo[:], k2 == 0, k2 == 1, [onesb, sqo], [SSB])
            yield
        rs = gettmp()
        rstd_from(SSB, SSB[:], rs, rs[:], 1.0 / 256)
        for k2 in range(2):
            P.stt(ckvn[:, k2, :], ckf[k2][:], vc[:, V_KVAG + k2:V_KVAG + k2 + 1], rs[:], ALU.mult, ALU.mult,
                  [ckf[k2], vc, rs], [ckvn])
        yield

    def stage_C2(g):
        hT = hTs[g % 2]
        hT_k = lambda kc: hT[:, kc, :]
        for half in range(2):
            slm = stream(10 + half)
            for hh in range(4):
                bk = pbank([0, 1, 3, 4, 5, 6, 7])
                proj_fm(bk, slm, hh * 128, 128, [hT], hT_k)
                P.act(ybT[half * 4 + hh][:], bk[:], AF.Silu, [bk], [ybT[half * 4 + hh]])

    sidx = [0]
    dstate = {"h": 0}
    ptidx = [0]
    psidx = [0]

    def stage_D(g):
        t0 = g * T
        SB3 = [B[3], B[4], B[5]]
        OAc, LAc = B[6], B[7]
        DEPTH = 2

        def nextbank():
            b_ = SB3[sidx[0] % 3]
            sidx[0] += 1
            return b_

        def proj_units(h):
            p2 = h % 2

            def u_k():
                bk = nextbank()
                for k2 in range(2):
                    P.mm(bk[:], wukv[:, k2, h * 256:h * 256 + 128], ckvn[:, k2, :], k2 == 0, k2 == 1,
                         [wukv, ckvn], [bk])
                P.cp("dve", Kn[p2][:], bk[:], [bk], [Kn[p2]])
                P.dma("pool", ksc[h, :, t0:t0 + T], Kn[p2][:], [Kn[p2]], [kscB[g]], Kn[p2])

            def u_v():
                bk = nextbank()
                for i in range(NT):
                    for k2 in range(2):
                        P.mm(bk[:, i * 128:(i + 1) * 128], ckvn[:, k2, i * 128:(i + 1) * 128],
                             wukv[:, k2, h * 256 + 128:h * 256 + 256], (i == 0 and k2 == 0),
                             (i == NT - 1 and k2 == 1), [ckvn, wukv], [bk], skip_group_check=True)
                P.cp("dve", Vn[p2][:].rearrange("p a b -> p (a b)"), bk[:], [bk], [Vn[p2]])
                P.dma("pool", vsc[h, :, g * NT:(g + 1) * NT, :], Vn[p2][:], [Vn[p2]], [vscB[g]], Vn[p2])

            def u_q():
                bk = nextbank()
                for k3 in range(3):
                    P.mm(bk[:], wuq[:, k3, h * 192:h * 192 + 128], cqn[:, k3, :], k3 == 0, k3 == 2,
                         [wuq, cqn], [bk])
                P.act(Qn[p2][:], bk[:], AF.Copy, [bk], [Qn[p2]], scale=QSCALE)

            def u_qa():
                bka = nextbank()
                for k3 in range(3):
                    P.mm(bka[0:64, :], wuq[:, k3, h * 192 + 128:h * 192 + 192], cqn[:, k3, :], k3 == 0, k3 == 2,
                         [wuq, cqn], [bka])
                ta = tmpD[0]
                P.tt("dve", ta[0:64, :], bka[0:64, :], ropet[:, 2, :], ALU.mult, [bka, ropet], [ta])

            def u_qb():
                bkb = nextbank()
                for k3 in range(3):
                    P.mm(bkb[0:64, :], wuqr[:, k3, h, :], cqn[:, k3, :], k3 == 0, k3 == 2, [wuqr, cqn], [bkb])
                ta = tmpD[0]
                P.stt(qpe[p2][0:64, :], bkb[0:64, :], 1.0, ropet[:, 3, :], ALU.mult, ALU.mult,
                      [bkb, ropet], [qpe[p2]])
                P.tt("dve", qpe[p2][0:64, :], qpe[p2][0:64, :], ta[0:64, :], ALU.add, [qpe[p2], ta], [qpe[p2]])

            return [u_k, u_v, u_q, u_qa, u_qb]

        for u in proj_units(0):
            u()
        yield
        for h in range(H):
            p2 = h % 2
            dstate["h"] = h
            nxt = proj_units(h + 1) if h + 1 < H else []

            blocks = []
            npast = T * g
            ci = 0
            for c0 in range(0, npast, KCH):
                n = min(KCH, npast - c0)
                for kb in range(n // 128):
                    blocks.append(("past", ci, c0, n, kb))
                ci += 1
            for j in range(NT):
                blocks.append(("diag", j))
            nblk = len(blocks)
            state = {}

            def front(bi):
                d = blocks[bi]
                sb_ = nextbank()
                pt_ = Pt[ptidx[0] % NPT]
                ptidx[0] += 1
                if d[0] == "past":
                    _, ci_, c0, n, kb = d
                    kc_, vc_ = Kc[ci_ % 2], Vc[ci_ % 2]
                    if kb == 0:
                        gs = list(range(c0 // T, (c0 + n) // T))
                        P.dma("pool", kc_[:, 0:n], ksc[h, :, c0:c0 + n], [kscB[q] for q in gs], [kc_], kc_)
                        P.dma("pool", vc_[:, 0:n // 128, :], vsc[h, :, c0 // 128:(c0 + n) // 128, :],
                              [vscB[q] for q in gs], [vc_], vc_)
                    klhs, k_tl, kabs = kc_[:, kb * 128:(kb + 1) * 128], kc_, c0 + kb * 128
                    vlhs, v_tl, q0, dj = vc_[:, kb, :], vc_, 0, None
                else:
                    j = d[1]
                    klhs, k_tl, kabs = Kn[p2][:, j * 128:(j + 1) * 128], Kn[p2], t0 + j * 128
                    vlhs, v_tl, q0, dj = Vn[p2][:, j, :], Vn[p2], j * 128, j
                gk = kabs // T
                P.mm(sb_[:, q0:T], klhs, Qn[p2][:, q0:T], True, False, [k_tl, Qn[p2]], [sb_])
                P.mm(sb_[:, q0:T], kpe[:, kabs:kabs + 128], qpe[p2][:, q0:T], False, True,
                     [kpeB[gk], qpe[p2]], [sb_])
                P.act(pt_[:, q0:T], sb_[:, q0:T], AF.Exp, [sb_], [pt_])
                if dj is not None:
                    P.tt("dve", pt_[:, q0:q0 + 128], pt_[:, q0:q0 + 128], trib[:], ALU.mult, [pt_, trib], [pt_])
                state[bi] = (pt_, vlhs, v_tl, q0)

            pending = []

            def flush_ones(upto=1 << 30):
                while pending and pending[0][2] <= upto:
                    ps_, grp, _ = pending.pop(0)
                    P.mm(LAc[:], onesb[:], ps_[:], grp == 0, grp == nblk // 4 - 1, [onesb, ps_], [LAc],
                         skip_group_check=True)

            def back(bi):
                pt_, vlhs, v_tl, q0 = state.pop(bi)
                first = bi == 0
                last = bi == nblk - 1
                flush_ones(bi)
                P.mm(OAc[:, q0:T], vlhs, pt_[:, q0:T], first, last, [v_tl, pt_], [OAc], skip_group_check=True)
                grp, pos = bi // 4, bi % 4
                ps_ = Ps[(psidx[0] + grp) % 2]
                eng_ = "dve"
                if pos == 0:
                    P.cp(eng_, ps_[:, q0:T], pt_[:, q0:T], [pt_], [ps_])
                else:
                    P.tt(eng_, ps_[:, q0:T], ps_[:, q0:T], pt_[:, q0:T], ALU.add, [ps_, pt_], [ps_])
                if pos == 3:
                    pending.append((ps_, grp, bi + 2))

            for it in range(nblk + DEPTH):
                if it < nblk:
                    front(it)
                if it >= DEPTH:
                    back(it - DEPTH)
                if it >= 1 and nxt:
                    nxt.pop(0)()
                yield
            while nxt:
                nxt.pop(0)()
            flush_ones()
            psidx[0] += nblk // 4
            rl = tmpD[0]
            t1 = tmpD[1]
            P.cp("dve", t1[:], OAc[:], [OAc], [t1])
            P.act(rl[:], LAc[:], AF.Ln, [LAc], [rl])
            P.act(rl[:], rl[:], AF.Exp, [rl], [rl], scale=-1.0)
            P.tt("dve", t1[:], t1[:], rl[:], ALU.mult, [t1, rl], [t1])
            P.tt("dve", ybT[h][:], t1[:], ybT[h][:], ALU.mult, [t1, ybT[h]], [ybT[h]])
            yield

    def stage_E(g):
        t0 = g * T
        hT = hTs[g % 2]
        hT_k = lambda kc: hT[:, kc, :]
        xtiles = [(xs[0], xs[0][:]), (xs[1], xs[1][:])]
        for i in range(2):
            P.dma("pool", xs[i][:], x[t0 + i * 128:t0 + (i + 1) * 128, :], [x], [xs[i]], xs[i])
        for c in range(8):
            slc = stream(12 + c)
            bga, bgb, bpa, bpb = [B[(c % 2) * 4 + k_] for k_ in range(4)]
            proj_fm(bga, slc, 0, 128, [hT], hT_k)
            proj_fm(bgb, slc, 128, 128, [hT], hT_k)
            ga, gb_ = gettmp(), gettmp()
            P.act(ga[:], bga[:], AF.Sigmoid, [bga, vc], [ga], bias=vc[:, V_BG + c:V_BG + c + 1])
            P.act(gb_[:], bgb[:], AF.Sigmoid, [bgb, vc], [gb_], bias=vc[:, V_BG + 8 + c:V_BG + 8 + c + 1])
            for kc in range(8):
                P.mm(bpa[:], slc[:, kc, 256:384], yaT[kc // 4][:, kc % 4, :], kc == 0, kc == 7,
                     [slc, yaT[kc // 4]], [bpa])
            for kc in range(8):
                P.mm(bpb[:], slc[:, kc, 384:512], ybT[kc][:], kc == 0, kc == 7, [slc, ybT[kc]], [bpb])
            P.tt("dve", ga[:], ga[:], bpa[:], ALU.mult, [ga, bpa], [ga])
            P.tt("dve", gb_[:], gb_[:], bpb[:], ALU.mult, [gb_, bpb], [gb_])
            P.tt("dve", mT[:, c, :], ga[:], gb_[:], ALU.add, [ga, gb_], [mT])
        for i in range(2, 4):
            yv = yaT[i - 2][:].rearrange("p a b -> p (a b)").bitcast(F32)
            xtiles.append((yaT[i - 2], yv))
            P.dma("pool", yv, x[t0 + i * 128:t0 + (i + 1) * 128, :], [x], [yaT[i - 2]], yaT[i - 2])
        slo = [stream(20), stream(21)]
        for i in range(NT):
            xt, xv = xtiles[i]
            r0 = t0 + i * 128
            for hf_ in range(2):
                bo = B[4 + (i % 2) * 2 + hf_]
                for kc in range(8):
                    P.mm(bo[:], mT[:, kc, i * 128:(i + 1) * 128], slo[hf_][:, kc, :], kc == 0, kc == 7,
                         [mT, slo[hf_]], [bo])
                P.tt("dve", xv[:, hf_ * 512:(hf_ + 1) * 512], xv[:, hf_ * 512:(hf_ + 1) * 512], bo[:], ALU.add,
                     [xt, bo], [xt])
            P.act(hb[:], xv, AF.Square, [xt], [hb, st4], accum_out=st4[:, 2 + i:3 + i])
            rstd_from(st4, st4[:, 2 + i:3 + i], st4, st4[:, 2 + i:3 + i], 1.0 / D)
            P.stt(xv, xv, st4[:, 2 + i:3 + i], fgb[:], ALU.mult, ALU.mult, [xt, st4, fgb], [xt])
            ob = Buf(f"out{g}_{i}")
            P.dma("pool", out[r0:r0 + 128, :], xv, [xt], [ob], xt)
            out_tiles.append(ob)

    NB_UNITS = 2 * (2 * 7 + 4 + 4 + 1 + 4 * 5)
    for _ in stage_A(0):
        pass
    c1_started = set()
    for g in range(NG):
        if g not in c1_started:
            for _ in stage_C1(g):
                pass
        stage_C2(g)
        gen_b = stage_B(g)
        gens = [gen_b]
        gen_a = None
        if g + 1 < NG:
            gen_a = stage_A(g + 1)
            gens.append(gen_a)
        gd = stage_D(g)
        dstate["h"] = 0
        nb_left = NB_UNITS
        nd_left = H * (4 * g + 4 + 4)
        nd_total = nd_left
        d_alive = True
        while gens or d_alive:
            for gen in list(gens):
                try:
                    next(gen)
                except StopIteration:
                    gens.remove(gen)
            nb_left -= 1
            if (d_alive and g + 1 < NG and (g + 1) not in c1_started and gen_b not in gens
                    and gen_a not in gens and dstate["h"] >= 7):
                gens.append(stage_C1(g + 1))
                c1_started.add(g + 1)
                nb_left = 12
                P.c1_overlapped = getattr(P, 'c1_overlapped', 0) + 1
            if d_alive:
                if not gens:
                    k = 1 if (g + 1 < NG and (g + 1) not in c1_started) else (1 << 30)
                elif gen_b in gens:
                    k = max(1, int(round(0.86 * nd_total / NB_UNITS)))
                else:
                    k = max(1, -(-nd_left // max(nb_left, 1)))
                for _ in range(k):
                    try:
                        next(gd)
                        nd_left -= 1
                    except StopIteration:
                        d_alive = False
                        break
        if stop_after == "D":
            break
        stage_E(g)

    P.emit(out_tiles + list(dbg_outs.values()))
    return nc, P


def host_consts(S):
    cst = np.zeros((128, 896), np.float32)
    cst[:, 0:128] = np.eye(128, dtype=np.float32)
    s = np.arange(128)[:, None]
    t = np.arange(128)[None, :]
    cst[:, 128:256] = ((s // 32 == t // 32) & (s <= t)).astype(np.float32)
    cst[:, 256:384] = (t >= s).astype(np.float32)
    rm = np.ones((128, 512), np.float32)
    rm[:, ::32] = 0.0
    cst[:, 384:896] = rm
    inv = (np.float32(10000.0) ** (-np.arange(0, 64, 2, dtype=np.float32) / np.float32(64))).astype(np.float32)
    ang = (np.arange(S, dtype=np.float32)[:, None] * inv[None, :]).astype(np.float32)
    cos = np.cos(ang).astype(np.float32).T
    sin = np.sin(ang).astype(np.float32).T
    rope = np.zeros((2, 64, S), np.float32)
    rope[0, 0:32] = cos
    rope[0, 32:64] = cos
    rope[1, 0:32] = -sin
    rope[1, 32:64] = sin
    return cst, rope


def pc(v):
    v = np.asarray(v, np.float32).reshape(-1, 128)
    return np.ascontiguousarray(v.T)


def make_in_maps(inputs, S):
    cst, rope = host_consts(S)
    vecs = np.concatenate([
        pc(inputs["norm_g"][0]), pc(inputs["b_gate"][0]), pc(inputs["lb_logits"][0]), pc(inputs["lb_logits"][1]),
        pc(inputs["hg_norm_g"][0]), pc(inputs["q_a_g"][0]), pc(inputs["kv_a_g"][0])], axis=1)
    assert vecs.shape == (128, 46)
    fgb = np.ascontiguousarray(np.broadcast_to(np.asarray(inputs["final_norm_g"], np.float32)[None, :], (128, D)))
    common = {
        "w_in": np.ascontiguousarray(inputs["w_in"][0], dtype=np.float32),
        "w_uq": np.ascontiguousarray(inputs["w_uq"][0], dtype=np.float32),
        "w_ukv": np.ascontiguousarray(inputs["w_ukv"][0], dtype=np.float32),
        "w_pa": np.ascontiguousarray(inputs["w_proj_a"][0], dtype=np.float32),
        "w_pb": np.ascontiguousarray(inputs["w_proj_b"][0], dtype=np.float32),
        "w_out": np.ascontiguousarray(inputs["w_out"][0], dtype=np.float32),
        "vecs": vecs, "fgb": fgb, "cst": cst, "rope": rope,
    }
    xa = np.asarray(inputs["x"], np.float32)
    return [dict(common, x=np.ascontiguousarray(xa[b])) for b in range(xa.shape[0])]


_CACHE = {}


def kernel(**inputs):
    xa = np.asarray(inputs["x"])
    Bn, S, _ = xa.shape
    if S not in _CACHE:
        _CACHE[S] = build(S)[0]
    nc = _CACHE[S]
    in_maps = make_in_maps(inputs, S)
    res = run_bass_kernel_spmd(nc, in_maps, core_ids=list(range(Bn)))
    return np.stack([np.asarray(r["out"], np.float32) for r in res.results], axis=0)
```

```python
import numpy as np
from contextlib import ExitStack
import concourse.bass as bass
import concourse.mybir as mybir
from concourse.bass_utils import run_bass_kernel_spmd

F32 = mybir.dt.float32
BF16 = mybir.dt.bfloat16
ALU = mybir.AluOpType
AF = mybir.ActivationFunctionType

D = 1024
H = 8
T = 512
NT = 4
EPS = 1e-6
QSCALE = 192.0 ** -0.5
IN_COLS = 7872
O_HQ, O_HF, O_HI, O_HZ, O_CQ, O_CKV, O_KR, O_MZ, O_GL = 0, 1024, 2048, 3072, 4096, 4480, 4736, 4800, 5824

COMPUTE = ("pe", "act", "dve", "pool")
SEM_CHUNK = 12000


class Buf:
    __slots__ = ("name", "writers", "readers", "dsem", "dcount", "psum")

    def __init__(self, name):
        self.name = name
        self.writers = {}
        self.readers = []
        self.dsem = None
        self.dcount = 0
        self.psum = False


class Tl(Buf):
    __slots__ = ("t",)

    def __init__(self, name, t):
        Buf.__init__(self, name)
        self.t = t

    def __getitem__(self, k):
        return self.t[k]


class Op:
    __slots__ = ("eng", "fn", "deps", "flag", "tok", "is_dma", "dbuf", "dval")

    def __init__(self, eng, fn, is_dma, dbuf):
        self.eng = eng
        self.fn = fn
        self.deps = []
        self.flag = False
        self.tok = None
        self.is_dma = is_dma
        self.dbuf = dbuf
        self.dval = 0


class Prog:
    def __init__(self, nc):
        self.nc = nc
        self.q = {k: [] for k in ("pe", "act", "dve", "pool", "sp")}
        self.dma_bufs = []
        self.owners = {}

    def sb(self, name, shape, dtype):
        return Tl(name, self.nc.alloc_sbuf_tensor(name, list(shape), dtype))

    def ps(self, name, shape, dtype=F32):
        t = Tl(name, self.nc.alloc_psum_tensor(name, list(shape), dtype))
        t.psum = True
        return t

    def dram(self, name, shape, dtype, kind="Internal"):
        return Tl(name, self.nc.dram_tensor(name, list(shape), dtype, kind=kind))

    def op(self, eng, fn, reads=(), writes=(), dma_dst=None):
        is_dma = dma_dst is not None
        o = Op(eng, fn, is_dma, dma_dst)
        deps = {}

        def add(d, kind):
            if d is o:
                return
            if d.is_dma:
                if is_dma and kind == "waw" and d.dbuf is dma_dst:
                    return
                deps[id(d)] = d
                return
            if (not is_dma) and d.eng == eng:
                if eng == "pe" or kind != "raw":
                    return
            deps[id(d)] = d

        for b in reads:
            for w in b.writers.values():
                add(w, "raw")
            if b.psum:
                for r in b.readers:
                    if r.eng != eng:
                        add(r, "raw")
        for b in writes:
            for r in b.readers:
                add(r, "war")
            for w in b.writers.values():
                add(w, "waw")
        o.deps = list(deps.values())
        for d in o.deps:
            if not d.is_dma:
                d.flag = True
        for b in reads:
            if not is_dma:
                b.readers = [r for r in b.readers if r.is_dma or r.eng != eng]
            b.readers.append(o)
        for b in writes:
            b.readers = []
            b.writers = {(("dma", id(dma_dst)) if is_dma else eng): o}
        if is_dma:
            if dma_dst.dcount == 0:
                self.dma_bufs.append(dma_dst)
            dma_dst.dcount += 1
            o.dval = 16 * dma_dst.dcount
        self.q[eng].append(o)
        return o

    def mm(self, out, lhsT, rhs, start, stop, reads, writes, **kw):
        return self.op("pe", lambda e: e.matmul(out, lhsT, rhs, start=start, stop=stop, **kw), reads, writes)

    def tr(self, out, in_, ident, reads, writes):
        return self.op("pe", lambda e: e.transpose(out, in_, ident), reads, writes)

    def act(self, out, in_, func, reads, writes, **kw):
        return self.op("act", lambda e: e.activation(out, in_, func, **kw), reads, writes)

    def dma(self, eng, out, in_, reads, writes, dst):
        key = (id(dst), eng)
        if key not in self.owners:
            self.owners[key] = Buf(dst.name + "@" + eng)
        return self.op(eng, lambda e: e.dma_start(out, in_), reads, writes, dma_dst=self.owners[key])

    def tt(self, eng, out, in0, in1, op, reads, writes):
        return self.op(eng, lambda e: e.tensor_tensor(out, in0, in1, op), reads, writes)

    def ts(self, eng, out, in0, s1, s2, op0, op1, reads, writes):
        return self.op(eng, lambda e: e.tensor_scalar(out, in0, s1, s2, op0, op1), reads, writes)

    def stt(self, out, in0, scalar, in1, op0, op1, reads, writes):
        return self.op("dve", lambda e: e.scalar_tensor_tensor(out, in0, scalar, in1, op0, op1), reads, writes)

    def cp(self, eng, out, in_, reads, writes):
        if eng == "act":
            return self.op("act", lambda e: e.activation(out, in_, AF.Copy), reads, writes)
        return self.op(eng, lambda e: e.tensor_copy(out, in_), reads, writes)

    def emit(self, final_reads=()):
        nc = self.nc
        self.op("sp", None, reads=final_reads)
        with ExitStack() as es:
            esems = {}
            for k in COMPUTE:
                n = sum(1 for o in self.q[k] if o.flag)
                ns = max(1, (n + SEM_CHUNK - 1) // SEM_CHUNK)
                esems[k] = [es.enter_context(nc.semaphore(f"s_{k}{i}")) for i in range(ns)]
                c = 0
                for o in self.q[k]:
                    if o.flag:
                        o.tok = (esems[k][c // SEM_CHUNK], (c % SEM_CHUNK) + 1)
                        c += 1
            for i, b in enumerate(self.dma_bufs):
                b.dsem = es.enter_context(nc.semaphore(f"d{i}"))
            for k in self.q:
                for o in self.q[k]:
                    if o.is_dma:
                        o.tok = (o.dbuf.dsem, o.dval)
            self.nsem = sum(len(v) for v in esems.values()) + len(self.dma_bufs)
            block = es.enter_context(nc.Block())

            def run(e, k):
                waited = {}
                for o in self.q[k]:
                    need = {}
                    for d in o.deps:
                        s, v = d.tok
                        sid = id(s)
                        if waited.get(sid, 0) >= v:
                            continue
                        if sid not in need or need[sid][1] < v:
                            need[sid] = (s, v)
                    for sid, (s, v) in need.items():
                        e.wait_ge(s, v)
                        waited[sid] = v
                    if o.fn is None:
                        continue
                    ins = o.fn(e)
                    if o.is_dma:
                        ins.then_inc(o.tok[0], 16)
                    elif o.flag:
                        ins.then_inc(o.tok[0], 1)

            @block.tensor
            def _(e):
                run(e, "pe")

            @block.scalar
            def _(e):
                run(e, "act")

            @block.vector
            def _(e):
                run(e, "dve")

            @block.gpsimd
            def _(e):
                run(e, "pool")

            @block.sync
            def _(e):
                run(e, "sp")


def build(S, dbg=None, stop_after=None):
    NG = S // T
    nc = bass.Bass("TRN2", target_bir_lowering=False)
    P = Prog(nc)
    dbg_outs = {}

    x = P.dram("x", [S, D], F32, kind="ExternalInput")
    w_in = P.dram("w_in", [D, IN_COLS], F32, kind="ExternalInput")
    w_uq = P.dram("w_uq", [384, 1536], F32, kind="ExternalInput")
    w_ukv = P.dram("w_ukv", [256, 2048], F32, kind="ExternalInput")
    w_pa = P.dram("w_pa", [D, D], F32, kind="ExternalInput")
    w_pb = P.dram("w_pb", [D, D], F32, kind="ExternalInput")
    w_out = P.dram("w_out", [D, D], F32, kind="ExternalInput")
    vecs = P.dram("vecs", [128, 46], F32, kind="ExternalInput")
    fgb_d = P.dram("fgb", [128, D], F32, kind="ExternalInput")
    cst_d = P.dram("cst", [128, 896], F32, kind="ExternalInput")
    rope_d = P.dram("rope", [2, 64, S], F32, kind="ExternalInput")
    out = P.dram("out", [S, D], F32, kind="ExternalOutput")

    NCH = 22
    wsc = nc.dram_tensor("wsc", [NCH, 128, 8, 512], BF16)
    wscB = [Buf(f"wsc{c}") for c in range(NCH)]
    ksc = nc.dram_tensor("ksc", [H, 128, S], BF16)
    vsc = nc.dram_tensor("vsc", [H, 128, S // 128, 128], BF16)
    kscB = [Buf(f"ksc{g}") for g in range(NG)]
    vscB = [Buf(f"vsc{g}") for g in range(NG)]

    cstf = P.sb("cstf", [128, 896], F32)
    identb = P.sb("identb", [128, 128], BF16)
    maskbd = P.sb("maskbd", [128, 128], BF16)
    trib = P.sb("trib", [128, 128], BF16)
    onesb = P.sb("onesb", [128, 128], BF16)
    vc = P.sb("vc", [128, 46], F32)
    lbv = P.sb("lbv", [128, 24], F32)
    fgb = P.sb("fgbs", [128, D], F32)
    wuq = P.sb("wuq", [128, 3, 1536], BF16)
    wuqr = P.sb("wuqr", [128, 3, 8, 64], BF16)
    wukv = P.sb("wukv", [128, 2, 2048], BF16)
    NSLOT = 3
    slots = [P.sb(f"slot{i}", [128, 8, 512], BF16) for i in range(NSLOT)]
    xs = [P.sb(f"xs{i}", [128, D], F32) for i in range(2)]
    hb = P.sb("hb", [128, D], BF16)
    st4 = P.sb("st4", [128, 8], F32)
    hTs = [P.sb(f"hT{i}", [128, 8, T], BF16) for i in range(2)]
    NTMP = 9
    tmp = [P.sb(f"tmp{i}", [128, T], F32) for i in range(NTMP)]
    qin = [P.sb(f"qin{i}", [128, T], BF16) for i in range(4)]
    kin = [P.sb(f"kin{i}", [128, T], BF16) for i in range(4)]
    koT = [P.sb(f"koT{i}", [128, T], BF16) for i in range(2)]
    ko = [P.sb(f"ko{i}", [128, NT, 128], BF16) for i in range(4)]
    szh = P.sb("szh", [128, 4, T], BF16)
    vT = P.sb("vT", [128, NT, 512], BF16)
    dec = [P.sb(f"dec{i}", [128, 16], F32) for i in range(4)]
    Sst = [P.sb(f"Sst{i}", [128, 4, 128], F32) for i in range(2)]
    Sbf = [P.sb(f"Sbf{i}", [128, 4, 128], BF16) for i in range(2)]
    scm = P.sb("scm", [128, 4, 128], BF16)
    sqo = P.sb("sqo", [128, T], BF16)
    yaT = [P.sb(f"yaT{i}", [128, 4, T], BF16) for i in range(2)]
    ybT = [P.sb(f"ybT{h}", [128, T], BF16) for h in range(H)]
    mT = P.sb("mT", [128, 8, T], BF16)
    cqn = P.sb("cqn", [128, 3, T], BF16)
    ckvn = P.sb("ckvn", [128, 2, T], BF16)
    kpe = P.sb("kpe", [128, S], BF16)
    kpeB = [Buf(f"kpe{g}") for g in range(NG)]
    ropet = P.sb("ropet", [64, 4, T], F32)
    Kn = [P.sb(f"Kn{i}", [128, T], BF16) for i in range(2)]
    Vn = [P.sb(f"Vn{i}", [128, NT, 128], BF16) for i in range(2)]
    Qn = [P.sb(f"Qn{i}", [128, T], BF16) for i in range(2)]
    qpe = [P.sb(f"qpe{i}", [128, T], BF16) for i in range(2)]
    tmpD = [P.sb(f"tmpD{i}", [128, T], F32) for i in range(2)]
    KCH = 1024
    Kc = [P.sb(f"Kc{i}", [128, KCH], BF16) for i in range(2)]
    Vc = [P.sb(f"Vc{i}", [128, KCH // 128, 128], BF16) for i in range(2)]
    NPT = 5
    Pt = [P.sb(f"Pt{i}", [128, T], BF16) for i in range(NPT)]
    Ps = [P.sb(f"Ps{i}", [128, T], BF16) for i in range(2)]

    B = [P.ps(f"bank{i}", [128, 512], F32) for i in range(8)]

    tmp_i = [0]

    def gettmp():
        t = tmp[tmp_i[0] % NTMP]
        tmp_i[0] += 1
        return t

    def tap(name, tl, ap, shape, dtype=F32):
        if dbg is None or name not in dbg:
            return
        d = P.dram("dbg_" + name, list(shape), dtype, kind="ExternalOutput")
        P.dma("sp", d[:], ap, [tl], [d], tl)
        dbg_outs[name] = d

    P.dma("sp", cstf[:], cst_d[:], [cst_d], [cstf], cstf)
    P.dma("sp", vc[:], vecs[:], [vecs], [vc], vc)
    P.dma("sp", fgb[:], fgb_d[:], [fgb_d], [fgb], fgb)
    P.cp("dve", identb[:], cstf[:, 0:128], [cstf], [identb])
    P.cp("dve", maskbd[:], cstf[:, 128:256], [cstf], [maskbd])
    P.cp("dve", trib[:], cstf[:, 256:384], [cstf], [trib])
    resetm = cstf
    P.op("dve", lambda e: e.memset(onesb[:], 1.0), [], [onesb])
    P.op("pool", lambda e: e.memset(kpe[64:128, :], 0.0), [], [kpeB[g_] for g_ in range(NG)])
    for i_ in range(2):
        P.op("pool", lambda e, i_=i_: e.memset(qpe[i_][64:128, :], 0.0), [], [qpe[i_]])
    for h in range(2):
        P.op("pool", lambda e, h=h: e.memset(Sst[h][:], 0.0), [], [Sst[h]])
        P.op("pool", lambda e, h=h: e.memset(Sbf[h][:], 0.0), [], [Sbf[h]])
    V_NG, V_BG, V_L0, V_L1, V_HGG, V_QAG, V_KVAG = 0, 8, 24, 32, 40, 41, 44
    P.tt("dve", lbv[:, 16:24], vc[:, V_L0:V_L0 + 8], vc[:, V_L1:V_L1 + 8], ALU.subtract, [vc], [lbv])
    P.act(lbv[:, 0:8], lbv[:, 16:24], AF.Sigmoid, [lbv], [lbv])
    P.act(lbv[:, 8:16], lbv[:, 16:24], AF.Sigmoid, [lbv], [lbv], scale=-1.0)
    P.ts("dve", lbv[:, 16:24], lbv[:, 8:16], -1.0, None, ALU.mult, ALU.bypass, [lbv], [lbv])

    def wsrc(wt, c0, n):
        return wt.t.ap().rearrange("(kc p) c -> p kc c", p=128)[:, :, c0:c0 + n]

    def conv(ci, col, wt, c0, n):
        P.dma("pool", wsc[ci, :, :, col:col + n], wsrc(wt, c0, n), [wt], [wscB[ci]], wscB[ci])

    conv(8, 0, w_in, O_CQ, 384)
    conv(8, 384, w_in, O_KR, 64)
    conv(8, 448, w_in, O_KR + 32, 32)
    conv(8, 480, w_in, O_KR, 32)
    conv(9, 0, w_in, O_CKV, 256)
    conv(10, 0, w_in, O_MZ, 512)
    conv(11, 0, w_in, O_MZ + 512, 512)
    P.dma("pool", wuq[:], w_uq.t.ap().rearrange("(kc p) c -> p kc c", p=128), [w_uq], [wuq], wuq)
    for hf_ in range(2):
        P.dma("pool", wukv[:, :, hf_ * 1024:(hf_ + 1) * 1024],
              w_ukv.t.ap().rearrange("(kc p) c -> p kc c", p=128)[:, :, hf_ * 1024:(hf_ + 1) * 1024],
              [w_ukv], [wukv], wukv)
    for half in range(2):
        for j, o in enumerate((O_HQ, O_HF, O_HI, O_HZ)):
            conv(half * 4 + j, 0, w_in, o + half * 512, 512)
    for c in range(8):
        conv(12 + c, 0, w_in, O_GL + c * 128, 128)
        conv(12 + c, 128, w_in, O_GL + 1024 + c * 128, 128)
        conv(12 + c, 256, w_pa, c * 128, 128)
        conv(12 + c, 384, w_pb, c * 128, 128)
    conv(20, 0, w_out, 0, 512)
    conv(21, 0, w_out, 512, 512)
    CH_NCOL = [512] * 8 + [512, 256, 512, 512] + [512] * 8 + [512, 512]
    wuq4 = wuq[:, :, :].rearrange("p k (h c) -> p k h c", c=192)
    P.cp("dve", wuqr[:, :, :, 0:32], wuq4[:, :, :, 160:192], [wuq], [wuqr])
    P.cp("dve", wuqr[:, :, :, 32:64], wuq4[:, :, :, 128:160], [wuq], [wuqr])

    sstate = {"n": 0}

    def stream(ci):
        sl = slots[sstate["n"] % NSLOT]
        sstate["n"] += 1
        n = CH_NCOL[ci]
        P.dma("sp", sl[:, :, 0:n], wsc[ci, :, :, 0:n], [wscB[ci]], [sl], sl)
        return sl

    bank_rr = [0]

    def pbank(cands):
        b = cands[bank_rr[0] % len(cands)]
        bank_rr[0] += 1
        return B[b]

    def proj_fm(bank, sl, col, m, rhs_tl, rhs_of_kc, nk=8, rows=128):
        for kc in range(nk):
            P.mm(bank[0:m, 0:T], sl[0:rows, kc, col:col + m], rhs_of_kc(kc), kc == 0, kc == nk - 1,
                 [sl] + rhs_tl, [bank])

    def rstd_from(bank_or_tl, src_ap, dst_tl, dst_ap, scale):
        P.act(dst_ap, src_ap, AF.Ln, [bank_or_tl], [dst_tl], scale=scale, bias=EPS)
        P.act(dst_ap, dst_ap, AF.Exp, [dst_tl], [dst_tl], scale=-0.5)

    out_tiles = []

    def stage_A(g):
        t0 = g * T
        hT = hTs[g % 2]

        def load(i):
            xt = xs[i % 2]
            P.dma("sp", xt[:], x[t0 + i * 128:t0 + (i + 1) * 128, :], [x], [xt], xt)

        load(0)
        load(1)
        yield
        for i in range(NT):
            xt = xs[i % 2]
            P.act(hb[:], xt[:], AF.Square, [xt], [hb, st4], accum_out=st4[:, 0:1])
            yield
            rstd_from(st4, st4[:, 0:1], st4, st4[:, 1:2], 1.0 / D)
            P.act(hb[:], xt[:], AF.Copy, [xt, st4], [hb], scale=st4[:, 1:2])
            if i + 2 < NT:
                load(i + 2)
            yield
            ptb = B[2][:].bitcast(BF16)
            for kc in range(8):
                P.tr(ptb[:, kc * 128:(kc + 1) * 128], hb[:, kc * 128:(kc + 1) * 128], identb[:],
                     [hb, identb], [B[2]])
            P.tt("dve", hT[:, :, i * 128:(i + 1) * 128], ptb.rearrange("p (k t) -> p k t", t=128),
                 vc[:, V_NG:V_NG + 8].unsqueeze(2).to_broadcast([128, 8, 128]), ALU.mult, [B[2], vc], [hT])
            yield

    def stage_B(g):
        hT = hTs[g % 2]
        hT_k = lambda kc: hT[:, kc, :]
        for half in range(2):
            sl_q = stream(half * 4 + 0)
            sl_f = stream(half * 4 + 1)
            for pair in range(2):
                hhs = [pair * 2, pair * 2 + 1]
                sg, lf, eb, en = {}, {}, {}, {}
                for hh in hhs:
                    h = half * 4 + hh
                    bk = pbank([0, 1])
                    proj_fm(bk, sl_f, hh * 128, 128, [hT], hT_k)
                    sg[hh] = gettmp()
                    P.act(sg[hh][:], bk[:], AF.Sigmoid, [bk], [sg[hh]])
                yield
                for hh in hhs:
                    h = half * 4 + hh
                    lf[hh] = gettmp()
                    P.act(lf[hh][:], sg[hh][:], AF.Ln, [sg[hh], lbv], [lf[hh]],
                          scale=lbv[:, 8 + h:9 + h], bias=lbv[:, h:h + 1])
                    P.op("dve", lambda e, a=lf[hh]: e.tensor_tensor_scan(a[:], resetm[:, 384:896], a[:], 0.0,
                                                                          ALU.mult, ALU.add),
                         [cstf, lf[hh]], [lf[hh]])
                    P.ts("dve", sg[hh][:], sg[hh][:], lbv[:, 16 + h:17 + h], lbv[:, 8 + h:9 + h], ALU.mult, ALU.add,
                         [sg[hh], lbv], [sg[hh]])
                yield
                for hh in hhs:
                    eb[hh] = gettmp()
                    en[hh] = gettmp()
                    P.act(eb[hh][:], lf[hh][:], AF.Exp, [lf[hh]], [eb[hh]])
                    P.act(en[hh][:], lf[hh][:], AF.Exp, [lf[hh]], [en[hh]], scale=-1.0)
                    b3 = lf[hh][:, :].rearrange("p (c t) -> p c t", t=32)
                    P.tt("dve", b3, b3[:, :, 31:32].to_broadcast([128, 16, 32]), b3, ALU.subtract,
                         [lf[hh]], [lf[hh]])
                    P.act(lf[hh][:], lf[hh][:], AF.Exp, [lf[hh]], [lf[hh]])
                    P.cp("dve", dec[hh][:], eb[hh][:, :].rearrange("p (c t) -> p c t", t=32)[:, :, 31],
                         [eb[hh]], [dec[hh]])
                yield
                for hh in hhs:
                    bk = pbank([0, 1])
                    proj_fm(bk, sl_q, hh * 128, 128, [hT], hT_k)
                    sq = gettmp()
                    P.act(sq[:], bk[:], AF.Silu, [bk], [sq])
                    P.tt("dve", qin[hh][:], sq[:], eb[hh][:], ALU.mult, [sq, eb[hh]], [qin[hh]])
                    P.tt("dve", kin[hh][:], sg[hh][:], en[hh][:], ALU.mult, [sg[hh], en[hh]], [kin[hh]])
                    kt = koT[hh % 2]
                    P.tt("dve", kt[:], sg[hh][:], lf[hh][:], ALU.mult, [sg[hh], lf[hh]], [kt])
                    yield
                    trb = B[2][:].bitcast(BF16)
                    for i in range(NT):
                        P.tr(trb[:, i * 128:(i + 1) * 128], kt[:, i * 128:(i + 1) * 128], identb[:],
                             [kt, identb], [B[2]])
                    P.cp("dve", ko[hh][:].rearrange("p a b -> p (a b)"), trb[:, 0:512], [B[2]], [ko[hh]])
                    yield
            sl_i = stream(half * 4 + 2)
            for i in range(NT):
                bk = pbank([0, 1])
                for kc in range(8):
                    P.mm(bk[:], hT[:, kc, i * 128:(i + 1) * 128], sl_i[:, kc, :], kc == 0, kc == 7,
                         [hT, sl_i], [bk])
                P.cp("dve", vT[:, i, :], bk[:], [bk], [vT])
                yield
            sl_z = stream(half * 4 + 3)
            for hh in range(4):
                bk = pbank([0, 1])
                proj_fm(bk, sl_z, hh * 128, 128, [hT], hT_k)
                P.act(szh[:, hh, :], bk[:], AF.Silu, [bk], [szh])
                yield
            SC, OA, DS, SSB = B[0], B[1], B[2], B[0]
            def sc_step(i):
                for hh in range(4):
                    P.mm(SC[:, hh * 128:(hh + 1) * 128], kin[hh][:, i * 128:(i + 1) * 128],
                         qin[hh][:, i * 128:(i + 1) * 128], True, True, [kin[hh], qin[hh]], [SC])
                P.tt("dve", scm[:], SC[:, :].rearrange("p (h t) -> p h t", t=128),
                     maskbd[:, :].unsqueeze(1).to_broadcast([128, 4, 128]), ALU.mult, [SC, maskbd], [scm])

            sc_step(0)
            yield
            for i in range(NT):
                for hh in range(4):
                    P.mm(OA[:, hh * 128:(hh + 1) * 128], vT[:, i, hh * 128:(hh + 1) * 128], scm[:, hh, :],
                         hh == 0, False, [vT, scm], [OA], skip_group_check=True)
                for j in range(4):
                    for hh in range(4):
                        h = half * 4 + hh
                        c0 = i * 128 + j * 32
                        P.mm(OA[:, hh * 128 + j * 32:hh * 128 + (j + 1) * 32], Sbf[half][:, hh, :],
                             qin[hh][:, c0:c0 + 32], False, (j == 3 and hh == 3), [Sbf[half], qin[hh]], [OA],
                             skip_group_check=True)
                    for hh in range(4):
                        P.mm(DS[:, hh * 128:(hh + 1) * 128], ko[hh][32 * j:32 * (j + 1), i, :],
                             vT[32 * j:32 * (j + 1), i, hh * 128:(hh + 1) * 128], True, True,
                             [ko[hh], vT], [DS], tile_position=(32 * j, 0), skip_group_check=True)
                    for hh in range(4):
                        h = half * 4 + hh
                        cidx = i * 4 + j
                        P.stt(Sst[half][:, hh, :], Sst[half][:, hh, :], dec[hh][:, cidx:cidx + 1],
                              DS[:, hh * 128:(hh + 1) * 128], ALU.mult, ALU.add, [Sst[half], dec[hh], DS], [Sst[half]])
                    P.cp("dve", Sbf[half][:], Sst[half][:], [Sst[half]], [Sbf[half]])
                    yield
                if i + 1 < NT:
                    sc_step(i + 1)
                P.act(sqo[:], OA[:], AF.Square, [OA], [sqo])
                P.mm(SSB[:], onesb[:], sqo[:], True, True, [onesb, sqo], [SSB])
                rs = gettmp()
                rstd_from(SSB, SSB[:], rs, rs[:], 1.0 / 128)
                t1 = gettmp()
                P.stt(t1[:], OA[:], vc[:, V_HGG:V_HGG + 1], rs[:], ALU.mult, ALU.mult, [OA, vc, rs], [t1])
                P.tt("dve", yaT[half][:, :, i * 128:(i + 1) * 128], t1[:, :].rearrange("p (h t) -> p h t", t=128),
                     szh[:, :, i * 128:(i + 1) * 128], ALU.mult, [t1, szh], [yaT[half]])
                yield

    def stage_C(g):
        t0 = g * T
        hT = hTs[g % 2]
        hT_k = lambda kc: hT[:, kc, :]
        P.dma("sp", ropet[:, 0, :], rope_d[0, :, t0:t0 + T], [rope_d], [ropet], ropet)
        P.dma("sp", ropet[:, 1, :], rope_d[1, :, t0:t0 + T], [rope_d], [ropet], ropet)
        P.ts("dve", ropet[:, 2:4, :], ropet[:, 0:2, :], QSCALE, None, ALU.mult, ALU.bypass, [ropet], [ropet])
        sl8 = stream(8)
        SSB = B[2]
        cqf = []
        for k3 in range(3):
            bk = pbank([0, 1, 3, 4, 5, 6, 7])
            proj_fm(bk, sl8, k3 * 128, 128, [hT], hT_k)
            cf = gettmp()
            cqf.append(cf)
            P.cp("dve", cf[:], bk[:], [bk], [cf])
            P.act(sqo[:], bk[:], AF.Square, [bk], [sqo])
            P.mm(SSB[:], onesb[:], sqo[:], k3 == 0, k3 == 2, [onesb, sqo], [SSB])
        rs = gettmp()
        rstd_from(SSB, SSB[:], rs, rs[:], 1.0 / 384)
        for k3 in range(3):
            P.stt(cqn[:, k3, :], cqf[k3][:], vc[:, V_QAG + k3:V_QAG + k3 + 1], rs[:], ALU.mult, ALU.mult,
                  [cqf[k3], vc, rs], [cqn])
        bka, bkb = pbank([0, 1, 3, 4, 5, 6, 7]), pbank([0, 1, 3, 4, 5, 6, 7])
        proj_fm(bka, sl8, 384, 64, [hT], hT_k)
        proj_fm(bkb, sl8, 448, 64, [hT], hT_k)
        ta, tb = gettmp(), gettmp()
        P.tt("dve", ta[0:64, :], bka[0:64, :], ropet[:, 0, :], ALU.mult, [bka, ropet], [ta])
        P.tt("dve", tb[0:64, :], bkb[0:64, :], ropet[:, 1, :], ALU.mult, [bkb, ropet], [tb])
        P.tt("dve", kpe[0:64, t0:t0 + T], ta[0:64, :], tb[0:64, :], ALU.add, [ta, tb], [kpeB[g]])
        sl9 = stream(9)
        ckf = []
        for k2 in range(2):
            bk = pbank([0, 1, 3, 4, 5, 6, 7])
            proj_fm(bk, sl9, k2 * 128, 128, [hT], hT_k)
            cf = gettmp()
            ckf.append(cf)
            P.cp("dve", cf[:], bk[:], [bk], [cf])
            P.act(sqo[:], bk[:], AF.Square, [bk], [sqo])
            P.mm(SSB[:], onesb[:], sqo[:], k2 == 0, k2 == 1, [onesb, sqo], [SSB])
        rs = gettmp()
        rstd_from(SSB, SSB[:], rs, rs[:], 1.0 / 256)
        for k2 in range(2):
            P.stt(ckvn[:, k2, :], ckf[k2][:], vc[:, V_KVAG + k2:V_KVAG + k2 + 1], rs[:], ALU.mult, ALU.mult,
                  [ckf[k2], vc, rs], [ckvn])
        for half in range(2):
            slm = stream(10 + half)
            for hh in range(4):
                bk = pbank([0, 1, 3, 4, 5, 6, 7])
                proj_fm(bk, slm, hh * 128, 128, [hT], hT_k)
                P.act(ybT[half * 4 + hh][:], bk[:], AF.Silu, [bk], [ybT[half * 4 + hh]])

    sidx = [0]
    ptidx = [0]
    psidx = [0]

    def stage_D(g):
        t0 = g * T
        SB3 = [B[3], B[4], B[5]]
        OAc, LAc = B[6], B[7]
        DEPTH = 2

        def nextbank():
            b_ = SB3[sidx[0] % 3]
            sidx[0] += 1
            return b_

        def proj_units(h):
            p2 = h % 2

            def u_k():
                bk = nextbank()
                for k2 in range(2):
                    P.mm(bk[:], wukv[:, k2, h * 256:h * 256 + 128], ckvn[:, k2, :], k2 == 0, k2 == 1,
                         [wukv, ckvn], [bk])
                P.cp("dve", Kn[p2][:], bk[:], [bk], [Kn[p2]])
                P.dma("pool", ksc[h, :, t0:t0 + T], Kn[p2][:], [Kn[p2]], [kscB[g]], Kn[p2])

            def u_v():
                bk = nextbank()
                for i in range(NT):
                    for k2 in range(2):
                        P.mm(bk[:, i * 128:(i + 1) * 128], ckvn[:, k2, i * 128:(i + 1) * 128],
                             wukv[:, k2, h * 256 + 128:h * 256 + 256], (i == 0 and k2 == 0),
                             (i == NT - 1 and k2 == 1), [ckvn, wukv], [bk], skip_group_check=True)
                P.cp("dve", Vn[p2][:].rearrange("p a b -> p (a b)"), bk[:], [bk], [Vn[p2]])
                P.dma("pool", vsc[h, :, g * NT:(g + 1) * NT, :], Vn[p2][:], [Vn[p2]], [vscB[g]], Vn[p2])

            def u_q():
                bk = nextbank()
                for k3 in range(3):
                    P.mm(bk[:], wuq[:, k3, h * 192:h * 192 + 128], cqn[:, k3, :], k3 == 0, k3 == 2,
                         [wuq, cqn], [bk])
                P.act(Qn[p2][:], bk[:], AF.Copy, [bk], [Qn[p2]], scale=QSCALE)

            def u_qa():
                bka = nextbank()
                for k3 in range(3):
                    P.mm(bka[0:64, :], wuq[:, k3, h * 192 + 128:h * 192 + 192], cqn[:, k3, :], k3 == 0, k3 == 2,
                         [wuq, cqn], [bka])
                ta = tmpD[0]
                P.tt("dve", ta[0:64, :], bka[0:64, :], ropet[:, 2, :], ALU.mult, [bka, ropet], [ta])

            def u_qb():
                bkb = nextbank()
                for k3 in range(3):
                    P.mm(bkb[0:64, :], wuqr[:, k3, h, :], cqn[:, k3, :], k3 == 0, k3 == 2, [wuqr, cqn], [bkb])
                ta = tmpD[0]
                P.stt(qpe[p2][0:64, :], bkb[0:64, :], 1.0, ropet[:, 3, :], ALU.mult, ALU.mult,
                      [bkb, ropet], [qpe[p2]])
                P.tt("dve", qpe[p2][0:64, :], qpe[p2][0:64, :], ta[0:64, :], ALU.add, [qpe[p2], ta], [qpe[p2]])

            return [u_k, u_v, u_q, u_qa, u_qb]

        for u in proj_units(0):
            u()
        yield
        for h in range(H):
            p2 = h % 2
            nxt = proj_units(h + 1) if h + 1 < H else []

            blocks = []
            npast = T * g
            ci = 0
            for c0 in range(0, npast, KCH):
                n = min(KCH, npast - c0)
                for kb in range(n // 128):
                    blocks.append(("past", ci, c0, n, kb))
                ci += 1
            for j in range(NT):
                blocks.append(("diag", j))
            nblk = len(blocks)
            state = {}

            def front(bi):
                d = blocks[bi]
                sb_ = nextbank()
                pt_ = Pt[ptidx[0] % NPT]
                ptidx[0] += 1
                if d[0] == "past":
                    _, ci_, c0, n, kb = d
                    kc_, vc_ = Kc[ci_ % 2], Vc[ci_ % 2]
                    if kb == 0:
                        gs = list(range(c0 // T, (c0 + n) // T))
                        P.dma("pool", kc_[:, 0:n], ksc[h, :, c0:c0 + n], [kscB[q] for q in gs], [kc_], kc_)
                        P.dma("pool", vc_[:, 0:n // 128, :], vsc[h, :, c0 // 128:(c0 + n) // 128, :],
                              [vscB[q] for q in gs], [vc_], vc_)
                    klhs, k_tl, kabs = kc_[:, kb * 128:(kb + 1) * 128], kc_, c0 + kb * 128
                    vlhs, v_tl, q0, dj = vc_[:, kb, :], vc_, 0, None
                else:
                    j = d[1]
                    klhs, k_tl, kabs = Kn[p2][:, j * 128:(j + 1) * 128], Kn[p2], t0 + j * 128
                    vlhs, v_tl, q0, dj = Vn[p2][:, j, :], Vn[p2], j * 128, j
                gk = kabs // T
                P.mm(sb_[:, q0:T], klhs, Qn[p2][:, q0:T], True, False, [k_tl, Qn[p2]], [sb_])
                P.mm(sb_[:, q0:T], kpe[:, kabs:kabs + 128], qpe[p2][:, q0:T], False, True,
                     [kpeB[gk], qpe[p2]], [sb_])
                P.act(pt_[:, q0:T], sb_[:, q0:T], AF.Exp, [sb_], [pt_])
                if dj is not None:
                    P.tt("dve", pt_[:, q0:q0 + 128], pt_[:, q0:q0 + 128], trib[:], ALU.mult, [pt_, trib], [pt_])
                state[bi] = (pt_, vlhs, v_tl, q0)

            pending = []
            held = [None]

            def flush_ones(upto=1 << 30):
                while pending and pending[0][2] <= upto:
                    ps_, grp, _ = pending.pop(0)
                    P.mm(LAc[:], onesb[:], ps_[:], grp == 0, grp == nblk // 4 - 1, [onesb, ps_], [LAc],
                         skip_group_check=True)

            def back(bi):
                pt_, vlhs, v_tl, q0 = state.pop(bi)
                first = bi == 0
                last = bi == nblk - 1
                flush_ones(bi)
                P.mm(OAc[:, q0:T], vlhs, pt_[:, q0:T], first, last, [v_tl, pt_], [OAc], skip_group_check=True)
                grp, pos = bi // 4, bi % 4
                ps_ = Ps[(psidx[0] + grp) % 2]
                eng_ = "dve"
                if pos == 0:
                    held[0] = (pt_, q0)
                elif pos == 1:
                    p0_, q00 = held[0]
                    P.tt(eng_, ps_[:, q0:T], p0_[:, q0:T], pt_[:, q0:T], ALU.add, [p0_, pt_], [ps_])
                    if q0 > q00:
                        P.cp(eng_, ps_[:, q00:q0], p0_[:, q00:q0], [p0_], [ps_])
                else:
                    P.tt(eng_, ps_[:, q0:T], ps_[:, q0:T], pt_[:, q0:T], ALU.add, [ps_, pt_], [ps_])
                if pos == 3:
                    pending.append((ps_, grp, bi + 2))

            for it in range(nblk + DEPTH):
                if it < nblk:
                    front(it)
                if it >= DEPTH:
                    back(it - DEPTH)
                if it >= 1 and nxt:
                    nxt.pop(0)()
                yield
            while nxt:
                nxt.pop(0)()
            flush_ones()
            psidx[0] += nblk // 4
            rl = tmpD[0]
            t1 = tmpD[1]
            P.cp("dve", t1[:], OAc[:], [OAc], [t1])
            P.act(rl[:], LAc[:], AF.Ln, [LAc], [rl])
            P.act(rl[:], rl[:], AF.Exp, [rl], [rl], scale=-1.0)
            P.tt("dve", t1[:], t1[:], rl[:], ALU.mult, [t1, rl], [t1])
            P.tt("dve", ybT[h][:], t1[:], ybT[h][:], ALU.mult, [t1, ybT[h]], [ybT[h]])
            yield

    def stage_E(g):
        t0 = g * T
        hT = hTs[g % 2]
        hT_k = lambda kc: hT[:, kc, :]
        xtiles = [(xs[0], xs[0][:]), (xs[1], xs[1][:])]
        for i in range(2):
            P.dma("pool", xs[i][:], x[t0 + i * 128:t0 + (i + 1) * 128, :], [x], [xs[i]], xs[i])
        for c in range(8):
            slc = stream(12 + c)
            bga, bgb, bpa, bpb = [B[(c % 2) * 4 + k_] for k_ in range(4)]
            proj_fm(bga, slc, 0, 128, [hT], hT_k)
            proj_fm(bgb, slc, 128, 128, [hT], hT_k)
            ga, gb_ = gettmp(), gettmp()
            P.act(ga[:], bga[:], AF.Sigmoid, [bga, vc], [ga], bias=vc[:, V_BG + c:V_BG + c + 1])
            P.act(gb_[:], bgb[:], AF.Sigmoid, [bgb, vc], [gb_], bias=vc[:, V_BG + 8 + c:V_BG + 8 + c + 1])
            for kc in range(8):
                P.mm(bpa[:], slc[:, kc, 256:384], yaT[kc // 4][:, kc % 4, :], kc == 0, kc == 7,
                     [slc, yaT[kc // 4]], [bpa])
            for kc in range(8):
                P.mm(bpb[:], slc[:, kc, 384:512], ybT[kc][:], kc == 0, kc == 7, [slc, ybT[kc]], [bpb])
            P.tt("dve", ga[:], ga[:], bpa[:], ALU.mult, [ga, bpa], [ga])
            P.tt("dve", gb_[:], gb_[:], bpb[:], ALU.mult, [gb_, bpb], [gb_])
            P.tt("dve", mT[:, c, :], ga[:], gb_[:], ALU.add, [ga, gb_], [mT])
        for i in range(2, 4):
            yv = yaT[i - 2][:].rearrange("p a b -> p (a b)").bitcast(F32)
            xtiles.append((yaT[i - 2], yv))
            P.dma("pool", yv, x[t0 + i * 128:t0 + (i + 1) * 128, :], [x], [yaT[i - 2]], yaT[i - 2])
        slo = [stream(20), stream(21)]
        for i in range(NT):
            xt, xv = xtiles[i]
            r0 = t0 + i * 128
            for hf_ in range(2):
                bo = B[4 + (i % 2) * 2 + hf_]
                for kc in range(8):
                    P.mm(bo[:], mT[:, kc, i * 128:(i + 1) * 128], slo[hf_][:, kc, :], kc == 0, kc == 7,
                         [mT, slo[hf_]], [bo])
                P.tt("dve", xv[:, hf_ * 512:(hf_ + 1) * 512], xv[:, hf_ * 512:(hf_ + 1) * 512], bo[:], ALU.add,
                     [xt, bo], [xt])
            P.act(hb[:], xv, AF.Square, [xt], [hb, st4], accum_out=st4[:, 2 + i:3 + i])
            rstd_from(st4, st4[:, 2 + i:3 + i], st4, st4[:, 2 + i:3 + i], 1.0 / D)
            P.stt(xv, xv, st4[:, 2 + i:3 + i], fgb[:], ALU.mult, ALU.mult, [xt, st4, fgb], [xt])
            ob = Buf(f"out{g}_{i}")
            P.dma("pool", out[r0:r0 + 128, :], xv, [xt], [ob], xt)
            out_tiles.append(ob)

    NB_UNITS = 2 * (2 * 7 + 4 + 4 + 1 + 4 * 5)
    for _ in stage_A(0):
        pass
    for g in range(NG):
        stage_C(g)
        gens = [stage_B(g)]
        if g + 1 < NG:
            gens.append(stage_A(g + 1))
        gd = stage_D(g)
        nb_left = NB_UNITS
        nd_left = H * (4 * g + 4 + 4)
        d_alive = True
        while gens or d_alive:
            for gen in list(gens):
                try:
                    next(gen)
                except StopIteration:
                    gens.remove(gen)
            nb_left -= 1
            if d_alive:
                k = (1 << 30) if not gens else max(1, -(-nd_left // max(nb_left, 1)))
                for _ in range(k):
                    try:
                        next(gd)
                        nd_left -= 1
                    except StopIteration:
                        d_alive = False
                        break
        if stop_after == "D":
            break
        stage_E(g)

    P.emit(out_tiles + list(dbg_outs.values()))
    return nc, P


def host_consts(S):
    cst = np.zeros((128, 896), np.float32)
    cst[:, 0:128] = np.eye(128, dtype=np.float32)
    s = np.arange(128)[:, None]
    t = np.arange(128)[None, :]
    cst[:, 128:256] = ((s // 32 == t // 32) & (s <= t)).astype(np.float32)
    cst[:, 256:384] = (t >= s).astype(np.float32)
    rm = np.ones((128, 512), np.float32)
    rm[:, ::32] = 0.0
    cst[:, 384:896] = rm
    inv = (np.float32(10000.0) ** (-np.arange(0, 64, 2, dtype=np.float32) / np.float32(64))).astype(np.float32)
    ang = (np.arange(S, dtype=np.float32)[:, None] * inv[None, :]).astype(np.float32)
    cos = np.cos(ang).astype(np.float32).T
    sin = np.sin(ang).astype(np.float32).T
    rope = np.zeros((2, 64, S), np.float32)
    rope[0, 0:32] = cos
    rope[0, 32:64] = cos
    rope[1, 0:32] = -sin
    rope[1, 32:64] = sin
    return cst, rope


def pc(v):
    v = np.asarray(v, np.float32).reshape(-1, 128)
    return np.ascontiguousarray(v.T)


def make_in_maps(inputs, S):
    cst, rope = host_consts(S)
    vecs = np.concatenate([
        pc(inputs["norm_g"][0]), pc(inputs["b_gate"][0]), pc(inputs["lb_logits"][0]), pc(inputs["lb_logits"][1]),
        pc(inputs["hg_norm_g"][0]), pc(inputs["q_a_g"][0]), pc(inputs["kv_a_g"][0])], axis=1)
    assert vecs.shape == (128, 46)
    fgb = np.ascontiguousarray(np.broadcast_to(np.asarray(inputs["final_norm_g"], np.float32)[None, :], (128, D)))
    common = {
        "w_in": np.ascontiguousarray(inputs["w_in"][0], dtype=np.float32),
        "w_uq": np.ascontiguousarray(inputs["w_uq"][0], dtype=np.float32),
        "w_ukv": np.ascontiguousarray(inputs["w_ukv"][0], dtype=np.float32),
        "w_pa": np.ascontiguousarray(inputs["w_proj_a"][0], dtype=np.float32),
        "w_pb": np.ascontiguousarray(inputs["w_proj_b"][0], dtype=np.float32),
        "w_out": np.ascontiguousarray(inputs["w_out"][0], dtype=np.float32),
        "vecs": vecs, "fgb": fgb, "cst": cst, "rope": rope,
    }
    xa = np.asarray(inputs["x"], np.float32)
    return [dict(common, x=np.ascontiguousarray(xa[b])) for b in range(xa.shape[0])]


_CACHE = {}


def kernel(**inputs):
    xa = np.asarray(inputs["x"])
    Bn, S, _ = xa.shape
    if S not in _CACHE:
        _CACHE[S] = build(S)[0]
    nc = _CACHE[S]
    in_maps = make_in_maps(inputs, S)
    res = run_bass_kernel_spmd(nc, in_maps, core_ids=list(range(Bn)))
    return np.stack([np.asarray(r["out"], np.float32) for r in res.results], axis=0)
```

```python
import numpy as np
from contextlib import ExitStack
import concourse.bass as bass
import concourse.mybir as mybir
from concourse.bass_utils import run_bass_kernel_spmd

F32 = mybir.dt.float32
BF16 = mybir.dt.bfloat16
ALU = mybir.AluOpType
AF = mybir.ActivationFunctionType

D = 1024
H = 8
T = 512
NT = 4
EPS = 1e-6
QSCALE = 192.0 ** -0.5
IN_COLS = 7872
O_HQ, O_HF, O_HI, O_HZ, O_CQ, O_CKV, O_KR, O_MZ, O_GL = 0, 1024, 2048, 3072, 4096, 4480, 4736, 4800, 5824

COMPUTE = ("pe", "act", "dve", "pool")
SEM_CHUNK = 12000


class Buf:
    __slots__ = ("name", "writers", "readers", "dsem", "dcount", "psum")

    def __init__(self, name):
        self.name = name
        self.writers = {}
        self.readers = []
        self.dsem = None
        self.dcount = 0
        self.psum = False


class Tl(Buf):
    __slots__ = ("t",)

    def __init__(self, name, t):
        Buf.__init__(self, name)
        self.t = t

    def __getitem__(self, k):
        return self.t[k]


class Op:
    __slots__ = ("eng", "fn", "deps", "flag", "tok", "is_dma", "dbuf", "dval")

    def __init__(self, eng, fn, is_dma, dbuf):
        self.eng = eng
        self.fn = fn
        self.deps = []
        self.flag = False
        self.tok = None
        self.is_dma = is_dma
        self.dbuf = dbuf
        self.dval = 0


class Prog:
    def __init__(self, nc):
        self.nc = nc
        self.q = {k: [] for k in ("pe", "act", "dve", "pool", "sp")}
        self.dma_bufs = []
        self.owners = {}

    def sb(self, name, shape, dtype):
        return Tl(name, self.nc.alloc_sbuf_tensor(name, list(shape), dtype))

    def ps(self, name, shape, dtype=F32):
        t = Tl(name, self.nc.alloc_psum_tensor(name, list(shape), dtype))
        t.psum = True
        return t

    def dram(self, name, shape, dtype, kind="Internal"):
        return Tl(name, self.nc.dram_tensor(name, list(shape), dtype, kind=kind))

    def op(self, eng, fn, reads=(), writes=(), dma_dst=None):
        is_dma = dma_dst is not None
        o = Op(eng, fn, is_dma, dma_dst)
        deps = {}

        def add(d, kind):
            if d is o:
                return
            if d.is_dma:
                if is_dma and kind == "waw" and d.dbuf is dma_dst:
                    return
                deps[id(d)] = d
                return
            if (not is_dma) and d.eng == eng:
                if eng == "pe" or kind != "raw":
                    return
            deps[id(d)] = d

        for b in reads:
            for w in b.writers.values():
                add(w, "raw")
            if b.psum:
                for r in b.readers:
                    if r.eng != eng:
                        add(r, "raw")
        for b in writes:
            for r in b.readers:
                add(r, "war")
            for w in b.writers.values():
                add(w, "waw")
        o.deps = list(deps.values())
        for d in o.deps:
            if not d.is_dma:
                d.flag = True
        for b in reads:
            if not is_dma:
                b.readers = [r for r in b.readers if r.is_dma or r.eng != eng]
            b.readers.append(o)
        for b in writes:
            b.readers = []
            b.writers = {(("dma", id(dma_dst)) if is_dma else eng): o}
        if is_dma:
            if dma_dst.dcount == 0:
                self.dma_bufs.append(dma_dst)
            dma_dst.dcount += 1
            o.dval = 16 * dma_dst.dcount
        self.q[eng].append(o)
        return o

    def mm(self, out, lhsT, rhs, start, stop, reads, writes, **kw):
        return self.op("pe", lambda e: e.matmul(out, lhsT, rhs, start=start, stop=stop, **kw), reads, writes)

    def tr(self, out, in_, ident, reads, writes):
        return self.op("pe", lambda e: e.transpose(out, in_, ident), reads, writes)

    def act(self, out, in_, func, reads, writes, **kw):
        return self.op("act", lambda e: e.activation(out, in_, func, **kw), reads, writes)

    def dma(self, eng, out, in_, reads, writes, dst):
        key = (id(dst), eng)
        if key not in self.owners:
            self.owners[key] = Buf(dst.name + "@" + eng)
        return self.op(eng, lambda e: e.dma_start(out, in_), reads, writes, dma_dst=self.owners[key])

    def tt(self, eng, out, in0, in1, op, reads, writes):
        return self.op(eng, lambda e: e.tensor_tensor(out, in0, in1, op), reads, writes)

    def ts(self, eng, out, in0, s1, s2, op0, op1, reads, writes):
        return self.op(eng, lambda e: e.tensor_scalar(out, in0, s1, s2, op0, op1), reads, writes)

    def stt(self, out, in0, scalar, in1, op0, op1, reads, writes):
        return self.op("dve", lambda e: e.scalar_tensor_tensor(out, in0, scalar, in1, op0, op1), reads, writes)

    def cp(self, eng, out, in_, reads, writes):
        if eng == "act":
            return self.op("act", lambda e: e.activation(out, in_, AF.Copy), reads, writes)
        return self.op(eng, lambda e: e.tensor_copy(out, in_), reads, writes)

    def emit(self, final_reads=()):
        nc = self.nc
        self.op("sp", None, reads=final_reads)
        with ExitStack() as es:
            esems = {}
            for k in COMPUTE:
                n = sum(1 for o in self.q[k] if o.flag)
                ns = max(1, (n + SEM_CHUNK - 1) // SEM_CHUNK)
                esems[k] = [es.enter_context(nc.semaphore(f"s_{k}{i}")) for i in range(ns)]
                c = 0
                for o in self.q[k]:
                    if o.flag:
                        o.tok = (esems[k][c // SEM_CHUNK], (c % SEM_CHUNK) + 1)
                        c += 1
            for i, b in enumerate(self.dma_bufs):
                b.dsem = es.enter_context(nc.semaphore(f"d{i}"))
            for k in self.q:
                for o in self.q[k]:
                    if o.is_dma:
                        o.tok = (o.dbuf.dsem, o.dval)
            self.nsem = sum(len(v) for v in esems.values()) + len(self.dma_bufs)
            block = es.enter_context(nc.Block())

            def run(e, k):
                waited = {}
                for o in self.q[k]:
                    need = {}
                    for d in o.deps:
                        s, v = d.tok
                        sid = id(s)
                        if waited.get(sid, 0) >= v:
                            continue
                        if sid not in need or need[sid][1] < v:
                            need[sid] = (s, v)
                    for sid, (s, v) in need.items():
                        e.wait_ge(s, v)
                        waited[sid] = v
                    if o.fn is None:
                        continue
                    ins = o.fn(e)
                    if o.is_dma:
                        ins.then_inc(o.tok[0], 16)
                    elif o.flag:
                        ins.then_inc(o.tok[0], 1)

            @block.tensor
            def _(e):
                run(e, "pe")

            @block.scalar
            def _(e):
                run(e, "act")

            @block.vector
            def _(e):
                run(e, "dve")

            @block.gpsimd
            def _(e):
                run(e, "pool")

            @block.sync
            def _(e):
                run(e, "sp")


def build(S, dbg=None, stop_after=None):
    NG = S // T
    nc = bass.Bass("TRN2", target_bir_lowering=False)
    P = Prog(nc)
    dbg_outs = {}

    x = P.dram("x", [S, D], F32, kind="ExternalInput")
    w_in = P.dram("w_in", [D, IN_COLS], F32, kind="ExternalInput")
    w_uq = P.dram("w_uq", [384, 1536], F32, kind="ExternalInput")
    w_ukv = P.dram("w_ukv", [256, 2048], F32, kind="ExternalInput")
    w_pa = P.dram("w_pa", [D, D], F32, kind="ExternalInput")
    w_pb = P.dram("w_pb", [D, D], F32, kind="ExternalInput")
    w_out = P.dram("w_out", [D, D], F32, kind="ExternalInput")
    vecs = P.dram("vecs", [128, 46], F32, kind="ExternalInput")
    fgb_d = P.dram("fgb", [128, D], F32, kind="ExternalInput")
    cst_d = P.dram("cst", [128, 896], F32, kind="ExternalInput")
    rope_d = P.dram("rope", [2, 64, S], F32, kind="ExternalInput")
    out = P.dram("out", [S, D], F32, kind="ExternalOutput")

    NCH = 22
    wsc = nc.dram_tensor("wsc", [NCH, 128, 8, 512], BF16)
    wscB = [Buf(f"wsc{c}") for c in range(NCH)]
    ksc = nc.dram_tensor("ksc", [H, 128, S], BF16)
    vsc = nc.dram_tensor("vsc", [H, 128, S // 128, 128], BF16)
    kscB = [Buf(f"ksc{g}") for g in range(NG)]
    vscB = [Buf(f"vsc{g}") for g in range(NG)]

    cstf = P.sb("cstf", [128, 512], F32)
    identb = P.sb("identb", [128, 128], BF16)
    maskbd = P.sb("maskbd", [128, 128], BF16)
    trib = P.sb("trib", [128, 128], BF16)
    onesb = P.sb("onesb", [128, 128], BF16)
    vc = P.sb("vc", [128, 46], F32)
    lbv = P.sb("lbv", [128, 24], F32)
    fgb = P.sb("fgbs", [128, D], F32)
    wuq = P.sb("wuq", [128, 3, 1536], BF16)
    wuqr = P.sb("wuqr", [128, 3, 8, 64], BF16)
    wukv = P.sb("wukv", [128, 2, 2048], BF16)
    NSLOT = 3
    slots = [P.sb(f"slot{i}", [128, 8, 512], BF16) for i in range(NSLOT)]
    xs = [P.sb(f"xs{i}", [128, D], F32) for i in range(2)]
    hb = P.sb("hb", [128, D], BF16)
    st4 = P.sb("st4", [128, 8], F32)
    hTs = [P.sb(f"hT{i}", [128, 8, T], BF16) for i in range(2)]
    NTMP = 9
    tmp = [P.sb(f"tmp{i}", [128, T], F32) for i in range(NTMP)]
    qin = [P.sb(f"qin{i}", [128, T], BF16) for i in range(4)]
    kin = [P.sb(f"kin{i}", [128, T], BF16) for i in range(4)]
    koT = [P.sb(f"koT{i}", [128, T], BF16) for i in range(2)]
    ko = [P.sb(f"ko{i}", [128, NT, 128], BF16) for i in range(4)]
    szh = P.sb("szh", [128, 4, T], BF16)
    vT = P.sb("vT", [128, NT, 512], BF16)
    dec = [P.sb(f"dec{i}", [128, 16], F32) for i in range(4)]
    Sst = [P.sb(f"Sst{i}", [128, 4, 128], F32) for i in range(2)]
    Sbf = [P.sb(f"Sbf{i}", [128, 4, 128], BF16) for i in range(2)]
    scm = P.sb("scm", [128, 4, 128], BF16)
    sqo = P.sb("sqo", [128, T], BF16)
    yaT = [P.sb(f"yaT{i}", [128, 4, T], BF16) for i in range(2)]
    ybT = [P.sb(f"ybT{h}", [128, T], BF16) for h in range(H)]
    mT = P.sb("mT", [128, 8, T], BF16)
    cqn = P.sb("cqn", [128, 3, T], BF16)
    ckvn = P.sb("ckvn", [128, 2, T], BF16)
    kpe = P.sb("kpe", [128, S], BF16)
    kpeB = [Buf(f"kpe{g}") for g in range(NG)]
    ropet = P.sb("ropet", [64, 4, T], F32)
    Kn = [P.sb(f"Kn{i}", [128, T], BF16) for i in range(2)]
    Vn = [P.sb(f"Vn{i}", [128, NT, 128], BF16) for i in range(2)]
    Qn = [P.sb(f"Qn{i}", [128, T], BF16) for i in range(2)]
    qpe = [P.sb(f"qpe{i}", [128, T], BF16) for i in range(2)]
    tmpD = [P.sb(f"tmpD{i}", [128, T], F32) for i in range(2)]
    KCH = 1024
    Kc = [P.sb(f"Kc{i}", [128, KCH], BF16) for i in range(2)]
    Vc = [P.sb(f"Vc{i}", [128, KCH // 128, 128], BF16) for i in range(2)]
    NPT = 6
    Pt = [P.sb(f"Pt{i}", [128, T], BF16) for i in range(NPT)]
    Ps = [P.sb(f"Ps{i}", [128, T], BF16) for i in range(2)]

    B = [P.ps(f"bank{i}", [128, 512], F32) for i in range(8)]

    tmp_i = [0]

    def gettmp():
        t = tmp[tmp_i[0] % NTMP]
        tmp_i[0] += 1
        return t

    def tap(name, tl, ap, shape, dtype=F32):
        if dbg is None or name not in dbg:
            return
        d = P.dram("dbg_" + name, list(shape), dtype, kind="ExternalOutput")
        P.dma("sp", d[:], ap, [tl], [d], tl)
        dbg_outs[name] = d

    P.dma("sp", cstf[:], cst_d[:, 384:896], [cst_d], [cstf], cstf)
    P.dma("sp", tmp[0][:, 0:384], cst_d[:, 0:384], [cst_d], [tmp[0]], tmp[0])
    P.dma("sp", vc[:], vecs[:], [vecs], [vc], vc)
    P.dma("sp", fgb[:], fgb_d[:], [fgb_d], [fgb], fgb)
    P.cp("dve", identb[:], tmp[0][:, 0:128], [tmp[0]], [identb])
    P.cp("dve", maskbd[:], tmp[0][:, 128:256], [tmp[0]], [maskbd])
    P.cp("dve", trib[:], tmp[0][:, 256:384], [tmp[0]], [trib])
    resetm = cstf
    P.op("dve", lambda e: e.memset(onesb[:], 1.0), [], [onesb])
    P.op("pool", lambda e: e.memset(kpe[64:128, :], 0.0), [], [kpeB[g_] for g_ in range(NG)])
    for i_ in range(2):
        P.op("pool", lambda e, i_=i_: e.memset(qpe[i_][64:128, :], 0.0), [], [qpe[i_]])
    for h in range(2):
        P.op("pool", lambda e, h=h: e.memset(Sst[h][:], 0.0), [], [Sst[h]])
        P.op("pool", lambda e, h=h: e.memset(Sbf[h][:], 0.0), [], [Sbf[h]])
    V_NG, V_BG, V_L0, V_L1, V_HGG, V_QAG, V_KVAG = 0, 8, 24, 32, 40, 41, 44
    P.tt("dve", lbv[:, 16:24], vc[:, V_L0:V_L0 + 8], vc[:, V_L1:V_L1 + 8], ALU.subtract, [vc], [lbv])
    P.act(lbv[:, 0:8], lbv[:, 16:24], AF.Sigmoid, [lbv], [lbv])
    P.act(lbv[:, 8:16], lbv[:, 16:24], AF.Sigmoid, [lbv], [lbv], scale=-1.0)
    P.ts("dve", lbv[:, 16:24], lbv[:, 8:16], -1.0, None, ALU.mult, ALU.bypass, [lbv], [lbv])

    def wsrc(wt, c0, n):
        return wt.t.ap().rearrange("(kc p) c -> p kc c", p=128)[:, :, c0:c0 + n]

    def conv(ci, col, wt, c0, n):
        P.dma("pool", wsc[ci, :, :, col:col + n], wsrc(wt, c0, n), [wt], [wscB[ci]], wscB[ci])

    conv(8, 0, w_in, O_CQ, 384)
    conv(8, 384, w_in, O_KR, 64)
    conv(8, 448, w_in, O_KR + 32, 32)
    conv(8, 480, w_in, O_KR, 32)
    conv(9, 0, w_in, O_CKV, 256)
    conv(10, 0, w_in, O_MZ, 512)
    conv(11, 0, w_in, O_MZ + 512, 512)
    P.dma("pool", wuq[:], w_uq.t.ap().rearrange("(kc p) c -> p kc c", p=128), [w_uq], [wuq], wuq)
    for hf_ in range(2):
        P.dma("pool", wukv[:, :, hf_ * 1024:(hf_ + 1) * 1024],
              w_ukv.t.ap().rearrange("(kc p) c -> p kc c", p=128)[:, :, hf_ * 1024:(hf_ + 1) * 1024],
              [w_ukv], [wukv], wukv)
    for half in range(2):
        for j, o in enumerate((O_HQ, O_HF, O_HI, O_HZ)):
            conv(half * 4 + j, 0, w_in, o + half * 512, 512)
    for c in range(8):
        conv(12 + c, 0, w_in, O_GL + c * 128, 128)
        conv(12 + c, 128, w_in, O_GL + 1024 + c * 128, 128)
        conv(12 + c, 256, w_pa, c * 128, 128)
        conv(12 + c, 384, w_pb, c * 128, 128)
    conv(20, 0, w_out, 0, 512)
    conv(21, 0, w_out, 512, 512)
    CH_NCOL = [512] * 8 + [512, 256, 512, 512] + [512] * 8 + [512, 512]
    wuq4 = wuq[:, :, :].rearrange("p k (h c) -> p k h c", c=192)
    P.cp("dve", wuqr[:, :, :, 0:32], wuq4[:, :, :, 160:192], [wuq], [wuqr])
    P.cp("dve", wuqr[:, :, :, 32:64], wuq4[:, :, :, 128:160], [wuq], [wuqr])

    sstate = {"n": 0}

    def stream(ci):
        sl = slots[sstate["n"] % NSLOT]
        sstate["n"] += 1
        n = CH_NCOL[ci]
        P.dma("sp", sl[:, :, 0:n], wsc[ci, :, :, 0:n], [wscB[ci]], [sl], sl)
        return sl

    bank_rr = [0]

    def pbank(cands):
        b = cands[bank_rr[0] % len(cands)]
        bank_rr[0] += 1
        return B[b]

    def proj_fm(bank, sl, col, m, rhs_tl, rhs_of_kc, nk=8, rows=128):
        for kc in range(nk):
            P.mm(bank[0:m, 0:T], sl[0:rows, kc, col:col + m], rhs_of_kc(kc), kc == 0, kc == nk - 1,
                 [sl] + rhs_tl, [bank])

    def rstd_from(bank_or_tl, src_ap, dst_tl, dst_ap, scale):
        P.act(dst_ap, src_ap, AF.Ln, [bank_or_tl], [dst_tl], scale=scale, bias=EPS)
        P.act(dst_ap, dst_ap, AF.Exp, [dst_tl], [dst_tl], scale=-0.5)

    out_tiles = []

    def stage_A(g):
        t0 = g * T
        hT = hTs[g % 2]

        def load(i):
            xt = xs[i % 2]
            P.dma("sp", xt[:], x[t0 + i * 128:t0 + (i + 1) * 128, :], [x], [xt], xt)

        load(0)
        load(1)
        yield
        for i in range(NT):
            xt = xs[i % 2]
            P.act(hb[:], xt[:], AF.Square, [xt], [hb, st4], accum_out=st4[:, 0:1])
            yield
            rstd_from(st4, st4[:, 0:1], st4, st4[:, 1:2], 1.0 / D)
            P.act(hb[:], xt[:], AF.Copy, [xt, st4], [hb], scale=st4[:, 1:2])
            if i + 2 < NT:
                load(i + 2)
            yield
            ptb = B[2][:].bitcast(BF16)
            for kc in range(8):
                P.tr(ptb[:, kc * 128:(kc + 1) * 128], hb[:, kc * 128:(kc + 1) * 128], identb[:],
                     [hb, identb], [B[2]])
            P.tt("dve", hT[:, :, i * 128:(i + 1) * 128], ptb.rearrange("p (k t) -> p k t", t=128),
                 vc[:, V_NG:V_NG + 8].unsqueeze(2).to_broadcast([128, 8, 128]), ALU.mult, [B[2], vc], [hT])
            yield

    def stage_B(g):
        hT = hTs[g % 2]
        hT_k = lambda kc: hT[:, kc, :]
        for half in range(2):
            sl_q = stream(half * 4 + 0)
            sl_f = stream(half * 4 + 1)
            for pair in range(2):
                hhs = [pair * 2, pair * 2 + 1]
                sg, lf, eb, en = {}, {}, {}, {}
                for hh in hhs:
                    h = half * 4 + hh
                    bk = pbank([0, 1])
                    proj_fm(bk, sl_f, hh * 128, 128, [hT], hT_k)
                    sg[hh] = gettmp()
                    P.act(sg[hh][:], bk[:], AF.Sigmoid, [bk], [sg[hh]])
                yield
                for hh in hhs:
                    h = half * 4 + hh
                    lf[hh] = gettmp()
                    P.act(lf[hh][:], sg[hh][:], AF.Ln, [sg[hh], lbv], [lf[hh]],
                          scale=lbv[:, 8 + h:9 + h], bias=lbv[:, h:h + 1])
                    P.op("dve", lambda e, a=lf[hh]: e.tensor_tensor_scan(a[:], resetm[:, 0:512], a[:], 0.0,
                                                                          ALU.mult, ALU.add),
                         [cstf, lf[hh]], [lf[hh]])
                    P.ts("dve", sg[hh][:], sg[hh][:], lbv[:, 16 + h:17 + h], lbv[:, 8 + h:9 + h], ALU.mult, ALU.add,
                         [sg[hh], lbv], [sg[hh]])
                yield
                for hh in hhs:
                    eb[hh] = gettmp()
                    en[hh] = gettmp()
                    P.act(eb[hh][:], lf[hh][:], AF.Exp, [lf[hh]], [eb[hh]])
                    P.act(en[hh][:], lf[hh][:], AF.Exp, [lf[hh]], [en[hh]], scale=-1.0)
                    b3 = lf[hh][:, :].rearrange("p (c t) -> p c t", t=32)
                    P.tt("dve", b3, b3[:, :, 31:32].to_broadcast([128, 16, 32]), b3, ALU.subtract,
                         [lf[hh]], [lf[hh]])
                    P.act(lf[hh][:], lf[hh][:], AF.Exp, [lf[hh]], [lf[hh]])
                    P.cp("dve", dec[hh][:], eb[hh][:, :].rearrange("p (c t) -> p c t", t=32)[:, :, 31],
                         [eb[hh]], [dec[hh]])
                yield
                for hh in hhs:
                    bk = pbank([0, 1])
                    proj_fm(bk, sl_q, hh * 128, 128, [hT], hT_k)
                    sq = gettmp()
                    P.act(sq[:], bk[:], AF.Silu, [bk], [sq])
                    P.tt("dve", qin[hh][:], sq[:], eb[hh][:], ALU.mult, [sq, eb[hh]], [qin[hh]])
                    P.tt("dve", kin[hh][:], sg[hh][:], en[hh][:], ALU.mult, [sg[hh], en[hh]], [kin[hh]])
                    kt = koT[hh % 2]
                    P.tt("dve", kt[:], sg[hh][:], lf[hh][:], ALU.mult, [sg[hh], lf[hh]], [kt])
                    yield
                    trb = B[2][:].bitcast(BF16)
                    for i in range(NT):
                        P.tr(trb[:, i * 128:(i + 1) * 128], kt[:, i * 128:(i + 1) * 128], identb[:],
                             [kt, identb], [B[2]])
                    P.cp("dve", ko[hh][:].rearrange("p a b -> p (a b)"), trb[:, 0:512], [B[2]], [ko[hh]])
                    yield
            sl_i = stream(half * 4 + 2)
            for i in range(NT):
                bk = pbank([0, 1])
                for kc in range(8):
                    P.mm(bk[:], hT[:, kc, i * 128:(i + 1) * 128], sl_i[:, kc, :], kc == 0, kc == 7,
                         [hT, sl_i], [bk])
                P.cp("dve", vT[:, i, :], bk[:], [bk], [vT])
                yield
            sl_z = stream(half * 4 + 3)
            for hh in range(4):
                bk = pbank([0, 1])
                proj_fm(bk, sl_z, hh * 128, 128, [hT], hT_k)
                P.act(szh[:, hh, :], bk[:], AF.Silu, [bk], [szh])
                yield
            SC, OA, DS, SSB = B[0], B[1], B[2], B[0]
            def sc_step(i):
                for hh in range(4):
                    P.mm(SC[:, hh * 128:(hh + 1) * 128], kin[hh][:, i * 128:(i + 1) * 128],
                         qin[hh][:, i * 128:(i + 1) * 128], True, True, [kin[hh], qin[hh]], [SC])
                P.tt("dve", scm[:], SC[:, :].rearrange("p (h t) -> p h t", t=128),
                     maskbd[:, :].unsqueeze(1).to_broadcast([128, 4, 128]), ALU.mult, [SC, maskbd], [scm])

            sc_step(0)
            yield
            for i in range(NT):
                for hh in range(4):
                    P.mm(OA[:, hh * 128:(hh + 1) * 128], vT[:, i, hh * 128:(hh + 1) * 128], scm[:, hh, :],
                         hh == 0, False, [vT, scm], [OA], skip_group_check=True)
                for j in range(4):
                    for hh in range(4):
                        h = half * 4 + hh
                        c0 = i * 128 + j * 32
                        P.mm(OA[:, hh * 128 + j * 32:hh * 128 + (j + 1) * 32], Sbf[half][:, hh, :],
                             qin[hh][:, c0:c0 + 32], False, (j == 3 and hh == 3), [Sbf[half], qin[hh]], [OA],
                             skip_group_check=True)
                    for hh in range(4):
                        P.mm(DS[:, hh * 128:(hh + 1) * 128], ko[hh][32 * j:32 * (j + 1), i, :],
                             vT[32 * j:32 * (j + 1), i, hh * 128:(hh + 1) * 128], True, True,
                             [ko[hh], vT], [DS], tile_position=(32 * j, 0), skip_group_check=True)
                    for hh in range(4):
                        h = half * 4 + hh
                        cidx = i * 4 + j
                        P.stt(Sst[half][:, hh, :], Sst[half][:, hh, :], dec[hh][:, cidx:cidx + 1],
                              DS[:, hh * 128:(hh + 1) * 128], ALU.mult, ALU.add, [Sst[half], dec[hh], DS], [Sst[half]])
                    P.cp("dve", Sbf[half][:], Sst[half][:], [Sst[half]], [Sbf[half]])
                    yield
                if i + 1 < NT:
                    sc_step(i + 1)
                P.act(sqo[:], OA[:], AF.Square, [OA], [sqo])
                P.mm(SSB[:], onesb[:], sqo[:], True, True, [onesb, sqo], [SSB])
                rs = gettmp()
                rstd_from(SSB, SSB[:], rs, rs[:], 1.0 / 128)
                t1 = gettmp()
                P.stt(t1[:], OA[:], vc[:, V_HGG:V_HGG + 1], rs[:], ALU.mult, ALU.mult, [OA, vc, rs], [t1])
                P.tt("dve", yaT[half][:, :, i * 128:(i + 1) * 128], t1[:, :].rearrange("p (h t) -> p h t", t=128),
                     szh[:, :, i * 128:(i + 1) * 128], ALU.mult, [t1, szh], [yaT[half]])
                yield

    def stage_C(g):
        t0 = g * T
        hT = hTs[g % 2]
        hT_k = lambda kc: hT[:, kc, :]
        P.dma("sp", ropet[:, 0, :], rope_d[0, :, t0:t0 + T], [rope_d], [ropet], ropet)
        P.dma("sp", ropet[:, 1, :], rope_d[1, :, t0:t0 + T], [rope_d], [ropet], ropet)
        P.ts("dve", ropet[:, 2:4, :], ropet[:, 0:2, :], QSCALE, None, ALU.mult, ALU.bypass, [ropet], [ropet])
        sl8 = stream(8)
        SSB = B[2]
        cqf = []
        for k3 in range(3):
            bk = pbank([0, 1, 3, 4, 5, 6, 7])
            proj_fm(bk, sl8, k3 * 128, 128, [hT], hT_k)
            cf = gettmp()
            cqf.append(cf)
            P.cp("dve", cf[:], bk[:], [bk], [cf])
            P.act(sqo[:], bk[:], AF.Square, [bk], [sqo])
            P.mm(SSB[:], onesb[:], sqo[:], k3 == 0, k3 == 2, [onesb, sqo], [SSB])
        rs = gettmp()
        rstd_from(SSB, SSB[:], rs, rs[:], 1.0 / 384)
        for k3 in range(3):
            P.stt(cqn[:, k3, :], cqf[k3][:], vc[:, V_QAG + k3:V_QAG + k3 + 1], rs[:], ALU.mult, ALU.mult,
                  [cqf[k3], vc, rs], [cqn])
        bka, bkb = pbank([0, 1, 3, 4, 5, 6, 7]), pbank([0, 1, 3, 4, 5, 6, 7])
        proj_fm(bka, sl8, 384, 64, [hT], hT_k)
        proj_fm(bkb, sl8, 448, 64, [hT], hT_k)
        ta, tb = gettmp(), gettmp()
        P.tt("dve", ta[0:64, :], bka[0:64, :], ropet[:, 0, :], ALU.mult, [bka, ropet], [ta])
        P.tt("dve", tb[0:64, :], bkb[0:64, :], ropet[:, 1, :], ALU.mult, [bkb, ropet], [tb])
        P.tt("dve", kpe[0:64, t0:t0 + T], ta[0:64, :], tb[0:64, :], ALU.add, [ta, tb], [kpeB[g]])
        sl9 = stream(9)
        ckf = []
        for k2 in range(2):
            bk = pbank([0, 1, 3, 4, 5, 6, 7])
            proj_fm(bk, sl9, k2 * 128, 128, [hT], hT_k)
            cf = gettmp()
            ckf.append(cf)
            P.cp("dve", cf[:], bk[:], [bk], [cf])
            P.act(sqo[:], bk[:], AF.Square, [bk], [sqo])
            P.mm(SSB[:], onesb[:], sqo[:], k2 == 0, k2 == 1, [onesb, sqo], [SSB])
        rs = gettmp()
        rstd_from(SSB, SSB[:], rs, rs[:], 1.0 / 256)
        for k2 in range(2):
            P.stt(ckvn[:, k2, :], ckf[k2][:], vc[:, V_KVAG + k2:V_KVAG + k2 + 1], rs[:], ALU.mult, ALU.mult,
                  [ckf[k2], vc, rs], [ckvn])
        for half in range(2):
            slm = stream(10 + half)
            for hh in range(4):
                bk = pbank([0, 1, 3, 4, 5, 6, 7])
                proj_fm(bk, slm, hh * 128, 128, [hT], hT_k)
                P.act(ybT[half * 4 + hh][:], bk[:], AF.Silu, [bk], [ybT[half * 4 + hh]])

    sidx = [0]
    ptidx = [0]
    psidx = [0]

    def stage_D(g):
        t0 = g * T
        SB3 = [B[3], B[4], B[5]]
        OAc, LAc = B[6], B[7]
        DEPTH = 3

        def nextbank():
            b_ = SB3[sidx[0] % 3]
            sidx[0] += 1
            return b_

        def proj_units(h):
            p2 = h % 2

            def u_k():
                bk = nextbank()
                for k2 in range(2):
                    P.mm(bk[:], wukv[:, k2, h * 256:h * 256 + 128], ckvn[:, k2, :], k2 == 0, k2 == 1,
                         [wukv, ckvn], [bk])
                P.cp("dve", Kn[p2][:], bk[:], [bk], [Kn[p2]])
                P.dma("pool", ksc[h, :, t0:t0 + T], Kn[p2][:], [Kn[p2]], [kscB[g]], Kn[p2])

            def u_v():
                bk = nextbank()
                for i in range(NT):
                    for k2 in range(2):
                        P.mm(bk[:, i * 128:(i + 1) * 128], ckvn[:, k2, i * 128:(i + 1) * 128],
                             wukv[:, k2, h * 256 + 128:h * 256 + 256], (i == 0 and k2 == 0),
                             (i == NT - 1 and k2 == 1), [ckvn, wukv], [bk], skip_group_check=True)
                P.cp("dve", Vn[p2][:].rearrange("p a b -> p (a b)"), bk[:], [bk], [Vn[p2]])
                P.dma("pool", vsc[h, :, g * NT:(g + 1) * NT, :], Vn[p2][:], [Vn[p2]], [vscB[g]], Vn[p2])

            def u_q():
                bk = nextbank()
                for k3 in range(3):
                    P.mm(bk[:], wuq[:, k3, h * 192:h * 192 + 128], cqn[:, k3, :], k3 == 0, k3 == 2,
                         [wuq, cqn], [bk])
                P.act(Qn[p2][:], bk[:], AF.Copy, [bk], [Qn[p2]], scale=QSCALE)

            def u_qa():
                bka = nextbank()
                for k3 in range(3):
                    P.mm(bka[0:64, :], wuq[:, k3, h * 192 + 128:h * 192 + 192], cqn[:, k3, :], k3 == 0, k3 == 2,
                         [wuq, cqn], [bka])
                ta = tmpD[0]
                P.tt("dve", ta[0:64, :], bka[0:64, :], ropet[:, 2, :], ALU.mult, [bka, ropet], [ta])

            def u_qb():
                bkb = nextbank()
                for k3 in range(3):
                    P.mm(bkb[0:64, :], wuqr[:, k3, h, :], cqn[:, k3, :], k3 == 0, k3 == 2, [wuqr, cqn], [bkb])
                ta = tmpD[0]
                P.stt(qpe[p2][0:64, :], bkb[0:64, :], 1.0, ropet[:, 3, :], ALU.mult, ALU.mult,
                      [bkb, ropet], [qpe[p2]])
                P.tt("dve", qpe[p2][0:64, :], qpe[p2][0:64, :], ta[0:64, :], ALU.add, [qpe[p2], ta], [qpe[p2]])

            return [u_k, u_v, u_q, u_qa, u_qb]

        for u in proj_units(0):
            u()
        yield
        for h in range(H):
            p2 = h % 2
            nxt = proj_units(h + 1) if h + 1 < H else []

            blocks = []
            npast = T * g
            ci = 0
            for c0 in range(0, npast, KCH):
                n = min(KCH, npast - c0)
                for kb in range(n // 128):
                    blocks.append(("past", ci, c0, n, kb))
                ci += 1
            for j in range(NT):
                blocks.append(("diag", j))
            nblk = len(blocks)
            state = {}

            def front(bi):
                d = blocks[bi]
                sb_ = nextbank()
                pt_ = Pt[ptidx[0] % NPT]
                ptidx[0] += 1
                if d[0] == "past":
                    _, ci_, c0, n, kb = d
                    kc_, vc_ = Kc[ci_ % 2], Vc[ci_ % 2]
                    if kb == 0:
                        gs = list(range(c0 // T, (c0 + n) // T))
                        P.dma("pool", kc_[:, 0:n], ksc[h, :, c0:c0 + n], [kscB[q] for q in gs], [kc_], kc_)
                        P.dma("pool", vc_[:, 0:n // 128, :], vsc[h, :, c0 // 128:(c0 + n) // 128, :],
                              [vscB[q] for q in gs], [vc_], vc_)
                    klhs, k_tl, kabs = kc_[:, kb * 128:(kb + 1) * 128], kc_, c0 + kb * 128
                    vlhs, v_tl, q0, dj = vc_[:, kb, :], vc_, 0, None
                else:
                    j = d[1]
                    klhs, k_tl, kabs = Kn[p2][:, j * 128:(j + 1) * 128], Kn[p2], t0 + j * 128
                    vlhs, v_tl, q0, dj = Vn[p2][:, j, :], Vn[p2], j * 128, j
                gk = kabs // T
                P.mm(sb_[:, q0:T], klhs, Qn[p2][:, q0:T], True, False, [k_tl, Qn[p2]], [sb_])
                P.mm(sb_[:, q0:T], kpe[:, kabs:kabs + 128], qpe[p2][:, q0:T], False, True,
                     [kpeB[gk], qpe[p2]], [sb_])
                P.act(pt_[:, q0:T], sb_[:, q0:T], AF.Exp, [sb_], [pt_])
                if dj is not None:
                    P.tt("dve", pt_[:, q0:q0 + 128], pt_[:, q0:q0 + 128], trib[:], ALU.mult, [pt_, trib], [pt_])
                state[bi] = (pt_, vlhs, v_tl, q0)

            pending = []
            held = [None]

            def flush_ones(upto=1 << 30):
                while pending and pending[0][2] <= upto:
                    ps_, grp, _ = pending.pop(0)
                    P.mm(LAc[:], onesb[:], ps_[:], grp == 0, grp == nblk // 4 - 1, [onesb, ps_], [LAc],
                         skip_group_check=True)

            def back(bi):
                pt_, vlhs, v_tl, q0 = state.pop(bi)
                first = bi == 0
                last = bi == nblk - 1
                flush_ones(bi)
                P.mm(OAc[:, q0:T], vlhs, pt_[:, q0:T], first, last, [v_tl, pt_], [OAc], skip_group_check=True)
                grp, pos = bi // 4, bi % 4
                ps_ = Ps[(psidx[0] + grp) % 2]
                eng_ = "dve"
                if pos == 0:
                    held[0] = (pt_, q0)
                elif pos == 1:
                    p0_, q00 = held[0]
                    P.tt(eng_, ps_[:, q0:T], p0_[:, q0:T], pt_[:, q0:T], ALU.add, [p0_, pt_], [ps_])
                    if q0 > q00:
                        P.cp(eng_, ps_[:, q00:q0], p0_[:, q00:q0], [p0_], [ps_])
                else:
                    P.tt(eng_, ps_[:, q0:T], ps_[:, q0:T], pt_[:, q0:T], ALU.add, [ps_, pt_], [ps_])
                if pos == 3:
                    pending.append((ps_, grp, bi + 2))

            for it in range(nblk + DEPTH):
                if it < nblk:
                    front(it)
                if it >= DEPTH:
                    back(it - DEPTH)
                if it >= 1 and nxt:
                    nxt.pop(0)()
                yield
            while nxt:
                nxt.pop(0)()
            flush_ones()
            psidx[0] += nblk // 4
            rl = tmpD[0]
            t1 = tmpD[1]
            P.cp("dve", t1[:], OAc[:], [OAc], [t1])
            P.act(rl[:], LAc[:], AF.Ln, [LAc], [rl])
            P.act(rl[:], rl[:], AF.Exp, [rl], [rl], scale=-1.0)
            P.tt("dve", t1[:], t1[:], rl[:], ALU.mult, [t1, rl], [t1])
            P.tt("dve", ybT[h][:], t1[:], ybT[h][:], ALU.mult, [t1, ybT[h]], [ybT[h]])
            yield

    def stage_E(g):
        t0 = g * T
        hT = hTs[g % 2]
        hT_k = lambda kc: hT[:, kc, :]
        xtiles = [(xs[0], xs[0][:]), (xs[1], xs[1][:])]
        for i in range(2):
            P.dma("pool", xs[i][:], x[t0 + i * 128:t0 + (i + 1) * 128, :], [x], [xs[i]], xs[i])
        for c in range(8):
            slc = stream(12 + c)
            bga, bgb, bpa, bpb = [B[(c % 2) * 4 + k_] for k_ in range(4)]
            proj_fm(bga, slc, 0, 128, [hT], hT_k)
            proj_fm(bgb, slc, 128, 128, [hT], hT_k)
            ga, gb_ = gettmp(), gettmp()
            P.act(ga[:], bga[:], AF.Sigmoid, [bga, vc], [ga], bias=vc[:, V_BG + c:V_BG + c + 1])
            P.act(gb_[:], bgb[:], AF.Sigmoid, [bgb, vc], [gb_], bias=vc[:, V_BG + 8 + c:V_BG + 8 + c + 1])
            for kc in range(8):
                P.mm(bpa[:], slc[:, kc, 256:384], yaT[kc // 4][:, kc % 4, :], kc == 0, kc == 7,
                     [slc, yaT[kc // 4]], [bpa])
            for kc in range(8):
                P.mm(bpb[:], slc[:, kc, 384:512], ybT[kc][:], kc == 0, kc == 7, [slc, ybT[kc]], [bpb])
            P.tt("dve", ga[:], ga[:], bpa[:], ALU.mult, [ga, bpa], [ga])
            P.tt("dve", gb_[:], gb_[:], bpb[:], ALU.mult, [gb_, bpb], [gb_])
            P.tt("dve", mT[:, c, :], ga[:], gb_[:], ALU.add, [ga, gb_], [mT])
        for i in range(2, 4):
            yv = yaT[i - 2][:].rearrange("p a b -> p (a b)").bitcast(F32)
            xtiles.append((yaT[i - 2], yv))
            P.dma("pool", yv, x[t0 + i * 128:t0 + (i + 1) * 128, :], [x], [yaT[i - 2]], yaT[i - 2])
        slo = [stream(20), stream(21)]
        for i in range(NT):
            xt, xv = xtiles[i]
            r0 = t0 + i * 128
            for hf_ in range(2):
                bo = B[4 + (i % 2) * 2 + hf_]
                for kc in range(8):
                    P.mm(bo[:], mT[:, kc, i * 128:(i + 1) * 128], slo[hf_][:, kc, :], kc == 0, kc == 7,
                         [mT, slo[hf_]], [bo])
                P.tt("dve", xv[:, hf_ * 512:(hf_ + 1) * 512], xv[:, hf_ * 512:(hf_ + 1) * 512], bo[:], ALU.add,
                     [xt, bo], [xt])
            P.act(hb[:], xv, AF.Square, [xt], [hb, st4], accum_out=st4[:, 2 + i:3 + i])
            rstd_from(st4, st4[:, 2 + i:3 + i], st4, st4[:, 2 + i:3 + i], 1.0 / D)
            P.stt(xv, xv, st4[:, 2 + i:3 + i], fgb[:], ALU.mult, ALU.mult, [xt, st4, fgb], [xt])
            ob = Buf(f"out{g}_{i}")
            P.dma("pool", out[r0:r0 + 128, :], xv, [xt], [ob], xt)
            out_tiles.append(ob)

    NB_UNITS = 2 * (2 * 7 + 4 + 4 + 1 + 4 * 5)
    for _ in stage_A(0):
        pass
    for g in range(NG):
        stage_C(g)
        gens = [stage_B(g)]
        if g + 1 < NG:
            gens.append(stage_A(g + 1))
        gd = stage_D(g)
        nb_left = NB_UNITS
        nd_left = H * (4 * g + 4 + 4)
        d_alive = True
        while gens or d_alive:
            for gen in list(gens):
                try:
                    next(gen)
                except StopIteration:
                    gens.remove(gen)
            nb_left -= 1
            if d_alive:
                k = (1 << 30) if not gens else max(1, -(-nd_left // max(nb_left, 1)))
                for _ in range(k):
                    try:
                        next(gd)
                        nd_left -= 1
                    except StopIteration:
                        d_alive = False
                        break
        if stop_after == "D":
            break
        stage_E(g)

    P.emit(out_tiles + list(dbg_outs.values()))
    return nc, P


def host_consts(S):
    cst = np.zeros((128, 896), np.float32)
    cst[:, 0:128] = np.eye(128, dtype=np.float32)
    s = np.arange(128)[:, None]
    t = np.arange(128)[None, :]
    cst[:, 128:256] = ((s // 32 == t // 32) & (s <= t)).astype(np.float32)
    cst[:, 256:384] = (t >= s).astype(np.float32)
    rm = np.ones((128, 512), np.float32)
    rm[:, ::32] = 0.0
    cst[:, 384:896] = rm
    inv = (np.float32(10000.0) ** (-np.arange(0, 64, 2, dtype=np.float32) / np.float32(64))).astype(np.float32)
    ang = (np.arange(S, dtype=np.float32)[:, None] * inv[None, :]).astype(np.float32)
    cos = np.cos(ang).astype(np.float32).T
    sin = np.sin(ang).astype(np.float32).T
    rope = np.zeros((2, 64, S), np.float32)
    rope[0, 0:32] = cos
    rope[0, 32:64] = cos
    rope[1, 0:32] = -sin
    rope[1, 32:64] = sin
    return cst, rope


def pc(v):
    v = np.asarray(v, np.float32).reshape(-1, 128)
    return np.ascontiguousarray(v.T)


def make_in_maps(inputs, S):
    cst, rope = host_consts(S)
    vecs = np.concatenate([
        pc(inputs["norm_g"][0]), pc(inputs["b_gate"][0]), pc(inputs["lb_logits"][0]), pc(inputs["lb_logits"][1]),
        pc(inputs["hg_norm_g"][0]), pc(inputs["q_a_g"][0]), pc(inputs["kv_a_g"][0])], axis=1)
    assert vecs.shape == (128, 46)
    fgb = np.ascontiguousarray(np.broadcast_to(np.asarray(inputs["final_norm_g"], np.float32)[None, :], (128, D)))
    common = {
        "w_in": np.ascontiguousarray(inputs["w_in"][0], dtype=np.float32),
        "w_uq": np.ascontiguousarray(inputs["w_uq"][0], dtype=np.float32),
        "w_ukv": np.ascontiguousarray(inputs["w_ukv"][0], dtype=np.float32),
        "w_pa": np.ascontiguousarray(inputs["w_proj_a"][0], dtype=np.float32),
        "w_pb": np.ascontiguousarray(inputs["w_proj_b"][0], dtype=np.float32),
        "w_out": np.ascontiguousarray(inputs["w_out"][0], dtype=np.float32),
        "vecs": vecs, "fgb": fgb, "cst": cst, "rope": rope,
    }
    xa = np.asarray(inputs["x"], np.float32)
    return [dict(common, x=np.ascontiguousarray(xa[b])) for b in range(xa.shape[0])]


_CACHE = {}


def kernel(**inputs):
    xa = np.asarray(inputs["x"])
    Bn, S, _ = xa.shape
    if S not in _CACHE:
        _CACHE[S] = build(S)[0]
    nc = _CACHE[S]
    in_maps = make_in_maps(inputs, S)
    res = run_bass_kernel_spmd(nc, in_maps, core_ids=list(range(Bn)))
    return np.stack([np.asarray(r["out"], np.float32) for r in res.results], axis=0)
```

```python
import numpy as np
from contextlib import ExitStack
import concourse.bass as bass
import concourse.mybir as mybir
from concourse.bass_utils import run_bass_kernel_spmd

F32 = mybir.dt.float32
BF16 = mybir.dt.bfloat16
ALU = mybir.AluOpType
AF = mybir.ActivationFunctionType

D = 1024
H = 8
T = 512
NT = 4
EPS = 1e-6
QSCALE = 192.0 ** -0.5
IN_COLS = 7872
O_HQ, O_HF, O_HI, O_HZ, O_CQ, O_CKV, O_KR, O_MZ, O_GL = 0, 1024, 2048, 3072, 4096, 4480, 4736, 4800, 5824

COMPUTE = ("pe", "act", "dve", "pool")
SEM_CHUNK = 12000


class Buf:
    __slots__ = ("name", "writers", "readers", "dsem", "dcount", "psum")

    def __init__(self, name):
        self.name = name
        self.writers = {}
        self.readers = []
        self.dsem = None
        self.dcount = 0
        self.psum = False


class Tl(Buf):
    __slots__ = ("t",)

    def __init__(self, name, t):
        Buf.__init__(self, name)
        self.t = t

    def __getitem__(self, k):
        return self.t[k]


class Op:
    __slots__ = ("eng", "fn", "deps", "flag", "tok", "is_dma", "dbuf", "dval")

    def __init__(self, eng, fn, is_dma, dbuf):
        self.eng = eng
        self.fn = fn
        self.deps = []
        self.flag = False
        self.tok = None
        self.is_dma = is_dma
        self.dbuf = dbuf
        self.dval = 0


class Prog:
    def __init__(self, nc):
        self.nc = nc
        self.q = {k: [] for k in ("pe", "act", "dve", "pool", "sp")}
        self.dma_bufs = []
        self.owners = {}

    def sb(self, name, shape, dtype):
        return Tl(name, self.nc.alloc_sbuf_tensor(name, list(shape), dtype))

    def ps(self, name, shape, dtype=F32):
        t = Tl(name, self.nc.alloc_psum_tensor(name, list(shape), dtype))
        t.psum = True
        return t

    def dram(self, name, shape, dtype, kind="Internal"):
        return Tl(name, self.nc.dram_tensor(name, list(shape), dtype, kind=kind))

    def op(self, eng, fn, reads=(), writes=(), dma_dst=None):
        is_dma = dma_dst is not None
        o = Op(eng, fn, is_dma, dma_dst)
        deps = {}

        def add(d, kind):
            if d is o:
                return
            if d.is_dma:
                if is_dma and kind == "waw" and d.dbuf is dma_dst:
                    return
                deps[id(d)] = d
                return
            if (not is_dma) and d.eng == eng:
                if eng == "pe" or kind != "raw":
                    return
            deps[id(d)] = d

        for b in reads:
            for w in b.writers.values():
                add(w, "raw")
            if b.psum:
                for r in b.readers:
                    if r.eng != eng:
                        add(r, "raw")
        for b in writes:
            for r in b.readers:
                add(r, "war")
            for w in b.writers.values():
                add(w, "waw")
        o.deps = list(deps.values())
        for d in o.deps:
            if not d.is_dma:
                d.flag = True
        for b in reads:
            if not is_dma:
                b.readers = [r for r in b.readers if r.is_dma or r.eng != eng]
            b.readers.append(o)
        for b in writes:
            b.readers = []
            b.writers = {(("dma", id(dma_dst)) if is_dma else eng): o}
        if is_dma:
            if dma_dst.dcount == 0:
                self.dma_bufs.append(dma_dst)
            dma_dst.dcount += 1
            o.dval = 16 * dma_dst.dcount
        self.q[eng].append(o)
        return o

    def mm(self, out, lhsT, rhs, start, stop, reads, writes, **kw):
        return self.op("pe", lambda e: e.matmul(out, lhsT, rhs, start=start, stop=stop, **kw), reads, writes)

    def tr(self, out, in_, ident, reads, writes):
        return self.op("pe", lambda e: e.transpose(out, in_, ident), reads, writes)

    def act(self, out, in_, func, reads, writes, **kw):
        return self.op("act", lambda e: e.activation(out, in_, func, **kw), reads, writes)

    def dma(self, eng, out, in_, reads, writes, dst):
        key = (id(dst), eng)
        if key not in self.owners:
            self.owners[key] = Buf(dst.name + "@" + eng)
        return self.op(eng, lambda e: e.dma_start(out, in_), reads, writes, dma_dst=self.owners[key])

    def tt(self, eng, out, in0, in1, op, reads, writes):
        return self.op(eng, lambda e: e.tensor_tensor(out, in0, in1, op), reads, writes)

    def ts(self, eng, out, in0, s1, s2, op0, op1, reads, writes):
        return self.op(eng, lambda e: e.tensor_scalar(out, in0, s1, s2, op0, op1), reads, writes)

    def stt(self, out, in0, scalar, in1, op0, op1, reads, writes):
        return self.op("dve", lambda e: e.scalar_tensor_tensor(out, in0, scalar, in1, op0, op1), reads, writes)

    def cp(self, eng, out, in_, reads, writes):
        if eng == "act":
            return self.op("act", lambda e: e.activation(out, in_, AF.Copy), reads, writes)
        return self.op(eng, lambda e: e.tensor_copy(out, in_), reads, writes)

    def emit(self, final_reads=()):
        nc = self.nc
        self.op("sp", None, reads=final_reads)
        with ExitStack() as es:
            esems = {}
            for k in COMPUTE:
                n = sum(1 for o in self.q[k] if o.flag)
                ns = max(1, (n + SEM_CHUNK - 1) // SEM_CHUNK)
                esems[k] = [es.enter_context(nc.semaphore(f"s_{k}{i}")) for i in range(ns)]
                c = 0
                for o in self.q[k]:
                    if o.flag:
                        o.tok = (esems[k][c // SEM_CHUNK], (c % SEM_CHUNK) + 1)
                        c += 1
            for i, b in enumerate(self.dma_bufs):
                b.dsem = es.enter_context(nc.semaphore(f"d{i}"))
            for k in self.q:
                for o in self.q[k]:
                    if o.is_dma:
                        o.tok = (o.dbuf.dsem, o.dval)
            self.nsem = sum(len(v) for v in esems.values()) + len(self.dma_bufs)
            block = es.enter_context(nc.Block())

            def run(e, k):
                waited = {}
                for o in self.q[k]:
                    need = {}
                    for d in o.deps:
                        s, v = d.tok
                        sid = id(s)
                        if waited.get(sid, 0) >= v:
                            continue
                        if sid not in need or need[sid][1] < v:
                            need[sid] = (s, v)
                    for sid, (s, v) in need.items():
                        e.wait_ge(s, v)
                        waited[sid] = v
                    if o.fn is None:
                        continue
                    ins = o.fn(e)
                    if o.is_dma:
                        ins.then_inc(o.tok[0], 16)
                    elif o.flag:
                        ins.then_inc(o.tok[0], 1)

            @block.tensor
            def _(e):
                run(e, "pe")

            @block.scalar
            def _(e):
                run(e, "act")

            @block.vector
            def _(e):
                run(e, "dve")

            @block.gpsimd
            def _(e):
                run(e, "pool")

            @block.sync
            def _(e):
                run(e, "sp")


def build(S, dbg=None, stop_after=None):
    NG = S // T
    nc = bass.Bass("TRN2", target_bir_lowering=False)
    P = Prog(nc)
    dbg_outs = {}

    x = P.dram("x", [S, D], F32, kind="ExternalInput")
    w_in = P.dram("w_in", [D, IN_COLS], F32, kind="ExternalInput")
    w_uq = P.dram("w_uq", [384, 1536], F32, kind="ExternalInput")
    w_ukv = P.dram("w_ukv", [256, 2048], F32, kind="ExternalInput")
    w_pa = P.dram("w_pa", [D, D], F32, kind="ExternalInput")
    w_pb = P.dram("w_pb", [D, D], F32, kind="ExternalInput")
    w_out = P.dram("w_out", [D, D], F32, kind="ExternalInput")
    vecs = P.dram("vecs", [128, 46], F32, kind="ExternalInput")
    fgb_d = P.dram("fgb", [128, D], F32, kind="ExternalInput")
    cst_d = P.dram("cst", [128, 896], F32, kind="ExternalInput")
    rope_d = P.dram("rope", [2, 64, S], F32, kind="ExternalInput")
    out = P.dram("out", [S, D], F32, kind="ExternalOutput")

    NCH = 22
    wsc = nc.dram_tensor("wsc", [NCH, 128, 8, 512], BF16)
    wscB = [Buf(f"wsc{c}") for c in range(NCH)]
    ksc = nc.dram_tensor("ksc", [H, 128, S], BF16)
    vsc = nc.dram_tensor("vsc", [H, 128, S // 128, 128], BF16)
    kscB = [Buf(f"ksc{g}") for g in range(NG)]
    vscB = [Buf(f"vsc{g}") for g in range(NG)]

    cstf = P.sb("cstf", [128, 512], F32)
    identb = P.sb("identb", [128, 128], BF16)
    maskbd = P.sb("maskbd", [128, 128], BF16)
    trib = P.sb("trib", [128, 128], BF16)
    onesb = P.sb("onesb", [128, 128], BF16)
    vc = P.sb("vc", [128, 46], F32)
    lbv = P.sb("lbv", [128, 24], F32)
    fgb = P.sb("fgbs", [128, D], F32)
    wuq = P.sb("wuq", [128, 3, 1536], BF16)
    wuqr = P.sb("wuqr", [128, 3, 8, 64], BF16)
    wukv = P.sb("wukv", [128, 2, 2048], BF16)
    NSLOT = 3
    slots = [P.sb(f"slot{i}", [128, 8, 512], BF16) for i in range(NSLOT)]
    xs = [P.sb(f"xs{i}", [128, D], F32) for i in range(2)]
    hb = P.sb("hb", [128, D], BF16)
    st4 = P.sb("st4", [128, 8], F32)
    hTs = [P.sb(f"hT{i}", [128, 8, T], BF16) for i in range(2)]
    NTMP = 9
    tmp = [P.sb(f"tmp{i}", [128, T], F32) for i in range(NTMP)]
    qin = [P.sb(f"qin{i}", [128, T], BF16) for i in range(4)]
    kin = [P.sb(f"kin{i}", [128, T], BF16) for i in range(4)]
    koT = [P.sb(f"koT{i}", [128, T], BF16) for i in range(2)]
    ko = [P.sb(f"ko{i}", [128, NT, 128], BF16) for i in range(4)]
    szh = P.sb("szh", [128, 4, T], BF16)
    vT = P.sb("vT", [128, NT, 512], BF16)
    dec = [P.sb(f"dec{i}", [128, 16], F32) for i in range(4)]
    Sst = [P.sb(f"Sst{i}", [128, 4, 128], F32) for i in range(2)]
    Sbf = [P.sb(f"Sbf{i}", [128, 4, 128], BF16) for i in range(2)]
    scm = P.sb("scm", [128, 4, 128], BF16)
    sqo = P.sb("sqo", [128, T], BF16)
    yaT = [P.sb(f"yaT{i}", [128, 4, T], BF16) for i in range(2)]
    ybT = [P.sb(f"ybT{h}", [128, T], BF16) for h in range(H)]
    mT = P.sb("mT", [128, 8, T], BF16)
    cqn = P.sb("cqn", [128, 3, T], BF16)
    ckvn = P.sb("ckvn", [128, 2, T], BF16)
    kpe = P.sb("kpe", [128, S], BF16)
    kpeB = [Buf(f"kpe{g}") for g in range(NG)]
    ropet = P.sb("ropet", [64, 4, T], F32)
    Kn = [P.sb(f"Kn{i}", [128, T], BF16) for i in range(2)]
    Vn = [P.sb(f"Vn{i}", [128, NT, 128], BF16) for i in range(2)]
    Qn = [P.sb(f"Qn{i}", [128, T], BF16) for i in range(2)]
    qpe = [P.sb(f"qpe{i}", [128, T], BF16) for i in range(2)]
    tmpD = [P.sb(f"tmpD{i}", [128, T], F32) for i in range(2)]
    KCH = 1024
    Kc = [P.sb(f"Kc{i}", [128, KCH], BF16) for i in range(2)]
    Vc = [P.sb(f"Vc{i}", [128, KCH // 128, 128], BF16) for i in range(2)]
    NPT = 6
    Pt = [P.sb(f"Pt{i}", [128, T], BF16) for i in range(NPT)]
    Ps = [P.sb(f"Ps{i}", [128, T], BF16) for i in range(2)]

    B = [P.ps(f"bank{i}", [128, 512], F32) for i in range(8)]

    tmp_i = [0]

    def gettmp():
        t = tmp[tmp_i[0] % NTMP]
        tmp_i[0] += 1
        return t

    def tap(name, tl, ap, shape, dtype=F32):
        if dbg is None or name not in dbg:
            return
        d = P.dram("dbg_" + name, list(shape), dtype, kind="ExternalOutput")
        P.dma("sp", d[:], ap, [tl], [d], tl)
        dbg_outs[name] = d

    P.dma("sp", cstf[:], cst_d[:, 384:896], [cst_d], [cstf], cstf)
    P.dma("sp", tmp[0][:, 0:384], cst_d[:, 0:384], [cst_d], [tmp[0]], tmp[0])
    P.dma("sp", vc[:], vecs[:], [vecs], [vc], vc)
    P.dma("sp", fgb[:], fgb_d[:], [fgb_d], [fgb], fgb)
    P.cp("dve", identb[:], tmp[0][:, 0:128], [tmp[0]], [identb])
    P.cp("dve", maskbd[:], tmp[0][:, 128:256], [tmp[0]], [maskbd])
    P.cp("dve", trib[:], tmp[0][:, 256:384], [tmp[0]], [trib])
    resetm = cstf
    P.op("dve", lambda e: e.memset(onesb[:], 1.0), [], [onesb])
    P.op("pool", lambda e: e.memset(kpe[64:128, :], 0.0), [], [kpeB[g_] for g_ in range(NG)])
    for i_ in range(2):
        P.op("pool", lambda e, i_=i_: e.memset(qpe[i_][64:128, :], 0.0), [], [qpe[i_]])
    for h in range(2):
        P.op("pool", lambda e, h=h: e.memset(Sst[h][:], 0.0), [], [Sst[h]])
        P.op("pool", lambda e, h=h: e.memset(Sbf[h][:], 0.0), [], [Sbf[h]])
    V_NG, V_BG, V_L0, V_L1, V_HGG, V_QAG, V_KVAG = 0, 8, 24, 32, 40, 41, 44
    P.tt("dve", lbv[:, 16:24], vc[:, V_L0:V_L0 + 8], vc[:, V_L1:V_L1 + 8], ALU.subtract, [vc], [lbv])
    P.act(lbv[:, 0:8], lbv[:, 16:24], AF.Sigmoid, [lbv], [lbv])
    P.act(lbv[:, 8:16], lbv[:, 16:24], AF.Sigmoid, [lbv], [lbv], scale=-1.0)
    P.ts("dve", lbv[:, 16:24], lbv[:, 8:16], -1.0, None, ALU.mult, ALU.bypass, [lbv], [lbv])

    def wsrc(wt, c0, n):
        return wt.t.ap().rearrange("(kc p) c -> p kc c", p=128)[:, :, c0:c0 + n]

    def conv(ci, col, wt, c0, n):
        P.dma("pool", wsc[ci, :, :, col:col + n], wsrc(wt, c0, n), [wt], [wscB[ci]], wscB[ci])

    conv(8, 0, w_in, O_CQ, 384)
    conv(8, 384, w_in, O_KR, 64)
    conv(8, 448, w_in, O_KR + 32, 32)
    conv(8, 480, w_in, O_KR, 32)
    conv(9, 0, w_in, O_CKV, 256)
    conv(10, 0, w_in, O_MZ, 512)
    conv(11, 0, w_in, O_MZ + 512, 512)
    P.dma("pool", wuq[:], w_uq.t.ap().rearrange("(kc p) c -> p kc c", p=128), [w_uq], [wuq], wuq)
    for hf_ in range(2):
        P.dma("pool", wukv[:, :, hf_ * 1024:(hf_ + 1) * 1024],
              w_ukv.t.ap().rearrange("(kc p) c -> p kc c", p=128)[:, :, hf_ * 1024:(hf_ + 1) * 1024],
              [w_ukv], [wukv], wukv)
    for half in range(2):
        for j, o in enumerate((O_HQ, O_HF, O_HI, O_HZ)):
            conv(half * 4 + j, 0, w_in, o + half * 512, 512)
    for c in range(8):
        conv(12 + c, 0, w_in, O_GL + c * 128, 128)
        conv(12 + c, 128, w_in, O_GL + 1024 + c * 128, 128)
        conv(12 + c, 256, w_pa, c * 128, 128)
        conv(12 + c, 384, w_pb, c * 128, 128)
    conv(20, 0, w_out, 0, 512)
    conv(21, 0, w_out, 512, 512)
    CH_NCOL = [512] * 8 + [512, 256, 512, 512] + [512] * 8 + [512, 512]
    wuq4 = wuq[:, :, :].rearrange("p k (h c) -> p k h c", c=192)
    P.cp("dve", wuqr[:, :, :, 0:32], wuq4[:, :, :, 160:192], [wuq], [wuqr])
    P.cp("dve", wuqr[:, :, :, 32:64], wuq4[:, :, :, 128:160], [wuq], [wuqr])

    sstate = {"n": 0}

    def stream(ci):
        sl = slots[sstate["n"] % NSLOT]
        sstate["n"] += 1
        n = CH_NCOL[ci]
        P.dma("sp", sl[:, :, 0:n], wsc[ci, :, :, 0:n], [wscB[ci]], [sl], sl)
        return sl

    bank_rr = [0]

    def pbank(cands):
        b = cands[bank_rr[0] % len(cands)]
        bank_rr[0] += 1
        return B[b]

    def proj_fm(bank, sl, col, m, rhs_tl, rhs_of_kc, nk=8, rows=128):
        for kc in range(nk):
            P.mm(bank[0:m, 0:T], sl[0:rows, kc, col:col + m], rhs_of_kc(kc), kc == 0, kc == nk - 1,
                 [sl] + rhs_tl, [bank])

    def rstd_from(bank_or_tl, src_ap, dst_tl, dst_ap, scale):
        P.act(dst_ap, src_ap, AF.Ln, [bank_or_tl], [dst_tl], scale=scale, bias=EPS)
        P.act(dst_ap, dst_ap, AF.Exp, [dst_tl], [dst_tl], scale=-0.5)

    out_tiles = []

    def stage_A(g):
        t0 = g * T
        hT = hTs[g % 2]

        def load(i):
            xt = xs[i % 2]
            P.dma("sp", xt[:], x[t0 + i * 128:t0 + (i + 1) * 128, :], [x], [xt], xt)

        load(0)
        load(1)
        yield
        for i in range(NT):
            xt = xs[i % 2]
            P.act(hb[:], xt[:], AF.Square, [xt], [hb, st4], accum_out=st4[:, 0:1])
            yield
            rstd_from(st4, st4[:, 0:1], st4, st4[:, 1:2], 1.0 / D)
            P.act(hb[:], xt[:], AF.Copy, [xt, st4], [hb], scale=st4[:, 1:2])
            if i + 2 < NT:
                load(i + 2)
            yield
            ptb = B[2][:].bitcast(BF16)
            for kc in range(8):
                P.tr(ptb[:, kc * 128:(kc + 1) * 128], hb[:, kc * 128:(kc + 1) * 128], identb[:],
                     [hb, identb], [B[2]])
            P.tt("dve", hT[:, :, i * 128:(i + 1) * 128], ptb.rearrange("p (k t) -> p k t", t=128),
                 vc[:, V_NG:V_NG + 8].unsqueeze(2).to_broadcast([128, 8, 128]), ALU.mult, [B[2], vc], [hT])
            yield

    def stage_B(g):
        hT = hTs[g % 2]
        hT_k = lambda kc: hT[:, kc, :]
        for half in range(2):
            sl_q = stream(half * 4 + 0)
            sl_f = stream(half * 4 + 1)
            for pair in range(2):
                hhs = [pair * 2, pair * 2 + 1]
                sg, lf, eb, en, sq = {}, {}, {}, {}, {}
                bks = {}
                for hh in hhs:
                    bks[hh] = pbank([0, 1])
                    proj_fm(bks[hh], sl_f, hh * 128, 128, [hT], hT_k)
                for hh in hhs:
                    sg[hh] = gettmp()
                    P.act(sg[hh][:], bks[hh][:], AF.Sigmoid, [bks[hh]], [sg[hh]])
                yield
                for hh in hhs:
                    h = half * 4 + hh
                    lf[hh] = gettmp()
                    P.act(lf[hh][:], sg[hh][:], AF.Ln, [sg[hh], lbv], [lf[hh]],
                          scale=lbv[:, 8 + h:9 + h], bias=lbv[:, h:h + 1])
                    P.op("dve", lambda e, a=lf[hh]: e.tensor_tensor_scan(a[:], resetm[:, 0:512], a[:], 0.0,
                                                                          ALU.mult, ALU.add),
                         [cstf, lf[hh]], [lf[hh]])
                    P.ts("dve", sg[hh][:], sg[hh][:], lbv[:, 16 + h:17 + h], lbv[:, 8 + h:9 + h], ALU.mult, ALU.add,
                         [sg[hh], lbv], [sg[hh]])
                yield
                for hh in hhs:
                    eb[hh] = gettmp()
                    en[hh] = gettmp()
                    P.act(eb[hh][:], lf[hh][:], AF.Exp, [lf[hh]], [eb[hh]])
                    P.act(en[hh][:], lf[hh][:], AF.Exp, [lf[hh]], [en[hh]], scale=-1.0)
                    b3 = lf[hh][:, :].rearrange("p (c t) -> p c t", t=32)
                    P.tt("dve", b3, b3[:, :, 31:32].to_broadcast([128, 16, 32]), b3, ALU.subtract,
                         [lf[hh]], [lf[hh]])
                    P.act(lf[hh][:], lf[hh][:], AF.Exp, [lf[hh]], [lf[hh]])
                    P.cp("dve", dec[hh][:], eb[hh][:, :].rearrange("p (c t) -> p c t", t=32)[:, :, 31],
                         [eb[hh]], [dec[hh]])
                yield
                for hh in hhs:
                    P.tt("dve", kin[hh][:], sg[hh][:], en[hh][:], ALU.mult, [sg[hh], en[hh]], [kin[hh]])
                    kt = koT[hh % 2]
                    P.tt("dve", kt[:], sg[hh][:], lf[hh][:], ALU.mult, [sg[hh], lf[hh]], [kt])
                yield
                for hh in hhs:
                    bks[hh] = pbank([0, 1])
                    proj_fm(bks[hh], sl_q, hh * 128, 128, [hT], hT_k)
                for hh in hhs:
                    sq[hh] = gettmp()
                    P.act(sq[hh][:], bks[hh][:], AF.Silu, [bks[hh]], [sq[hh]])
                yield
                for hh in hhs:
                    P.tt("dve", qin[hh][:], sq[hh][:], eb[hh][:], ALU.mult, [sq[hh], eb[hh]], [qin[hh]])
                    kt = koT[hh % 2]
                    trb = B[2][:].bitcast(BF16)
                    for i in range(NT):
                        P.tr(trb[:, i * 128:(i + 1) * 128], kt[:, i * 128:(i + 1) * 128], identb[:],
                             [kt, identb], [B[2]])
                    P.cp("dve", ko[hh][:].rearrange("p a b -> p (a b)"), trb[:, 0:512], [B[2]], [ko[hh]])
                    yield
            sl_i = stream(half * 4 + 2)
            for i in range(NT):
                bk = pbank([0, 1])
                for kc in range(8):
                    P.mm(bk[:], hT[:, kc, i * 128:(i + 1) * 128], sl_i[:, kc, :], kc == 0, kc == 7,
                         [hT, sl_i], [bk])
                P.cp("dve", vT[:, i, :], bk[:], [bk], [vT])
                yield
            sl_z = stream(half * 4 + 3)
            for pz in range(2):
                for hh in (2 * pz, 2 * pz + 1):
                    bks[hh] = pbank([0, 1])
                    proj_fm(bks[hh], sl_z, hh * 128, 128, [hT], hT_k)
                for hh in (2 * pz, 2 * pz + 1):
                    P.act(szh[:, hh, :], bks[hh][:], AF.Silu, [bks[hh]], [szh])
                yield
            SC, OA, DS, SSB = B[0], B[1], B[2], B[0]
            def sc_step(i):
                for hh in range(4):
                    P.mm(SC[:, hh * 128:(hh + 1) * 128], kin[hh][:, i * 128:(i + 1) * 128],
                         qin[hh][:, i * 128:(i + 1) * 128], True, True, [kin[hh], qin[hh]], [SC])
                P.tt("dve", scm[:], SC[:, :].rearrange("p (h t) -> p h t", t=128),
                     maskbd[:, :].unsqueeze(1).to_broadcast([128, 4, 128]), ALU.mult, [SC, maskbd], [scm])

            sc_step(0)
            yield
            for i in range(NT):
                for hh in range(4):
                    P.mm(OA[:, hh * 128:(hh + 1) * 128], vT[:, i, hh * 128:(hh + 1) * 128], scm[:, hh, :],
                         hh == 0, False, [vT, scm], [OA], skip_group_check=True)
                for j in range(4):
                    for hh in range(4):
                        h = half * 4 + hh
                        c0 = i * 128 + j * 32
                        P.mm(OA[:, hh * 128 + j * 32:hh * 128 + (j + 1) * 32], Sbf[half][:, hh, :],
                             qin[hh][:, c0:c0 + 32], False, (j == 3 and hh == 3), [Sbf[half], qin[hh]], [OA],
                             skip_group_check=True)
                    for hh in range(4):
                        P.mm(DS[:, hh * 128:(hh + 1) * 128], ko[hh][32 * j:32 * (j + 1), i, :],
                             vT[32 * j:32 * (j + 1), i, hh * 128:(hh + 1) * 128], True, True,
                             [ko[hh], vT], [DS], tile_position=(32 * j, 0), skip_group_check=True)
                    for hh in range(4):
                        h = half * 4 + hh
                        cidx = i * 4 + j
                        P.stt(Sst[half][:, hh, :], Sst[half][:, hh, :], dec[hh][:, cidx:cidx + 1],
                              DS[:, hh * 128:(hh + 1) * 128], ALU.mult, ALU.add, [Sst[half], dec[hh], DS], [Sst[half]])
                    P.cp("dve", Sbf[half][:], Sst[half][:], [Sst[half]], [Sbf[half]])
                    yield
                if i + 1 < NT:
                    sc_step(i + 1)
                P.act(sqo[:], OA[:], AF.Square, [OA], [sqo])
                P.mm(SSB[:], onesb[:], sqo[:], True, True, [onesb, sqo], [SSB])
                rs = gettmp()
                rstd_from(SSB, SSB[:], rs, rs[:], 1.0 / 128)
                t1 = gettmp()
                P.stt(t1[:], OA[:], vc[:, V_HGG:V_HGG + 1], rs[:], ALU.mult, ALU.mult, [OA, vc, rs], [t1])
                P.tt("dve", yaT[half][:, :, i * 128:(i + 1) * 128], t1[:, :].rearrange("p (h t) -> p h t", t=128),
                     szh[:, :, i * 128:(i + 1) * 128], ALU.mult, [t1, szh], [yaT[half]])
                yield

    def stage_C(g):
        t0 = g * T
        hT = hTs[g % 2]
        hT_k = lambda kc: hT[:, kc, :]
        P.dma("sp", ropet[:, 0, :], rope_d[0, :, t0:t0 + T], [rope_d], [ropet], ropet)
        P.dma("sp", ropet[:, 1, :], rope_d[1, :, t0:t0 + T], [rope_d], [ropet], ropet)
        P.ts("dve", ropet[:, 2:4, :], ropet[:, 0:2, :], QSCALE, None, ALU.mult, ALU.bypass, [ropet], [ropet])
        sl8 = stream(8)
        SSB = B[2]
        cqf = []
        for k3 in range(3):
            bk = pbank([0, 1, 3, 4, 5, 6, 7])
            proj_fm(bk, sl8, k3 * 128, 128, [hT], hT_k)
            cf = gettmp()
            cqf.append(cf)
            P.cp("dve", cf[:], bk[:], [bk], [cf])
            P.act(sqo[:], bk[:], AF.Square, [bk], [sqo])
            P.mm(SSB[:], onesb[:], sqo[:], k3 == 0, k3 == 2, [onesb, sqo], [SSB])
        rs = gettmp()
        rstd_from(SSB, SSB[:], rs, rs[:], 1.0 / 384)
        for k3 in range(3):
            P.stt(cqn[:, k3, :], cqf[k3][:], vc[:, V_QAG + k3:V_QAG + k3 + 1], rs[:], ALU.mult, ALU.mult,
                  [cqf[k3], vc, rs], [cqn])
        bka, bkb = pbank([0, 1, 3, 4, 5, 6, 7]), pbank([0, 1, 3, 4, 5, 6, 7])
        proj_fm(bka, sl8, 384, 64, [hT], hT_k)
        proj_fm(bkb, sl8, 448, 64, [hT], hT_k)
        ta, tb = gettmp(), gettmp()
        P.tt("dve", ta[0:64, :], bka[0:64, :], ropet[:, 0, :], ALU.mult, [bka, ropet], [ta])
        P.tt("dve", tb[0:64, :], bkb[0:64, :], ropet[:, 1, :], ALU.mult, [bkb, ropet], [tb])
        P.tt("dve", kpe[0:64, t0:t0 + T], ta[0:64, :], tb[0:64, :], ALU.add, [ta, tb], [kpeB[g]])
        sl9 = stream(9)
        ckf = []
        for k2 in range(2):
            bk = pbank([0, 1, 3, 4, 5, 6, 7])
            proj_fm(bk, sl9, k2 * 128, 128, [hT], hT_k)
            cf = gettmp()
            ckf.append(cf)
            P.cp("dve", cf[:], bk[:], [bk], [cf])
            P.act(sqo[:], bk[:], AF.Square, [bk], [sqo])
            P.mm(SSB[:], onesb[:], sqo[:], k2 == 0, k2 == 1, [onesb, sqo], [SSB])
        rs = gettmp()
        rstd_from(SSB, SSB[:], rs, rs[:], 1.0 / 256)
        for k2 in range(2):
            P.stt(ckvn[:, k2, :], ckf[k2][:], vc[:, V_KVAG + k2:V_KVAG + k2 + 1], rs[:], ALU.mult, ALU.mult,
                  [ckf[k2], vc, rs], [ckvn])
        for half in range(2):
            slm = stream(10 + half)
            for hh in range(4):
                bk = pbank([0, 1, 3, 4, 5, 6, 7])
                proj_fm(bk, slm, hh * 128, 128, [hT], hT_k)
                P.act(ybT[half * 4 + hh][:], bk[:], AF.Silu, [bk], [ybT[half * 4 + hh]])

    sidx = [0]
    ptidx = [0]
    psidx = [0]

    def stage_D(g):
        t0 = g * T
        SB3 = [B[3], B[4], B[5]]
        OAc, LAc = B[6], B[7]
        DEPTH = 3

        def nextbank():
            b_ = SB3[sidx[0] % 3]
            sidx[0] += 1
            return b_

        def proj_units(h):
            p2 = h % 2

            def u_k():
                bk = nextbank()
                for k2 in range(2):
                    P.mm(bk[:], wukv[:, k2, h * 256:h * 256 + 128], ckvn[:, k2, :], k2 == 0, k2 == 1,
                         [wukv, ckvn], [bk])
                P.cp("dve", Kn[p2][:], bk[:], [bk], [Kn[p2]])
                P.dma("pool", ksc[h, :, t0:t0 + T], Kn[p2][:], [Kn[p2]], [kscB[g]], Kn[p2])

            def u_v():
                bk = nextbank()
                for i in range(NT):
                    for k2 in range(2):
                        P.mm(bk[:, i * 128:(i + 1) * 128], ckvn[:, k2, i * 128:(i + 1) * 128],
                             wukv[:, k2, h * 256 + 128:h * 256 + 256], (i == 0 and k2 == 0),
                             (i == NT - 1 and k2 == 1), [ckvn, wukv], [bk], skip_group_check=True)
                P.cp("dve", Vn[p2][:].rearrange("p a b -> p (a b)"), bk[:], [bk], [Vn[p2]])
                P.dma("pool", vsc[h, :, g * NT:(g + 1) * NT, :], Vn[p2][:], [Vn[p2]], [vscB[g]], Vn[p2])

            def u_q():
                bk = nextbank()
                for k3 in range(3):
                    P.mm(bk[:], wuq[:, k3, h * 192:h * 192 + 128], cqn[:, k3, :], k3 == 0, k3 == 2,
                         [wuq, cqn], [bk])
                P.act(Qn[p2][:], bk[:], AF.Copy, [bk], [Qn[p2]], scale=QSCALE)

            def u_qa():
                bka = nextbank()
                for k3 in range(3):
                    P.mm(bka[0:64, :], wuq[:, k3, h * 192 + 128:h * 192 + 192], cqn[:, k3, :], k3 == 0, k3 == 2,
                         [wuq, cqn], [bka])
                ta = tmpD[0]
                P.tt("dve", ta[0:64, :], bka[0:64, :], ropet[:, 2, :], ALU.mult, [bka, ropet], [ta])

            def u_qb():
                bkb = nextbank()
                for k3 in range(3):
                    P.mm(bkb[0:64, :], wuqr[:, k3, h, :], cqn[:, k3, :], k3 == 0, k3 == 2, [wuqr, cqn], [bkb])
                ta = tmpD[0]
                P.stt(qpe[p2][0:64, :], bkb[0:64, :], 1.0, ropet[:, 3, :], ALU.mult, ALU.mult,
                      [bkb, ropet], [qpe[p2]])
                P.tt("dve", qpe[p2][0:64, :], qpe[p2][0:64, :], ta[0:64, :], ALU.add, [qpe[p2], ta], [qpe[p2]])

            return [u_k, u_v, u_q, u_qa, u_qb]

        for u in proj_units(0):
            u()
        yield
        for h in range(H):
            p2 = h % 2
            nxt = proj_units(h + 1) if h + 1 < H else []

            blocks = []
            npast = T * g
            ci = 0
            for c0 in range(0, npast, KCH):
                n = min(KCH, npast - c0)
                for kb in range(n // 128):
                    blocks.append(("past", ci, c0, n, kb))
                ci += 1
            for j in range(NT):
                blocks.append(("diag", j))
            nblk = len(blocks)
            state = {}

            def front(bi):
                d = blocks[bi]
                sb_ = nextbank()
                pt_ = Pt[ptidx[0] % NPT]
                ptidx[0] += 1
                if d[0] == "past":
                    _, ci_, c0, n, kb = d
                    kc_, vc_ = Kc[ci_ % 2], Vc[ci_ % 2]
                    if kb == 0:
                        gs = list(range(c0 // T, (c0 + n) // T))
                        P.dma("pool", kc_[:, 0:n], ksc[h, :, c0:c0 + n], [kscB[q] for q in gs], [kc_], kc_)
                        P.dma("pool", vc_[:, 0:n // 128, :], vsc[h, :, c0 // 128:(c0 + n) // 128, :],
                              [vscB[q] for q in gs], [vc_], vc_)
                    klhs, k_tl, kabs = kc_[:, kb * 128:(kb + 1) * 128], kc_, c0 + kb * 128
                    vlhs, v_tl, q0, dj = vc_[:, kb, :], vc_, 0, None
                else:
                    j = d[1]
                    klhs, k_tl, kabs = Kn[p2][:, j * 128:(j + 1) * 128], Kn[p2], t0 + j * 128
                    vlhs, v_tl, q0, dj = Vn[p2][:, j, :], Vn[p2], j * 128, j
                gk = kabs // T
                P.mm(sb_[:, q0:T], klhs, Qn[p2][:, q0:T], True, False, [k_tl, Qn[p2]], [sb_])
                P.mm(sb_[:, q0:T], kpe[:, kabs:kabs + 128], qpe[p2][:, q0:T], False, True,
                     [kpeB[gk], qpe[p2]], [sb_])
                P.act(pt_[:, q0:T], sb_[:, q0:T], AF.Exp, [sb_], [pt_])
                if dj is not None:
                    P.tt("dve", pt_[:, q0:q0 + 128], pt_[:, q0:q0 + 128], trib[:], ALU.mult, [pt_, trib], [pt_])
                state[bi] = (pt_, vlhs, v_tl, q0)

            pending = []
            held = [None]

            def flush_ones(upto=1 << 30):
                while pending and pending[0][2] <= upto:
                    ps_, grp, _ = pending.pop(0)
                    P.mm(LAc[:], onesb[:], ps_[:], grp == 0, grp == nblk // 4 - 1, [onesb, ps_], [LAc],
                         skip_group_check=True)

            def back(bi):
                pt_, vlhs, v_tl, q0 = state.pop(bi)
                first = bi == 0
                last = bi == nblk - 1
                flush_ones(bi)
                P.mm(OAc[:, q0:T], vlhs, pt_[:, q0:T], first, last, [v_tl, pt_], [OAc], skip_group_check=True)
                grp, pos = bi // 4, bi % 4
                ps_ = Ps[(psidx[0] + grp) % 2]
                eng_ = "dve"
                if pos == 0:
                    held[0] = (pt_, q0)
                elif pos == 1:
                    p0_, q00 = held[0]
                    P.tt(eng_, ps_[:, q0:T], p0_[:, q0:T], pt_[:, q0:T], ALU.add, [p0_, pt_], [ps_])
                    if q0 > q00:
                        P.cp(eng_, ps_[:, q00:q0], p0_[:, q00:q0], [p0_], [ps_])
                else:
                    P.tt(eng_, ps_[:, q0:T], ps_[:, q0:T], pt_[:, q0:T], ALU.add, [ps_, pt_], [ps_])
                if pos == 3:
                    pending.append((ps_, grp, bi + 2))

            for it in range(nblk + DEPTH):
                if it < nblk:
                    front(it)
                if it >= DEPTH:
                    back(it - DEPTH)
                if it >= 1 and nxt:
                    nxt.pop(0)()
                yield
            while nxt:
                nxt.pop(0)()
            flush_ones()
            psidx[0] += nblk // 4
            rl = tmpD[0]
            t1 = tmpD[1]
            P.cp("dve", t1[:], OAc[:], [OAc], [t1])
            P.act(rl[:], LAc[:], AF.Ln, [LAc], [rl])
            P.act(rl[:], rl[:], AF.Exp, [rl], [rl], scale=-1.0)
            P.tt("dve", t1[:], t1[:], rl[:], ALU.mult, [t1, rl], [t1])
            P.tt("dve", ybT[h][:], t1[:], ybT[h][:], ALU.mult, [t1, ybT[h]], [ybT[h]])
            yield

    def stage_E(g):
        t0 = g * T
        hT = hTs[g % 2]
        hT_k = lambda kc: hT[:, kc, :]
        xtiles = [(xs[0], xs[0][:]), (xs[1], xs[1][:])]
        for i in range(2):
            P.dma("pool", xs[i][:], x[t0 + i * 128:t0 + (i + 1) * 128, :], [x], [xs[i]], xs[i])
        for c in range(8):
            slc = stream(12 + c)
            bga, bgb, bpa, bpb = [B[(c % 2) * 4 + k_] for k_ in range(4)]
            proj_fm(bga, slc, 0, 128, [hT], hT_k)
            proj_fm(bgb, slc, 128, 128, [hT], hT_k)
            ga, gb_ = gettmp(), gettmp()
            P.act(ga[:], bga[:], AF.Sigmoid, [bga, vc], [ga], bias=vc[:, V_BG + c:V_BG + c + 1])
            P.act(gb_[:], bgb[:], AF.Sigmoid, [bgb, vc], [gb_], bias=vc[:, V_BG + 8 + c:V_BG + 8 + c + 1])
            for kc in range(8):
                P.mm(bpa[:], slc[:, kc, 256:384], yaT[kc // 4][:, kc % 4, :], kc == 0, kc == 7,
                     [slc, yaT[kc // 4]], [bpa])
            for kc in range(8):
                P.mm(bpb[:], slc[:, kc, 384:512], ybT[kc][:], kc == 0, kc == 7, [slc, ybT[kc]], [bpb])
            P.tt("dve", ga[:], ga[:], bpa[:], ALU.mult, [ga, bpa], [ga])
            P.tt("dve", gb_[:], gb_[:], bpb[:], ALU.mult, [gb_, bpb], [gb_])
            P.tt("dve", mT[:, c, :], ga[:], gb_[:], ALU.add, [ga, gb_], [mT])
        for i in range(2, 4):
            yv = yaT[i - 2][:].rearrange("p a b -> p (a b)").bitcast(F32)
            xtiles.append((yaT[i - 2], yv))
            P.dma("pool", yv, x[t0 + i * 128:t0 + (i + 1) * 128, :], [x], [yaT[i - 2]], yaT[i - 2])
        slo = [stream(20), stream(21)]
        for i in range(NT):
            xt, xv = xtiles[i]
            r0 = t0 + i * 128
            for hf_ in range(2):
                bo = B[4 + (i % 2) * 2 + hf_]
                for kc in range(8):
                    P.mm(bo[:], mT[:, kc, i * 128:(i + 1) * 128], slo[hf_][:, kc, :], kc == 0, kc == 7,
                         [mT, slo[hf_]], [bo])
                P.tt("dve", xv[:, hf_ * 512:(hf_ + 1) * 512], xv[:, hf_ * 512:(hf_ + 1) * 512], bo[:], ALU.add,
                     [xt, bo], [xt])
            P.act(hb[:], xv, AF.Square, [xt], [hb, st4], accum_out=st4[:, 2 + i:3 + i])
            rstd_from(st4, st4[:, 2 + i:3 + i], st4, st4[:, 2 + i:3 + i], 1.0 / D)
            P.stt(xv, xv, st4[:, 2 + i:3 + i], fgb[:], ALU.mult, ALU.mult, [xt, st4, fgb], [xt])
            ob = Buf(f"out{g}_{i}")
            P.dma("pool", out[r0:r0 + 128, :], xv, [xt], [ob], xt)
            out_tiles.append(ob)

    NB_UNITS = 2 * (2 * 7 + 4 + 2 + 1 + 4 * 5)
    for _ in stage_A(0):
        pass
    for g in range(NG):
        stage_C(g)
        gens = [stage_B(g)]
        if g + 1 < NG:
            gens.append(stage_A(g + 1))
        gd = stage_D(g)
        nb_left = NB_UNITS
        nd_left = H * (4 * g + 4 + 4)
        d_alive = True
        while gens or d_alive:
            for gen in list(gens):
                try:
                    next(gen)
                except StopIteration:
                    gens.remove(gen)
            nb_left -= 1
            if d_alive:
                k = (1 << 30) if not gens else max(1, -(-nd_left // max(nb_left, 1)))
                for _ in range(k):
                    try:
                        next(gd)
                        nd_left -= 1
                    except StopIteration:
                        d_alive = False
                        break
        if stop_after == "D":
            break
        stage_E(g)

    P.emit(out_tiles + list(dbg_outs.values()))
    return nc, P


def host_consts(S):
    cst = np.zeros((128, 896), np.float32)
    cst[:, 0:128] = np.eye(128, dtype=np.float32)
    s = np.arange(128)[:, None]
    t = np.arange(128)[None, :]
    cst[:, 128:256] = ((s // 32 == t // 32) & (s <= t)).astype(np.float32)
    cst[:, 256:384] = (t >= s).astype(np.float32)
    rm = np.ones((128, 512), np.float32)
    rm[:, ::32] = 0.0
    cst[:, 384:896] = rm
    inv = (np.float32(10000.0) ** (-np.arange(0, 64, 2, dtype=np.float32) / np.float32(64))).astype(np.float32)
    ang = (np.arange(S, dtype=np.float32)[:, None] * inv[None, :]).astype(np.float32)
    cos = np.cos(ang).astype(np.float32).T
    sin = np.sin(ang).astype(np.float32).T
    rope = np.zeros((2, 64, S), np.float32)
    rope[0, 0:32] = cos
    rope[0, 32:64] = cos
    rope[1, 0:32] = -sin
    rope[1, 32:64] = sin
    return cst, rope


def pc(v):
    v = np.asarray(v, np.float32).reshape(-1, 128)
    return np.ascontiguousarray(v.T)


def make_in_maps(inputs, S):
    cst, rope = host_consts(S)
    vecs = np.concatenate([
        pc(inputs["norm_g"][0]), pc(inputs["b_gate"][0]), pc(inputs["lb_logits"][0]), pc(inputs["lb_logits"][1]),
        pc(inputs["hg_norm_g"][0]), pc(inputs["q_a_g"][0]), pc(inputs["kv_a_g"][0])], axis=1)
    assert vecs.shape == (128, 46)
    fgb = np.ascontiguousarray(np.broadcast_to(np.asarray(inputs["final_norm_g"], np.float32)[None, :], (128, D)))
    common = {
        "w_in": np.ascontiguousarray(inputs["w_in"][0], dtype=np.float32),
        "w_uq": np.ascontiguousarray(inputs["w_uq"][0], dtype=np.float32),
        "w_ukv": np.ascontiguousarray(inputs["w_ukv"][0], dtype=np.float32),
        "w_pa": np.ascontiguousarray(inputs["w_proj_a"][0], dtype=np.float32),
        "w_pb": np.ascontiguousarray(inputs["w_proj_b"][0], dtype=np.float32),
        "w_out": np.ascontiguousarray(inputs["w_out"][0], dtype=np.float32),
        "vecs": vecs, "fgb": fgb, "cst": cst, "rope": rope,
    }
    xa = np.asarray(inputs["x"], np.float32)
    return [dict(common, x=np.ascontiguousarray(xa[b])) for b in range(xa.shape[0])]


_CACHE = {}


def kernel(**inputs):
    xa = np.asarray(inputs["x"])
    Bn, S, _ = xa.shape
    if S not in _CACHE:
        _CACHE[S] = build(S)[0]
    nc = _CACHE[S]
    in_maps = make_in_maps(inputs, S)
    res = run_bass_kernel_spmd(nc, in_maps, core_ids=list(range(Bn)))
    return np.stack([np.asarray(r["out"], np.float32) for r in res.results], axis=0)
```

```python
import numpy as np
from contextlib import ExitStack
import concourse.bass as bass
import concourse.mybir as mybir
from concourse.bass_utils import run_bass_kernel_spmd

F32 = mybir.dt.float32
BF16 = mybir.dt.bfloat16
ALU = mybir.AluOpType
AF = mybir.ActivationFunctionType

D = 1024
H = 8
T = 512
NT = 4
EPS = 1e-6
QSCALE = 192.0 ** -0.5
IN_COLS = 7872
O_HQ, O_HF, O_HI, O_HZ, O_CQ, O_CKV, O_KR, O_MZ, O_GL = 0, 1024, 2048, 3072, 4096, 4480, 4736, 4800, 5824

COMPUTE = ("pe", "act", "dve", "pool")
SEM_CHUNK = 12000


class Buf:
    __slots__ = ("name", "writers", "readers", "dsem", "dcount", "psum")

    def __init__(self, name):
        self.name = name
        self.writers = {}
        self.readers = []
        self.dsem = None
        self.dcount = 0
        self.psum = False


class Tl(Buf):
    __slots__ = ("t",)

    def __init__(self, name, t):
        Buf.__init__(self, name)
        self.t = t

    def __getitem__(self, k):
        return self.t[k]


class Op:
    __slots__ = ("eng", "fn", "deps", "flag", "tok", "is_dma", "dbuf", "dval")

    def __init__(self, eng, fn, is_dma, dbuf):
        self.eng = eng
        self.fn = fn
        self.deps = []
        self.flag = False
        self.tok = None
        self.is_dma = is_dma
        self.dbuf = dbuf
        self.dval = 0


class Prog:
    def __init__(self, nc):
        self.nc = nc
        self.q = {k: [] for k in ("pe", "act", "dve", "pool", "sp")}
        self.dma_bufs = []
        self.owners = {}

    def sb(self, name, shape, dtype):
        return Tl(name, self.nc.alloc_sbuf_tensor(name, list(shape), dtype))

    def ps(self, name, shape, dtype=F32):
        t = Tl(name, self.nc.alloc_psum_tensor(name, list(shape), dtype))
        t.psum = True
        return t

    def dram(self, name, shape, dtype, kind="Internal"):
        return Tl(name, self.nc.dram_tensor(name, list(shape), dtype, kind=kind))

    def op(self, eng, fn, reads=(), writes=(), dma_dst=None):
        is_dma = dma_dst is not None
        o = Op(eng, fn, is_dma, dma_dst)
        deps = {}

        def add(d, kind):
            if d is o:
                return
            if d.is_dma:
                if is_dma and kind == "waw" and d.dbuf is dma_dst:
                    return
                deps[id(d)] = d
                return
            if (not is_dma) and d.eng == eng:
                if eng == "pe" or kind != "raw":
                    return
            deps[id(d)] = d

        for b in reads:
            for w in b.writers.values():
                add(w, "raw")
            if b.psum:
                for r in b.readers:
                    if r.eng != eng:
                        add(r, "raw")
        for b in writes:
            for r in b.readers:
                add(r, "war")
            for w in b.writers.values():
                add(w, "waw")
        o.deps = list(deps.values())
        for d in o.deps:
            if not d.is_dma:
                d.flag = True
        for b in reads:
            if not is_dma:
                b.readers = [r for r in b.readers if r.is_dma or r.eng != eng]
            b.readers.append(o)
        for b in writes:
            b.readers = []
            b.writers = {(("dma", id(dma_dst)) if is_dma else eng): o}
        if is_dma:
            if dma_dst.dcount == 0:
                self.dma_bufs.append(dma_dst)
            dma_dst.dcount += 1
            o.dval = 16 * dma_dst.dcount
        self.q[eng].append(o)
        return o

    def mm(self, out, lhsT, rhs, start, stop, reads, writes, **kw):
        return self.op("pe", lambda e: e.matmul(out, lhsT, rhs, start=start, stop=stop, **kw), reads, writes)

    def tr(self, out, in_, ident, reads, writes):
        return self.op("pe", lambda e: e.transpose(out, in_, ident), reads, writes)

    def act(self, out, in_, func, reads, writes, **kw):
        return self.op("act", lambda e: e.activation(out, in_, func, **kw), reads, writes)

    def dma(self, eng, out, in_, reads, writes, dst):
        key = (id(dst), eng)
        if key not in self.owners:
            self.owners[key] = Buf(dst.name + "@" + eng)
        return self.op(eng, lambda e: e.dma_start(out, in_), reads, writes, dma_dst=self.owners[key])

    def tt(self, eng, out, in0, in1, op, reads, writes):
        return self.op(eng, lambda e: e.tensor_tensor(out, in0, in1, op), reads, writes)

    def ts(self, eng, out, in0, s1, s2, op0, op1, reads, writes):
        return self.op(eng, lambda e: e.tensor_scalar(out, in0, s1, s2, op0, op1), reads, writes)

    def stt(self, out, in0, scalar, in1, op0, op1, reads, writes):
        return self.op("dve", lambda e: e.scalar_tensor_tensor(out, in0, scalar, in1, op0, op1), reads, writes)

    def cp(self, eng, out, in_, reads, writes):
        if eng == "act":
            return self.op("act", lambda e: e.activation(out, in_, AF.Copy), reads, writes)
        return self.op(eng, lambda e: e.tensor_copy(out, in_), reads, writes)

    def emit(self, final_reads=()):
        nc = self.nc
        self.op("sp", None, reads=final_reads)
        with ExitStack() as es:
            esems = {}
            for k in COMPUTE:
                n = sum(1 for o in self.q[k] if o.flag)
                ns = max(1, (n + SEM_CHUNK - 1) // SEM_CHUNK)
                esems[k] = [es.enter_context(nc.semaphore(f"s_{k}{i}")) for i in range(ns)]
                c = 0
                for o in self.q[k]:
                    if o.flag:
                        o.tok = (esems[k][c // SEM_CHUNK], (c % SEM_CHUNK) + 1)
                        c += 1
            for i, b in enumerate(self.dma_bufs):
                b.dsem = es.enter_context(nc.semaphore(f"d{i}"))
            for k in self.q:
                for o in self.q[k]:
                    if o.is_dma:
                        o.tok = (o.dbuf.dsem, o.dval)
            self.nsem = sum(len(v) for v in esems.values()) + len(self.dma_bufs)
            block = es.enter_context(nc.Block())

            def run(e, k):
                waited = {}
                for o in self.q[k]:
                    need = {}
                    for d in o.deps:
                        s, v = d.tok
                        sid = id(s)
                        if waited.get(sid, 0) >= v:
                            continue
                        if sid not in need or need[sid][1] < v:
                            need[sid] = (s, v)
                    for sid, (s, v) in need.items():
                        e.wait_ge(s, v)
                        waited[sid] = v
                    if o.fn is None:
                        continue
                    ins = o.fn(e)
                    if o.is_dma:
                        ins.then_inc(o.tok[0], 16)
                    elif o.flag:
                        ins.then_inc(o.tok[0], 1)

            @block.tensor
            def _(e):
                run(e, "pe")

            @block.scalar
            def _(e):
                run(e, "act")

            @block.vector
            def _(e):
                run(e, "dve")

            @block.gpsimd
            def _(e):
                run(e, "pool")

            @block.sync
            def _(e):
                run(e, "sp")


def build(S, dbg=None, stop_after=None):
    NG = S // T
    nc = bass.Bass("TRN2", target_bir_lowering=False)
    P = Prog(nc)
    dbg_outs = {}

    x = P.dram("x", [S, D], F32, kind="ExternalInput")
    w_in = P.dram("w_in", [D, IN_COLS], F32, kind="ExternalInput")
    w_uq = P.dram("w_uq", [384, 1536], F32, kind="ExternalInput")
    w_ukv = P.dram("w_ukv", [256, 2048], F32, kind="ExternalInput")
    w_pa = P.dram("w_pa", [D, D], F32, kind="ExternalInput")
    w_pb = P.dram("w_pb", [D, D], F32, kind="ExternalInput")
    w_out = P.dram("w_out", [D, D], F32, kind="ExternalInput")
    vecs = P.dram("vecs", [128, 46], F32, kind="ExternalInput")
    fgb_d = P.dram("fgb", [128, D], F32, kind="ExternalInput")
    cst_d = P.dram("cst", [128, 896], F32, kind="ExternalInput")
    rope_d = P.dram("rope", [2, 64, S], F32, kind="ExternalInput")
    out = P.dram("out", [S, D], F32, kind="ExternalOutput")

    NCH = 22
    wsc = nc.dram_tensor("wsc", [NCH, 128, 8, 512], BF16)
    wscB = [Buf(f"wsc{c}") for c in range(NCH)]
    ksc = nc.dram_tensor("ksc", [H, 128, S], BF16)
    vsc = nc.dram_tensor("vsc", [H, 128, S // 128, 128], BF16)
    kscB = [Buf(f"ksc{g}") for g in range(NG)]
    vscB = [Buf(f"vsc{g}") for g in range(NG)]

    cstf = P.sb("cstf", [128, 512], F32)
    identb = P.sb("identb", [128, 128], BF16)
    maskbd = P.sb("maskbd", [128, 128], BF16)
    trib = P.sb("trib", [128, 128], BF16)
    onesb = P.sb("onesb", [128, 128], BF16)
    vc = P.sb("vc", [128, 46], F32)
    lbv = P.sb("lbv", [128, 24], F32)
    fgb = P.sb("fgbs", [128, D], F32)
    wuq = P.sb("wuq", [128, 3, 1536], BF16)
    wuqr = P.sb("wuqr", [128, 3, 8, 64], BF16)
    wukv = P.sb("wukv", [128, 2, 2048], BF16)
    NSLOT = 3
    slots = [P.sb(f"slot{i}", [128, 8, 512], BF16) for i in range(NSLOT)]
    xs = [P.sb(f"xs{i}", [128, D], F32) for i in range(2)]
    hb = P.sb("hb", [128, D], BF16)
    st4 = P.sb("st4", [128, 8], F32)
    hTs = [P.sb(f"hT{i}", [128, 8, T], BF16) for i in range(2)]
    NTMP = 9
    tmp = [P.sb(f"tmp{i}", [128, T], F32) for i in range(NTMP)]
    qin = [P.sb(f"qin{i}", [128, T], BF16) for i in range(4)]
    kin = [P.sb(f"kin{i}", [128, T], BF16) for i in range(4)]
    koT = [P.sb(f"koT{i}", [128, T], BF16) for i in range(2)]
    ko = [P.sb(f"ko{i}", [128, NT, 128], BF16) for i in range(4)]
    szh = P.sb("szh", [128, 4, T], BF16)
    vT = P.sb("vT", [128, NT, 512], BF16)
    dec = [P.sb(f"dec{i}", [128, 16], F32) for i in range(4)]
    Sst = [P.sb(f"Sst{i}", [128, 4, 128], F32) for i in range(2)]
    Sbf = [P.sb(f"Sbf{i}", [128, 4, 128], BF16) for i in range(2)]
    scm = P.sb("scm", [128, 4, 128], BF16)
    sqo = P.sb("sqo", [128, T], BF16)
    yaT = [P.sb(f"yaT{i}", [128, 4, T], BF16) for i in range(2)]
    ybT = [P.sb(f"ybT{h}", [128, T], BF16) for h in range(H)]
    mT = P.sb("mT", [128, 8, T], BF16)
    cqn = P.sb("cqn", [128, 3, T], BF16)
    ckvn = P.sb("ckvn", [128, 2, T], BF16)
    kpe = P.sb("kpe", [128, S], BF16)
    kpeB = [Buf(f"kpe{g}") for g in range(NG)]
    ropet = P.sb("ropet", [64, 4, T], F32)
    Kn = [P.sb(f"Kn{i}", [128, T], BF16) for i in range(2)]
    Vn = [P.sb(f"Vn{i}", [128, NT, 128], BF16) for i in range(2)]
    Qn = [P.sb(f"Qn{i}", [128, T], BF16) for i in range(2)]
    qpe = [P.sb(f"qpe{i}", [128, T], BF16) for i in range(2)]
    tmpD = [P.sb(f"tmpD{i}", [128, T], F32) for i in range(2)]
    KCH = 1024
    Kc = [P.sb(f"Kc{i}", [128, KCH], BF16) for i in range(2)]
    Vc = [P.sb(f"Vc{i}", [128, KCH // 128, 128], BF16) for i in range(2)]
    NPT = 6
    Pt = [P.sb(f"Pt{i}", [128, T], BF16) for i in range(NPT)]
    Ps = [P.sb(f"Ps{i}", [128, T], BF16) for i in range(2)]

    B = [P.ps(f"bank{i}", [128, 512], F32) for i in range(8)]

    tmp_i = [0]

    def gettmp():
        t = tmp[tmp_i[0] % NTMP]
        tmp_i[0] += 1
        return t

    def tap(name, tl, ap, shape, dtype=F32):
        if dbg is None or name not in dbg:
            return
        d = P.dram("dbg_" + name, list(shape), dtype, kind="ExternalOutput")
        P.dma("sp", d[:], ap, [tl], [d], tl)
        dbg_outs[name] = d

    P.dma("sp", cstf[:], cst_d[:, 384:896], [cst_d], [cstf], cstf)
    P.dma("sp", tmp[0][:, 0:384], cst_d[:, 0:384], [cst_d], [tmp[0]], tmp[0])
    P.dma("sp", vc[:], vecs[:], [vecs], [vc], vc)
    P.dma("sp", fgb[:], fgb_d[:], [fgb_d], [fgb], fgb)
    P.cp("dve", identb[:], tmp[0][:, 0:128], [tmp[0]], [identb])
    P.cp("dve", maskbd[:], tmp[0][:, 128:256], [tmp[0]], [maskbd])
    P.cp("dve", trib[:], tmp[0][:, 256:384], [tmp[0]], [trib])
    resetm = cstf
    P.op("dve", lambda e: e.memset(onesb[:], 1.0), [], [onesb])
    P.op("pool", lambda e: e.memset(kpe[64:128, :], 0.0), [], [kpeB[g_] for g_ in range(NG)])
    for i_ in range(2):
        P.op("pool", lambda e, i_=i_: e.memset(qpe[i_][64:128, :], 0.0), [], [qpe[i_]])
    for h in range(2):
        P.op("pool", lambda e, h=h: e.memset(Sst[h][:], 0.0), [], [Sst[h]])
        P.op("pool", lambda e, h=h: e.memset(Sbf[h][:], 0.0), [], [Sbf[h]])
    V_NG, V_BG, V_L0, V_L1, V_HGG, V_QAG, V_KVAG = 0, 8, 24, 32, 40, 41, 44
    P.tt("dve", lbv[:, 16:24], vc[:, V_L0:V_L0 + 8], vc[:, V_L1:V_L1 + 8], ALU.subtract, [vc], [lbv])
    P.act(lbv[:, 0:8], lbv[:, 16:24], AF.Sigmoid, [lbv], [lbv])
    P.act(lbv[:, 8:16], lbv[:, 16:24], AF.Sigmoid, [lbv], [lbv], scale=-1.0)
    P.ts("dve", lbv[:, 16:24], lbv[:, 8:16], -1.0, None, ALU.mult, ALU.bypass, [lbv], [lbv])

    def wsrc(wt, c0, n):
        return wt.t.ap().rearrange("(kc p) c -> p kc c", p=128)[:, :, c0:c0 + n]

    def conv(ci, col, wt, c0, n):
        P.dma("pool", wsc[ci, :, :, col:col + n], wsrc(wt, c0, n), [wt], [wscB[ci]], wscB[ci])

    conv(8, 0, w_in, O_CQ, 384)
    conv(8, 384, w_in, O_KR, 64)
    conv(8, 448, w_in, O_KR + 32, 32)
    conv(8, 480, w_in, O_KR, 32)
    conv(9, 0, w_in, O_CKV, 256)
    conv(10, 0, w_in, O_MZ, 512)
    conv(11, 0, w_in, O_MZ + 512, 512)
    P.dma("pool", wuq[:], w_uq.t.ap().rearrange("(kc p) c -> p kc c", p=128), [w_uq], [wuq], wuq)
    for hf_ in range(2):
        P.dma("pool", wukv[:, :, hf_ * 1024:(hf_ + 1) * 1024],
              w_ukv.t.ap().rearrange("(kc p) c -> p kc c", p=128)[:, :, hf_ * 1024:(hf_ + 1) * 1024],
              [w_ukv], [wukv], wukv)
    for half in range(2):
        for j, o in enumerate((O_HQ, O_HF, O_HI, O_HZ)):
            conv(half * 4 + j, 0, w_in, o + half * 512, 512)
    for c in range(8):
        conv(12 + c, 0, w_in, O_GL + c * 128, 128)
        conv(12 + c, 128, w_in, O_GL + 1024 + c * 128, 128)
        conv(12 + c, 256, w_pa, c * 128, 128)
        conv(12 + c, 384, w_pb, c * 128, 128)
    conv(20, 0, w_out, 0, 512)
    conv(21, 0, w_out, 512, 512)
    CH_NCOL = [512] * 8 + [512, 256, 512, 512] + [512] * 8 + [512, 512]
    wuq4 = wuq[:, :, :].rearrange("p k (h c) -> p k h c", c=192)
    P.cp("dve", wuqr[:, :, :, 0:32], wuq4[:, :, :, 160:192], [wuq], [wuqr])
    P.cp("dve", wuqr[:, :, :, 32:64], wuq4[:, :, :, 128:160], [wuq], [wuqr])

    sstate = {"n": 0}

    def stream(ci):
        sl = slots[sstate["n"] % NSLOT]
        sstate["n"] += 1
        n = CH_NCOL[ci]
        P.dma("sp", sl[:, :, 0:n], wsc[ci, :, :, 0:n], [wscB[ci]], [sl], sl)
        return sl

    bank_rr = [0]

    def pbank(cands):
        b = cands[bank_rr[0] % len(cands)]
        bank_rr[0] += 1
        return B[b]

    def proj_fm(bank, sl, col, m, rhs_tl, rhs_of_kc, nk=8, rows=128):
        for kc in range(nk):
            P.mm(bank[0:m, 0:T], sl[0:rows, kc, col:col + m], rhs_of_kc(kc), kc == 0, kc == nk - 1,
                 [sl] + rhs_tl, [bank])

    def rstd_from(bank_or_tl, src_ap, dst_tl, dst_ap, scale):
        P.act(dst_ap, src_ap, AF.Ln, [bank_or_tl], [dst_tl], scale=scale, bias=EPS)
        P.act(dst_ap, dst_ap, AF.Exp, [dst_tl], [dst_tl], scale=-0.5)

    out_tiles = []

    def stage_A(g):
        t0 = g * T
        hT = hTs[g % 2]

        def load(i):
            xt = xs[i % 2]
            P.dma("sp", xt[:], x[t0 + i * 128:t0 + (i + 1) * 128, :], [x], [xt], xt)

        load(0)
        load(1)
        yield
        for i in range(NT):
            xt = xs[i % 2]
            P.act(hb[:], xt[:], AF.Square, [xt], [hb, st4], accum_out=st4[:, 0:1])
            yield
            rstd_from(st4, st4[:, 0:1], st4, st4[:, 1:2], 1.0 / D)
            P.act(hb[:], xt[:], AF.Copy, [xt, st4], [hb], scale=st4[:, 1:2])
            if i + 2 < NT:
                load(i + 2)
            yield
            bkA = nextbank_g()
            ptb = bkA[:].bitcast(BF16)
            for kc in range(8):
                P.tr(ptb[:, kc * 128:(kc + 1) * 128], hb[:, kc * 128:(kc + 1) * 128], identb[:],
                     [hb, identb], [bkA])
            P.tt("dve", hT[:, :, i * 128:(i + 1) * 128], ptb.rearrange("p (k t) -> p k t", t=128),
                 vc[:, V_NG:V_NG + 8].unsqueeze(2).to_broadcast([128, 8, 128]), ALU.mult, [bkA, vc], [hT])
            yield

    def stage_B(g):
        hT = hTs[g % 2]
        hT_k = lambda kc: hT[:, kc, :]
        for half in range(2):
            sl_q = stream(half * 4 + 0)
            sl_f = stream(half * 4 + 1)
            for pair in range(2):
                hhs = [pair * 2, pair * 2 + 1]
                sg, lf, eb, en, sq = {}, {}, {}, {}, {}
                bks = {}
                for hh in hhs:
                    bks[hh] = pbank([0, 1])
                    proj_fm(bks[hh], sl_f, hh * 128, 128, [hT], hT_k)
                for hh in hhs:
                    sg[hh] = gettmp()
                    P.act(sg[hh][:], bks[hh][:], AF.Sigmoid, [bks[hh]], [sg[hh]])
                yield
                for hh in hhs:
                    h = half * 4 + hh
                    lf[hh] = gettmp()
                    P.act(lf[hh][:], sg[hh][:], AF.Ln, [sg[hh], lbv], [lf[hh]],
                          scale=lbv[:, 8 + h:9 + h], bias=lbv[:, h:h + 1])
                    P.op("dve", lambda e, a=lf[hh]: e.tensor_tensor_scan(a[:], resetm[:, 0:512], a[:], 0.0,
                                                                          ALU.mult, ALU.add),
                         [cstf, lf[hh]], [lf[hh]])
                    P.ts("dve", sg[hh][:], sg[hh][:], lbv[:, 16 + h:17 + h], lbv[:, 8 + h:9 + h], ALU.mult, ALU.add,
                         [sg[hh], lbv], [sg[hh]])
                yield
                for hh in hhs:
                    eb[hh] = gettmp()
                    en[hh] = gettmp()
                    P.act(eb[hh][:], lf[hh][:], AF.Exp, [lf[hh]], [eb[hh]])
                    P.act(en[hh][:], lf[hh][:], AF.Exp, [lf[hh]], [en[hh]], scale=-1.0)
                    b3 = lf[hh][:, :].rearrange("p (c t) -> p c t", t=32)
                    P.tt("dve", b3, b3[:, :, 31:32].to_broadcast([128, 16, 32]), b3, ALU.subtract,
                         [lf[hh]], [lf[hh]])
                    P.act(lf[hh][:], lf[hh][:], AF.Exp, [lf[hh]], [lf[hh]])
                    P.cp("dve", dec[hh][:], eb[hh][:, :].rearrange("p (c t) -> p c t", t=32)[:, :, 31],
                         [eb[hh]], [dec[hh]])
                yield
                for hh in hhs:
                    P.tt("dve", kin[hh][:], sg[hh][:], en[hh][:], ALU.mult, [sg[hh], en[hh]], [kin[hh]])
                    kt = koT[hh % 2]
                    P.tt("dve", kt[:], sg[hh][:], lf[hh][:], ALU.mult, [sg[hh], lf[hh]], [kt])
                yield
                for hh in hhs:
                    bks[hh] = pbank([0, 1])
                    proj_fm(bks[hh], sl_q, hh * 128, 128, [hT], hT_k)
                for hh in hhs:
                    sq[hh] = gettmp()
                    P.act(sq[hh][:], bks[hh][:], AF.Silu, [bks[hh]], [sq[hh]])
                yield
                for hh in hhs:
                    P.tt("dve", qin[hh][:], sq[hh][:], eb[hh][:], ALU.mult, [sq[hh], eb[hh]], [qin[hh]])
                    kt = koT[hh % 2]
                    trb = B[2][:].bitcast(BF16)
                    for i in range(NT):
                        P.tr(trb[:, i * 128:(i + 1) * 128], kt[:, i * 128:(i + 1) * 128], identb[:],
                             [kt, identb], [B[2]])
                    yield
                    P.cp("dve", ko[hh][:].rearrange("p a b -> p (a b)"), trb[:, 0:512], [B[2]], [ko[hh]])
            sl_i = stream(half * 4 + 2)
            prev = None
            for i in range(NT):
                bk = pbank([0, 1])
                for kc in range(8):
                    P.mm(bk[:], hT[:, kc, i * 128:(i + 1) * 128], sl_i[:, kc, :], kc == 0, kc == 7,
                         [hT, sl_i], [bk])
                if prev is not None:
                    P.cp("dve", vT[:, prev[0], :], prev[1][:], [prev[1]], [vT])
                prev = (i, bk)
                yield
            P.cp("dve", vT[:, prev[0], :], prev[1][:], [prev[1]], [vT])
            sl_z = stream(half * 4 + 3)
            for pz in range(2):
                for hh in (2 * pz, 2 * pz + 1):
                    bks[hh] = pbank([0, 1])
                    proj_fm(bks[hh], sl_z, hh * 128, 128, [hT], hT_k)
                for hh in (2 * pz, 2 * pz + 1):
                    P.act(szh[:, hh, :], bks[hh][:], AF.Silu, [bks[hh]], [szh])
                yield
            SC, OA, DS, SSB = B[0], B[1], B[2], B[0]
            def sc_mm(i):
                for hh in range(4):
                    P.mm(SC[:, hh * 128:(hh + 1) * 128], kin[hh][:, i * 128:(i + 1) * 128],
                         qin[hh][:, i * 128:(i + 1) * 128], True, True, [kin[hh], qin[hh]], [SC])

            def sc_mask():
                P.tt("dve", scm[:], SC[:, :].rearrange("p (h t) -> p h t", t=128),
                     maskbd[:, :].unsqueeze(1).to_broadcast([128, 4, 128]), ALU.mult, [SC, maskbd], [scm])

            sc_mm(0)
            yield
            sc_mask()
            yield
            for i in range(NT):
                for hh in range(4):
                    P.mm(OA[:, hh * 128:(hh + 1) * 128], vT[:, i, hh * 128:(hh + 1) * 128], scm[:, hh, :],
                         hh == 0, False, [vT, scm], [OA], skip_group_check=True)
                for j in range(4):
                    for hh in range(4):
                        h = half * 4 + hh
                        c0 = i * 128 + j * 32
                        P.mm(OA[:, hh * 128 + j * 32:hh * 128 + (j + 1) * 32], Sbf[half][:, hh, :],
                             qin[hh][:, c0:c0 + 32], False, (j == 3 and hh == 3), [Sbf[half], qin[hh]], [OA],
                             skip_group_check=True)
                    for hh in range(4):
                        P.mm(DS[:, hh * 128:(hh + 1) * 128], ko[hh][32 * j:32 * (j + 1), i, :],
                             vT[32 * j:32 * (j + 1), i, hh * 128:(hh + 1) * 128], True, True,
                             [ko[hh], vT], [DS], tile_position=(32 * j, 0), skip_group_check=True)
                    yield
                    for hh in range(4):
                        cidx = i * 4 + j
                        P.stt(Sst[half][:, hh, :], Sst[half][:, hh, :], dec[hh][:, cidx:cidx + 1],
                              DS[:, hh * 128:(hh + 1) * 128], ALU.mult, ALU.add, [Sst[half], dec[hh], DS], [Sst[half]])
                    P.cp("dve", Sbf[half][:], Sst[half][:], [Sst[half]], [Sbf[half]])
                    yield
                if i + 1 < NT:
                    sc_mm(i + 1)
                    yield
                    sc_mask()
                P.act(sqo[:], OA[:], AF.Square, [OA], [sqo])
                yield
                P.mm(SSB[:], onesb[:], sqo[:], True, True, [onesb, sqo], [SSB])
                yield
                rs = gettmp()
                rstd_from(SSB, SSB[:], rs, rs[:], 1.0 / 128)
                yield
                t1 = gettmp()
                P.stt(t1[:], OA[:], vc[:, V_HGG:V_HGG + 1], rs[:], ALU.mult, ALU.mult, [OA, vc, rs], [t1])
                P.tt("dve", yaT[half][:, :, i * 128:(i + 1) * 128], t1[:, :].rearrange("p (h t) -> p h t", t=128),
                     szh[:, :, i * 128:(i + 1) * 128], ALU.mult, [t1, szh], [yaT[half]])
                yield

    def stage_C(g):
        t0 = g * T
        hT = hTs[g % 2]
        hT_k = lambda kc: hT[:, kc, :]
        P.dma("sp", ropet[:, 0, :], rope_d[0, :, t0:t0 + T], [rope_d], [ropet], ropet)
        P.dma("sp", ropet[:, 1, :], rope_d[1, :, t0:t0 + T], [rope_d], [ropet], ropet)
        P.ts("dve", ropet[:, 2:4, :], ropet[:, 0:2, :], QSCALE, None, ALU.mult, ALU.bypass, [ropet], [ropet])
        sl8 = stream(8)
        SSB = B[2]
        cqf = []
        for k3 in range(3):
            bk = pbank([0, 1, 3, 4, 5, 6, 7])
            proj_fm(bk, sl8, k3 * 128, 128, [hT], hT_k)
            cf = gettmp()
            cqf.append(cf)
            P.cp("dve", cf[:], bk[:], [bk], [cf])
            P.act(sqo[:], bk[:], AF.Square, [bk], [sqo])
            P.mm(SSB[:], onesb[:], sqo[:], k3 == 0, k3 == 2, [onesb, sqo], [SSB])
        rs = gettmp()
        rstd_from(SSB, SSB[:], rs, rs[:], 1.0 / 384)
        for k3 in range(3):
            P.stt(cqn[:, k3, :], cqf[k3][:], vc[:, V_QAG + k3:V_QAG + k3 + 1], rs[:], ALU.mult, ALU.mult,
                  [cqf[k3], vc, rs], [cqn])
        bka, bkb = pbank([0, 1, 3, 4, 5, 6, 7]), pbank([0, 1, 3, 4, 5, 6, 7])
        proj_fm(bka, sl8, 384, 64, [hT], hT_k)
        proj_fm(bkb, sl8, 448, 64, [hT], hT_k)
        ta, tb = gettmp(), gettmp()
        P.tt("dve", ta[0:64, :], bka[0:64, :], ropet[:, 0, :], ALU.mult, [bka, ropet], [ta])
        P.tt("dve", tb[0:64, :], bkb[0:64, :], ropet[:, 1, :], ALU.mult, [bkb, ropet], [tb])
        P.tt("dve", kpe[0:64, t0:t0 + T], ta[0:64, :], tb[0:64, :], ALU.add, [ta, tb], [kpeB[g]])
        sl9 = stream(9)
        ckf = []
        for k2 in range(2):
            bk = pbank([0, 1, 3, 4, 5, 6, 7])
            proj_fm(bk, sl9, k2 * 128, 128, [hT], hT_k)
            cf = gettmp()
            ckf.append(cf)
            P.cp("dve", cf[:], bk[:], [bk], [cf])
            P.act(sqo[:], bk[:], AF.Square, [bk], [sqo])
            P.mm(SSB[:], onesb[:], sqo[:], k2 == 0, k2 == 1, [onesb, sqo], [SSB])
        rs = gettmp()
        rstd_from(SSB, SSB[:], rs, rs[:], 1.0 / 256)
        for k2 in range(2):
            P.stt(ckvn[:, k2, :], ckf[k2][:], vc[:, V_KVAG + k2:V_KVAG + k2 + 1], rs[:], ALU.mult, ALU.mult,
                  [ckf[k2], vc, rs], [ckvn])
        for half in range(2):
            slm = stream(10 + half)
            for hh in range(4):
                bk = pbank([0, 1, 3, 4, 5, 6, 7])
                proj_fm(bk, slm, hh * 128, 128, [hT], hT_k)
                P.act(ybT[half * 4 + hh][:], bk[:], AF.Silu, [bk], [ybT[half * 4 + hh]])

    sidx = [0]
    SB3g = [B[3], B[4], B[5]]

    def nextbank_g():
        b_ = SB3g[sidx[0] % 3]
        sidx[0] += 1
        return b_

    ptidx = [0]
    psidx = [0]

    def stage_D(g):
        t0 = g * T
        SB3 = [B[3], B[4], B[5]]
        OAc, LAc = B[6], B[7]
        DEPTH = 3

        def nextbank():
            b_ = SB3[sidx[0] % 3]
            sidx[0] += 1
            return b_

        def proj_units(h):
            p2 = h % 2

            def u_k():
                bk = nextbank()
                for k2 in range(2):
                    P.mm(bk[:], wukv[:, k2, h * 256:h * 256 + 128], ckvn[:, k2, :], k2 == 0, k2 == 1,
                         [wukv, ckvn], [bk])
                P.cp("dve", Kn[p2][:], bk[:], [bk], [Kn[p2]])
                P.dma("pool", ksc[h, :, t0:t0 + T], Kn[p2][:], [Kn[p2]], [kscB[g]], Kn[p2])

            def u_v():
                bk = nextbank()
                for i in range(NT):
                    for k2 in range(2):
                        P.mm(bk[:, i * 128:(i + 1) * 128], ckvn[:, k2, i * 128:(i + 1) * 128],
                             wukv[:, k2, h * 256 + 128:h * 256 + 256], (i == 0 and k2 == 0),
                             (i == NT - 1 and k2 == 1), [ckvn, wukv], [bk], skip_group_check=True)
                P.cp("dve", Vn[p2][:].rearrange("p a b -> p (a b)"), bk[:], [bk], [Vn[p2]])
                P.dma("pool", vsc[h, :, g * NT:(g + 1) * NT, :], Vn[p2][:], [Vn[p2]], [vscB[g]], Vn[p2])

            def u_q():
                bk = nextbank()
                for k3 in range(3):
                    P.mm(bk[:], wuq[:, k3, h * 192:h * 192 + 128], cqn[:, k3, :], k3 == 0, k3 == 2,
                         [wuq, cqn], [bk])
                P.act(Qn[p2][:], bk[:], AF.Copy, [bk], [Qn[p2]], scale=QSCALE)

            def u_qa():
                bka = nextbank()
                for k3 in range(3):
                    P.mm(bka[0:64, :], wuq[:, k3, h * 192 + 128:h * 192 + 192], cqn[:, k3, :], k3 == 0, k3 == 2,
                         [wuq, cqn], [bka])
                ta = tmpD[0]
                P.tt("dve", ta[0:64, :], bka[0:64, :], ropet[:, 2, :], ALU.mult, [bka, ropet], [ta])

            def u_qb():
                bkb = nextbank()
                for k3 in range(3):
                    P.mm(bkb[0:64, :], wuqr[:, k3, h, :], cqn[:, k3, :], k3 == 0, k3 == 2, [wuqr, cqn], [bkb])
                ta = tmpD[0]
                P.stt(qpe[p2][0:64, :], bkb[0:64, :], 1.0, ropet[:, 3, :], ALU.mult, ALU.mult,
                      [bkb, ropet], [qpe[p2]])
                P.tt("dve", qpe[p2][0:64, :], qpe[p2][0:64, :], ta[0:64, :], ALU.add, [qpe[p2], ta], [qpe[p2]])

            return [u_k, u_v, u_q, u_qa, u_qb]

        for u in proj_units(0):
            u()
        yield
        pend_epi = []
        for h in range(H):
            p2 = h % 2
            nxt = proj_units(h + 1) if h + 1 < H else []

            blocks = []
            npast = T * g
            ci = 0
            for c0 in range(0, npast, KCH):
                n = min(KCH, npast - c0)
                for kb in range(n // 128):
                    blocks.append(("past", ci, c0, n, kb))
                ci += 1
            for j in range(NT):
                blocks.append(("diag", j))
            nblk = len(blocks)
            state = {}

            def front(bi):
                d = blocks[bi]
                sb_ = nextbank()
                pt_ = Pt[ptidx[0] % NPT]
                ptidx[0] += 1
                if d[0] == "past":
                    _, ci_, c0, n, kb = d
                    kc_, vc_ = Kc[ci_ % 2], Vc[ci_ % 2]
                    if kb == 0:
                        gs = list(range(c0 // T, (c0 + n) // T))
                        P.dma("pool", kc_[:, 0:n], ksc[h, :, c0:c0 + n], [kscB[q] for q in gs], [kc_], kc_)
                        P.dma("pool", vc_[:, 0:n // 128, :], vsc[h, :, c0 // 128:(c0 + n) // 128, :],
                              [vscB[q] for q in gs], [vc_], vc_)
                    klhs, k_tl, kabs = kc_[:, kb * 128:(kb + 1) * 128], kc_, c0 + kb * 128
                    vlhs, v_tl, q0, dj = vc_[:, kb, :], vc_, 0, None
                else:
                    j = d[1]
                    klhs, k_tl, kabs = Kn[p2][:, j * 128:(j + 1) * 128], Kn[p2], t0 + j * 128
                    vlhs, v_tl, q0, dj = Vn[p2][:, j, :], Vn[p2], j * 128, j
                gk = kabs // T
                P.mm(sb_[:, q0:T], klhs, Qn[p2][:, q0:T], True, False, [k_tl, Qn[p2]], [sb_])
                P.mm(sb_[:, q0:T], kpe[:, kabs:kabs + 128], qpe[p2][:, q0:T], False, True,
                     [kpeB[gk], qpe[p2]], [sb_])
                P.act(pt_[:, q0:T], sb_[:, q0:T], AF.Exp, [sb_], [pt_])
                if dj is not None:
                    P.tt("dve", pt_[:, q0:q0 + 128], pt_[:, q0:q0 + 128], trib[:], ALU.mult, [pt_, trib], [pt_])
                state[bi] = (pt_, vlhs, v_tl, q0)

            pending = []
            held = [None]

            def flush_ones(upto=1 << 30):
                while pending and pending[0][2] <= upto:
                    ps_, grp, _ = pending.pop(0)
                    P.mm(LAc[:], onesb[:], ps_[:], grp == 0, grp == nblk // 4 - 1, [onesb, ps_], [LAc],
                         skip_group_check=True)

            def back(bi):
                pt_, vlhs, v_tl, q0 = state.pop(bi)
                first = bi == 0
                last = bi == nblk - 1
                flush_ones(bi)
                P.mm(OAc[:, q0:T], vlhs, pt_[:, q0:T], first, last, [v_tl, pt_], [OAc], skip_group_check=True)
                grp, pos = bi // 4, bi % 4
                ps_ = Ps[(psidx[0] + grp) % 2]
                eng_ = "dve"
                if pos == 0:
                    held[0] = (pt_, q0)
                elif pos == 1:
                    p0_, q00 = held[0]
                    P.tt(eng_, ps_[:, q0:T], p0_[:, q0:T], pt_[:, q0:T], ALU.add, [p0_, pt_], [ps_])
                    if q0 > q00:
                        P.cp(eng_, ps_[:, q00:q0], p0_[:, q00:q0], [p0_], [ps_])
                else:
                    P.tt(eng_, ps_[:, q0:T], ps_[:, q0:T], pt_[:, q0:T], ALU.add, [ps_, pt_], [ps_])
                if pos == 3:
                    pending.append((ps_, grp, bi + 2))

            for it in range(nblk + DEPTH):
                if it < nblk:
                    front(it)
                if it >= DEPTH:
                    back(it - DEPTH)
                if it == 1 and pend_epi:
                    pend_epi.pop(0)()
                if it >= 1 and nxt:
                    nxt.pop(0)()
                yield
            while nxt:
                nxt.pop(0)()
            flush_ones()
            psidx[0] += nblk // 4
            def epilogue(h=h):
                rl = tmpD[0]
                t1 = tmpD[1]
                P.cp("dve", t1[:], OAc[:], [OAc], [t1])
                P.act(rl[:], LAc[:], AF.Ln, [LAc], [rl])
                P.act(rl[:], rl[:], AF.Exp, [rl], [rl], scale=-1.0)
                P.tt("dve", t1[:], t1[:], rl[:], ALU.mult, [t1, rl], [t1])
                P.tt("dve", ybT[h][:], t1[:], ybT[h][:], ALU.mult, [t1, ybT[h]], [ybT[h]])

            pend_epi.append(epilogue)
            yield
        while pend_epi:
            pend_epi.pop(0)()
        yield

    def stage_E(g):
        t0 = g * T
        hT = hTs[g % 2]
        hT_k = lambda kc: hT[:, kc, :]
        xtiles = [(xs[0], xs[0][:]), (xs[1], xs[1][:])]
        for i in range(2):
            P.dma("pool", xs[i][:], x[t0 + i * 128:t0 + (i + 1) * 128, :], [x], [xs[i]], xs[i])
        for c in range(8):
            slc = stream(12 + c)
            bga, bgb, bpa, bpb = [B[(c % 2) * 4 + k_] for k_ in range(4)]
            proj_fm(bga, slc, 0, 128, [hT], hT_k)
            proj_fm(bgb, slc, 128, 128, [hT], hT_k)
            ga, gb_ = gettmp(), gettmp()
            P.act(ga[:], bga[:], AF.Sigmoid, [bga, vc], [ga], bias=vc[:, V_BG + c:V_BG + c + 1])
            P.act(gb_[:], bgb[:], AF.Sigmoid, [bgb, vc], [gb_], bias=vc[:, V_BG + 8 + c:V_BG + 8 + c + 1])
            for kc in range(8):
                P.mm(bpa[:], slc[:, kc, 256:384], yaT[kc // 4][:, kc % 4, :], kc == 0, kc == 7,
                     [slc, yaT[kc // 4]], [bpa])
            for kc in range(8):
                P.mm(bpb[:], slc[:, kc, 384:512], ybT[kc][:], kc == 0, kc == 7, [slc, ybT[kc]], [bpb])
            P.tt("dve", ga[:], ga[:], bpa[:], ALU.mult, [ga, bpa], [ga])
            P.tt("dve", gb_[:], gb_[:], bpb[:], ALU.mult, [gb_, bpb], [gb_])
            P.tt("dve", mT[:, c, :], ga[:], gb_[:], ALU.add, [ga, gb_], [mT])
        for i in range(2, 4):
            yv = yaT[i - 2][:].rearrange("p a b -> p (a b)").bitcast(F32)
            xtiles.append((yaT[i - 2], yv))
            P.dma("pool", yv, x[t0 + i * 128:t0 + (i + 1) * 128, :], [x], [yaT[i - 2]], yaT[i - 2])
        slo = [stream(20), stream(21)]
        for i in range(NT):
            xt, xv = xtiles[i]
            r0 = t0 + i * 128
            for hf_ in range(2):
                bo = B[4 + (i % 2) * 2 + hf_]
                for kc in range(8):
                    P.mm(bo[:], mT[:, kc, i * 128:(i + 1) * 128], slo[hf_][:, kc, :], kc == 0, kc == 7,
                         [mT, slo[hf_]], [bo])
                P.tt("dve", xv[:, hf_ * 512:(hf_ + 1) * 512], xv[:, hf_ * 512:(hf_ + 1) * 512], bo[:], ALU.add,
                     [xt, bo], [xt])
            P.act(hb[:], xv, AF.Square, [xt], [hb, st4], accum_out=st4[:, 2 + i:3 + i])
            rstd_from(st4, st4[:, 2 + i:3 + i], st4, st4[:, 2 + i:3 + i], 1.0 / D)
            P.stt(xv, xv, st4[:, 2 + i:3 + i], fgb[:], ALU.mult, ALU.mult, [xt, st4, fgb], [xt])
            ob = Buf(f"out{g}_{i}")
            P.dma("pool", out[r0:r0 + 128, :], xv, [xt], [ob], xt)
            out_tiles.append(ob)

    NB_UNITS = 2 * (2 * 7 + 4 + 2 + 2 + 4 * 13)
    for _ in stage_A(0):
        pass
    for g in range(NG):
        stage_C(g)
        gens = [stage_B(g)]
        if g + 1 < NG:
            gens.append(stage_A(g + 1))
        gd = stage_D(g)
        nb_left = NB_UNITS
        nd_left = H * (4 * g + 4 + 4)
        d_alive = True
        while gens or d_alive:
            for gen in list(gens):
                try:
                    next(gen)
                except StopIteration:
                    gens.remove(gen)
            nb_left -= 1
            if d_alive:
                k = (1 << 30) if not gens else max(1, -(-nd_left // max(nb_left, 1)))
                for _ in range(k):
                    try:
                        next(gd)
                        nd_left -= 1
                    except StopIteration:
                        d_alive = False
                        break
        if stop_after == "D":
            break
        stage_E(g)

    P.emit(out_tiles + list(dbg_outs.values()))
    return nc, P


def host_consts(S):
    cst = np.zeros((128, 896), np.float32)
    cst[:, 0:128] = np.eye(128, dtype=np.float32)
    s = np.arange(128)[:, None]
    t = np.arange(128)[None, :]
    cst[:, 128:256] = ((s // 32 == t // 32) & (s <= t)).astype(np.float32)
    cst[:, 256:384] = (t >= s).astype(np.float32)
    rm = np.ones((128, 512), np.float32)
    rm[:, ::32] = 0.0
    cst[:, 384:896] = rm
    inv = (np.float32(10000.0) ** (-np.arange(0, 64, 2, dtype=np.float32) / np.float32(64))).astype(np.float32)
    ang = (np.arange(S, dtype=np.float32)[:, None] * inv[None, :]).astype(np.float32)
    cos = np.cos(ang).astype(np.float32).T
    sin = np.sin(ang).astype(np.float32).T
    rope = np.zeros((2, 64, S), np.float32)
    rope[0, 0:32] = cos
    rope[0, 32:64] = cos
    rope[1, 0:32] = -sin
    rope[1, 32:64] = sin
    return cst, rope


def pc(v):
    v = np.asarray(v, np.float32).reshape(-1, 128)
    return np.ascontiguousarray(v.T)


def make_in_maps(inputs, S):
    cst, rope = host_consts(S)
    vecs = np.concatenate([
        pc(inputs["norm_g"][0]), pc(inputs["b_gate"][0]), pc(inputs["lb_logits"][0]), pc(inputs["lb_logits"][1]),
        pc(inputs["hg_norm_g"][0]), pc(inputs["q_a_g"][0]), pc(inputs["kv_a_g"][0])], axis=1)
    assert vecs.shape == (128, 46)
    fgb = np.ascontiguousarray(np.broadcast_to(np.asarray(inputs["final_norm_g"], np.float32)[None, :], (128, D)))
    common = {
        "w_in": np.ascontiguousarray(inputs["w_in"][0], dtype=np.float32),
        "w_uq": np.ascontiguousarray(inputs["w_uq"][0], dtype=np.float32),
        "w_ukv": np.ascontiguousarray(inputs["w_ukv"][0], dtype=np.float32),
        "w_pa": np.ascontiguousarray(inputs["w_proj_a"][0], dtype=np.float32),
        "w_pb": np.ascontiguousarray(inputs["w_proj_b"][0], dtype=np.float32),
        "w_out": np.ascontiguousarray(inputs["w_out"][0], dtype=np.float32),
        "vecs": vecs, "fgb": fgb, "cst": cst, "rope": rope,
    }
    xa = np.asarray(inputs["x"], np.float32)
    return [dict(common, x=np.ascontiguousarray(xa[b])) for b in range(xa.shape[0])]


_CACHE = {}


def kernel(**inputs):
    xa = np.asarray(inputs["x"])
    Bn, S, _ = xa.shape
    if S not in _CACHE:
        _CACHE[S] = build(S)[0]
    nc = _CACHE[S]
    in_maps = make_in_maps(inputs, S)
    res = run_bass_kernel_spmd(nc, in_maps, core_ids=list(range(Bn)))
    return np.stack([np.asarray(r["out"], np.float32) for r in res.results], axis=0)
```
